# Optimizing a Trainium2 kernel written in Bass

```python
import jax
import jax.numpy as jnp
from jax import lax
import numpy as np

D_MODEL = 4096
BATCH = 4
SEQ = 2048
DEPTH = 1

GRID_W = 64
CTX_LEN = 256
EPS = 1e-6
N_MOD = 6
RET_HEADS = D_MODEL // 256
RET_QK_DIM = 128
RET_V_DIM = 256
RET_QK_W = RET_HEADS * RET_QK_DIM
RET_V_W = RET_HEADS * RET_V_DIM
RET_CHUNK = 128
ROPE_BASE = 10000.0
LRU_WIDTH = D_MODEL
LRU_BLOCKS = 16
LRU_BLOCK = LRU_WIDTH // LRU_BLOCKS
LRU_C = 8.0
LRU_CONV_W = 4
LRU_CONV_PAD = (2, 1)
FFN_DIM = ((8 * D_MODEL // 3 + 255) // 256) * 256
FFN_CONV_W = 3
FFN_CONV_PAD = (1, 1)
COL_SIZES = (RET_QK_W, RET_V_W, LRU_WIDTH, RET_QK_W, RET_V_W, LRU_WIDTH, D_MODEL, D_MODEL)
STATE_COLS = RET_QK_W + RET_V_W + LRU_WIDTH
IN_COLS = sum(COL_SIZES)
SPLIT_POINTS = tuple(int(s) for s in np.cumsum(COL_SIZES)[:-1])
STATE_SPLIT_POINTS = (RET_QK_W, RET_QK_W + RET_V_W)

kernel_name = 'hybrid_retention_rglru_convffn_dit_layer'


def rms_norm(x, gain):
    x32 = x.astype(jnp.float32)
    y = x32 * lax.rsqrt(jnp.mean(x32 * x32, axis=-1, keepdims=True) + EPS)
    return y.astype(x.dtype) * gain


def modulate(h, shift, scale):
    return h * (1 + scale) + shift


def depthwise_conv(x, w, b, pad):
    y = lax.conv_general_dilated(x, w[:, None, :], window_strides=(1,), padding=[pad],
                                 dimension_numbers=('NWC', 'WIO', 'NWC'),
                                 feature_group_count=x.shape[-1])
    return y + b


def rope_2d(n_tokens):
    rows = n_tokens // GRID_W
    row_ids = jnp.repeat(jnp.arange(rows, dtype=jnp.float32), GRID_W)
    col_ids = jnp.tile(jnp.arange(GRID_W, dtype=jnp.float32), rows)
    n_freq = RET_QK_DIM // 4
    inv_freq = ROPE_BASE ** (-jnp.arange(n_freq, dtype=jnp.float32) / n_freq)
    ang = jnp.concatenate([row_ids[:, None] * inv_freq, col_ids[:, None] * inv_freq], axis=-1)
    return jnp.cos(ang), jnp.sin(ang)


def apply_rope(t, rope):
    cos, sin = rope
    cos = cos.astype(t.dtype)
    sin = sin.astype(t.dtype)
    t1, t2 = jnp.split(t, 2, axis=-1)
    return jnp.concatenate([t1 * cos - t2 * sin, t1 * sin + t2 * cos], axis=-1)


def to_heads(t, head_dim):
    bsz, n, _ = t.shape
    return t.reshape(bsz, n, -1, head_dim).transpose(0, 2, 1, 3)


def both_directions(t, axis):
    return jnp.stack([t, jnp.flip(t, axis)], axis=0)


def retention_kv(k, v, rope):
    k = to_heads(k, RET_QK_DIM)
    if rope is not None:
        k = apply_rope(k, rope)
    k = k * (RET_QK_DIM ** -0.5)
    return both_directions(k, 2), both_directions(to_heads(v, RET_V_DIM), 2)


def retention_scan(q, k, v, log_gamma, s0):
    z, bsz, nh, t, _ = q.shape
    dv = v.shape[-1]
    n_chunks = t // RET_CHUNK
    pos = jnp.arange(RET_CHUNK, dtype=jnp.float32)
    lg = log_gamma[:, None, :, None]
    rel = pos[:, None] - pos[None, :]
    decay_mat = jnp.where(rel >= 0, jnp.exp(jnp.maximum(rel, 0.0) * lg[..., None]), 0.0)
    q_decay = jnp.exp((pos + 1.0) * lg)[..., None]
    k_decay = jnp.exp((RET_CHUNK - 1.0 - pos) * lg)[..., None]
    chunk_decay = jnp.exp(RET_CHUNK * lg)[..., None]

    def chunks(a):
        a = a.astype(jnp.float32).reshape(z, bsz, nh, n_chunks, RET_CHUNK, a.shape[-1])
        return jnp.moveaxis(a, 3, 0)

    def step(s, inp):
        qc, kc, vc = inp
        scores = jnp.einsum('zbhid,zbhjd->zbhij', qc, kc) * decay_mat
        o = (jnp.einsum('zbhij,zbhje->zbhie', scores, vc)
             + jnp.einsum('zbhid,zbhde->zbhie', qc * q_decay, s))
        s = chunk_decay * s + jnp.einsum('zbhjd,zbhje->zbhde', kc * k_decay, vc)
        return s, o

    s_final, o = lax.scan(step, s0, (chunks(q), chunks(k), chunks(v)))
    o = jnp.moveaxis(o, 0, 3).reshape(z, bsz, nh, t, dv)
    return o, s_final


def retention_final_state(k, v, log_gamma):
    n = k.shape[3]
    pos = jnp.arange(n, dtype=jnp.float32)
    w = jnp.exp((n - 1.0 - pos) * log_gamma[:, None, :, None])[..., None]
    return jnp.einsum('zbhjd,zbhje->zbhde', k.astype(jnp.float32) * w, v.astype(jnp.float32))


def head_norm(o):
    mu = jnp.mean(o, axis=-1, keepdims=True)
    var = jnp.mean(jnp.square(o - mu), axis=-1, keepdims=True)
    return (o - mu) * lax.rsqrt(var + EPS)


def lru_inputs(xl, lp):
    xc = depthwise_conv(xl, lp['lru_conv_w'], lp['lru_conv_b'], LRU_CONV_PAD)
    bsz, t, w = xc.shape
    xb = xc.reshape(bsz, t, LRU_BLOCKS, LRU_BLOCK)
    gate_r = jnp.einsum('btnd,zndf->zbtnf', xb, lp['lru_wa']).reshape(2, bsz, t, w) + lp['lru_ba'][:, None, None, :]
    gate_i = jnp.einsum('btnd,zndf->zbtnf', xb, lp['lru_wx']).reshape(2, bsz, t, w) + lp['lru_bx'][:, None, None, :]
    r = jax.nn.sigmoid(gate_r.astype(jnp.float32))
    i = jax.nn.sigmoid(gate_i.astype(jnp.float32))
    log_a = -LRU_C * jax.nn.softplus(-lp['lru_lambda'].astype(jnp.float32))[:, None, None, :] * r
    a = jnp.exp(log_a)
    b = jnp.sqrt(-jnp.expm1(2.0 * log_a)) * i * xc.astype(jnp.float32)[None]
    a = jnp.stack([a[0], jnp.flip(a[1], 1)])
    b = jnp.stack([b[0], jnp.flip(b[1], 1)])
    return a, b


def linear_scan(a, b, h0):
    def combine(left, right):
        return left[0] * right[0], right[0] * left[1] + right[1]
    a_cum, h = lax.associative_scan(combine, (a, b), axis=2)
    return a_cum * h0[:, :, None, :] + h


def mixer(h, lp, ret_s0, lru_h0, rope):
    k, v, xl, q, g_ret, g_lru, m_ret, m_lru = jnp.split(h @ lp['w_in'], SPLIT_POINTS, axis=-1)
    bsz, n, _ = h.shape
    q = to_heads(q, RET_QK_DIM)
    if rope is not None:
        q = apply_rope(q, rope)
    k2, v2 = retention_kv(k, v, rope)
    o, ret_sf = retention_scan(both_directions(q, 2), k2, v2, lp['log_gamma'], ret_s0)
    o = head_norm(o[0] + jnp.flip(o[1], 2))
    o = o.transpose(0, 2, 1, 3).reshape(bsz, n, RET_V_W).astype(h.dtype)
    y_ret = (jax.nn.silu(g_ret) * o) @ lp['w_ret_o']
    a, b = lru_inputs(xl, lp)
    hs = linear_scan(a, b, lru_h0)
    lru_hf = hs[:, :, -1]
    y_l = (hs[0] + jnp.flip(hs[1], 1)).astype(h.dtype)
    y_lru = (jax.nn.gelu(g_lru) * y_l) @ lp['w_lru_o']
    y = jax.nn.sigmoid(m_ret) * y_ret + jax.nn.sigmoid(m_lru) * y_lru
    return y @ lp['w_out'], ret_sf, lru_hf


def context_states(h, lp):
    k, v, xl = jnp.split(h @ lp['w_in'][:, :STATE_COLS], STATE_SPLIT_POINTS, axis=-1)
    k2, v2 = retention_kv(k, v, None)
    ret_s = retention_final_state(k2, v2, lp['log_gamma'])
    a, b = lru_inputs(xl, lp)
    h0 = jnp.zeros((2, h.shape[0], LRU_WIDTH), jnp.float32)
    lru_h = linear_scan(a, b, h0)[:, :, -1]
    return ret_s, lru_h


def conv_ffn(h, lp):
    u = depthwise_conv(h @ lp['w_up'], lp['ffn_conv_w'], lp['ffn_conv_b'], FFN_CONV_PAD)
    g, v = jnp.split(u, 2, axis=-1)
    return (jax.nn.silu(g) * v) @ lp['w_down']


def setup_inputs(seed: int = 0) -> dict:
    key = jax.random.key(seed)
    ks = jax.random.split(key, 25)
    f32 = jnp.float32
    D = D_MODEL

    def nrm(k, shape, scale):
        return jax.random.normal(k, shape, f32) * scale

    h_idx = np.arange(RET_HEADS, dtype=np.float32)
    gamma0 = 1.0 - np.float32(2.0) ** (-5.0 - h_idx)
    logit0 = jnp.asarray(np.log(gamma0) - np.log1p(-gamma0), f32)
    a0 = jax.random.uniform(ks[16], (DEPTH, 2, LRU_WIDTH), f32, 0.9, 0.999)
    s0 = a0 ** (1.0 / LRU_C)
    lam = jnp.log(s0) - jnp.log1p(-s0)
    return {
        'x': nrm(ks[0], (BATCH, SEQ, D), 1.0),
        'c': nrm(ks[1], (BATCH, D), 1.0),
        'ctx': nrm(ks[2], (BATCH, CTX_LEN, D), 1.0),
        'c_ctx': nrm(ks[3], (D,), 1.0),
        'w_ada': nrm(ks[4], (DEPTH, D, N_MOD * D), 0.5 * D ** -0.5),
        'b_ada': nrm(ks[5], (DEPTH, N_MOD * D), 0.01),
        'norm1': 1.0 + nrm(ks[6], (DEPTH, D), 0.02),
        'norm2': 1.0 + nrm(ks[7], (DEPTH, D), 0.02),
        'w_in': nrm(ks[8], (DEPTH, D, IN_COLS), D ** -0.5),
        'ret_decay_logit': logit0[None, None, :] + nrm(ks[9], (DEPTH, 2, RET_HEADS), 0.1),
        'lru_conv_w': nrm(ks[10], (DEPTH, LRU_CONV_W, LRU_WIDTH), LRU_CONV_W ** -0.5),
        'lru_conv_b': nrm(ks[11], (DEPTH, LRU_WIDTH), 0.01),
        'lru_wa': nrm(ks[12], (DEPTH, 2, LRU_BLOCKS, LRU_BLOCK, LRU_BLOCK), LRU_BLOCK ** -0.5),
        'lru_ba': nrm(ks[13], (DEPTH, 2, LRU_WIDTH), 0.01),
        'lru_wx': nrm(ks[14], (DEPTH, 2, LRU_BLOCKS, LRU_BLOCK, LRU_BLOCK), LRU_BLOCK ** -0.5),
        'lru_bx': nrm(ks[15], (DEPTH, 2, LRU_WIDTH), 0.01),
        'lru_lambda': lam,
        'w_ret_o': nrm(ks[17], (DEPTH, RET_V_W, D), RET_V_W ** -0.5),
        'w_lru_o': nrm(ks[18], (DEPTH, LRU_WIDTH, D), LRU_WIDTH ** -0.5),
        'w_out': nrm(ks[19], (DEPTH, D, D), D ** -0.5),
        'w_up': nrm(ks[20], (DEPTH, D, 2 * FFN_DIM), D ** -0.5),
        'ffn_conv_w': nrm(ks[21], (DEPTH, FFN_CONV_W, 2 * FFN_DIM), FFN_CONV_W ** -0.5),
        'ffn_conv_b': nrm(ks[22], (DEPTH, 2 * FFN_DIM), 0.01),
        'w_down': nrm(ks[23], (DEPTH, FFN_DIM, D), FFN_DIM ** -0.5),
        'final_norm': 1.0 + nrm(ks[24], (D,), 0.02),
    }


def reference(x, c, ctx, c_ctx, w_ada, b_ada, norm1, norm2, w_in, ret_decay_logit,
              lru_conv_w, lru_conv_b, lru_wa, lru_ba, lru_wx, lru_bx, lru_lambda,
              w_ret_o, w_lru_o, w_out, w_up, ffn_conv_w, ffn_conv_b, w_down, final_norm):
    bsz = x.shape[0]
    rope = rope_2d(x.shape[1])
    silu_c = jax.nn.silu(c)
    silu_cc = jax.nn.silu(c_ctx)
    x_lat, x_ctx = x, ctx
    for l in range(DEPTH):
        last = l == DEPTH - 1
        lp = {
            'w_in': w_in[l],
            'log_gamma': jax.nn.log_sigmoid(ret_decay_logit[l].astype(jnp.float32)),
            'lru_conv_w': lru_conv_w[l], 'lru_conv_b': lru_conv_b[l],
            'lru_wa': lru_wa[l], 'lru_ba': lru_ba[l],
            'lru_wx': lru_wx[l], 'lru_bx': lru_bx[l], 'lru_lambda': lru_lambda[l],
            'w_ret_o': w_ret_o[l], 'w_lru_o': w_lru_o[l], 'w_out': w_out[l],
            'w_up': w_up[l], 'ffn_conv_w': ffn_conv_w[l], 'ffn_conv_b': ffn_conv_b[l],
            'w_down': w_down[l],
        }
        mod = silu_c @ w_ada[l] + b_ada[l]
        sh1, sc1, g1, sh2, sc2, g2 = jnp.split(mod[:, None, :], N_MOD, axis=-1)
        n_ctx_mod = 2 if last else N_MOD
        cmod = jnp.split(silu_cc @ w_ada[l][:, :n_ctx_mod * D_MODEL] + b_ada[l][:n_ctx_mod * D_MODEL], n_ctx_mod)
        h_ctx = modulate(rms_norm(x_ctx, norm1[l]), cmod[0], cmod[1])
        if last:
            ret_s, lru_h = context_states(h_ctx, lp)
        else:
            ret_zero = jnp.zeros((2, bsz, RET_HEADS, RET_QK_DIM, RET_V_DIM), jnp.float32)
            lru_zero = jnp.zeros((2, bsz, LRU_WIDTH), jnp.float32)
            y_ctx, ret_s, lru_h = mixer(h_ctx, lp, ret_zero, lru_zero, None)
            x_ctx = x_ctx + cmod[2] * y_ctx
            h2c = modulate(rms_norm(x_ctx, norm2[l]), cmod[3], cmod[4])
            x_ctx = x_ctx + cmod[5] * conv_ffn(h2c, lp)
        h_lat = modulate(rms_norm(x_lat, norm1[l]), sh1, sc1)
        y_lat, _, _ = mixer(h_lat, lp, ret_s, lru_h, rope)
        x_lat = x_lat + g1 * y_lat
        h2 = modulate(rms_norm(x_lat, norm2[l]), sh2, sc2)
        x_lat = x_lat + g2 * conv_ffn(h2, lp)
    return rms_norm(x_lat, final_norm)
```

```python
import contextlib
import numpy as np
import concourse.bass as bass
import concourse.mybir as mybir
from concourse.bass_utils import run_bass_kernel_spmd

F32 = mybir.dt.float32
BF16 = mybir.dt.bfloat16
I32 = mybir.dt.int32
AF = mybir.ActivationFunctionType
ALU = mybir.AluOpType
AX = mybir.AxisListType

D = 4096
KT = 32
NT = 1025
NTP = 1026
NPF = 1279
NPFP = 1280
H = 16
FF = 11008
FFT = 86
EPS = 1e-6
TWO_PI = 6.283185307179586


class Sched:
    ENG = ('pe', 'act', 'dve', 'pool', 'sp')

    def __init__(self, nc, stack, n_dma_sems=(('sp', 14), ('pool', 14))):
        self.nc = nc
        self.sem = {}
        for e in self.ENG:
            self.sem['c_' + e] = stack.enter_context(nc.semaphore('c_' + e))
        self.cnt = {e: 0 for e in self.ENG}
        self.pending = {e: False for e in self.ENG}
        self.seen = {e: {} for e in self.ENG}
        self.prog = {e: [] for e in self.ENG}
        self.last_write = {}
        self.readers = {}
        self.dma_sems = {}
        self.dma_uses = {}
        self.dma_rr = {}
        for q, n in n_dma_sems:
            names = []
            for i in range(n):
                nm = 'd_%s_%d' % (q, i)
                self.sem[nm] = stack.enter_context(nc.semaphore(nm))
                self.dma_uses[nm] = 0
                names.append(nm)
            self.dma_sems[q] = names
            self.dma_rr[q] = 0

    def _deps(self, eng, reads, writes):
        deps = {}

        def add(tok):
            if tok is None:
                return
            s, v = tok
            if deps.get(s, 0) < v:
                deps[s] = v
        for r in reads:
            add(self.last_write.get(r))
            if isinstance(r, tuple) and r[0] == 'ps':
                for s, v in self.readers.get(r, {}).items():
                    if s != 'c_' + eng:
                        add((s, v))
        for r in writes:
            add(self.last_write.get(r))
            for s, v in self.readers.get(r, {}).items():
                add((s, v))
        waits = []
        for s, v in deps.items():
            if eng == 'pe' and s == 'c_pe':
                continue
            if self.seen[eng].get(s, 0) < v:
                self.seen[eng][s] = v
                waits.append((s, v))
        return waits

    def _commit(self, tok, reads, writes):
        for r in writes:
            self.last_write[r] = tok
            self.readers[r] = {}
        for r in reads:
            d = self.readers.setdefault(r, {})
            if d.get(tok[0], 0) < tok[1]:
                d[tok[0]] = tok[1]

    def op(self, eng, emit, reads=(), writes=(), inc=True):
        waits = self._deps(eng, reads, writes)
        if inc:
            self.cnt[eng] += 1
            self.pending[eng] = False
            tok = ('c_' + eng, self.cnt[eng])
        else:
            self.pending[eng] = True
            tok = ('c_' + eng, self.cnt[eng] + 1)
        self.prog[eng].append((waits, emit, ('c_' + eng, 1) if inc else None))
        self._commit(tok, reads, writes)
        return tok

    def dma(self, q, out, in_, reads=(), writes=(), **kw):
        names = self.dma_sems[q]
        nm = names[self.dma_rr[q] % len(names)]
        self.dma_rr[q] += 1
        waits = self._deps(q, reads, writes)
        prev = self.dma_uses[nm] * 16
        if prev and self.seen[q].get(nm, 0) < prev:
            self.seen[q][nm] = prev
            waits.append((nm, prev))
        self.dma_uses[nm] += 1
        tok = (nm, self.dma_uses[nm] * 16)

        def emit(e, out=out, in_=in_, kw=kw):
            return e.dma_start(out=out, in_=in_, **kw)
        self.prog[q].append((waits, emit, (nm, 16)))
        self._commit(tok, reads, writes)
        return tok

    def barrier(self):
        toks = []
        for e in ('pe', 'act', 'dve', 'pool'):
            if self.pending[e]:
                raise RuntimeError('pending non-inc op on %s at barrier' % e)
            if self.cnt[e]:
                toks.append(('c_' + e, self.cnt[e]))
        for nm, u in self.dma_uses.items():
            if u:
                toks.append((nm, u * 16))
        for e in self.ENG:
            waits = []
            for s, v in toks:
                if self.seen[e].get(s, 0) < v:
                    self.seen[e][s] = v
                    waits.append((s, v))
            if waits:
                self.prog[e].append((waits, None, None))
        self.last_write = {}
        self.readers = {}

    def finish(self):
        nc = self.nc
        self.barrier()
        sem = self.sem
        prog = self.prog
        with nc.Block() as block:
            def run(e):
                def body(engobj):
                    for waits, emit, inc in prog[e]:
                        for s, v in waits:
                            engobj.wait_ge(sem[s], v)
                        if emit is None:
                            continue
                        ins = emit(engobj)
                        if inc is not None:
                            ins.then_inc(sem[inc[0]], inc[1])
                return body
            block.tensor(run('pe'))
            block.scalar(run('act'))
            block.vector(run('dve'))
            block.gpsimd(run('pool'))
            block.sync(run('sp'))


class Prog:
    def __init__(self, dbg=None, only_inputs=None):
        self.dbg = dbg or {}
        self.only_inputs = only_inputs
        self.nc = bass.Bass("TRN2", target_bir_lowering=False)
        self.outer = contextlib.ExitStack()
        self.S = Sched(self.nc, self.outer)
        self.cast_rr = 0
        self.evac_rr = 0

    def din(self, name, shape, dt=F32):
        if self.only_inputs is not None and name not in self.only_inputs:
            return None
        return self.nc.dram_tensor(name, list(shape), dt, kind="ExternalInput").ap()

    def dscr(self, name, shape, dt=F32):
        kind = "ExternalOutput" if name in self.dbg else "Internal"
        return self.nc.dram_tensor(name, list(shape), dt, kind=kind).ap()

    def sb(self, st, name, shape, dt=F32):
        self._uid = getattr(self, "_uid", 0) + 1
        return st.enter_context(self.nc.sbuf_tensor("s%d_%s" % (self._uid, name), list(shape), dt))

    def proj_fm(self, tag, w, K, col_segs, act, act_key, tok_groups, epilogue, banks, ps):
        S = self.S
        nc = self.nc
        kt_n = K // 128
        SW = sum(n for _, n in col_segs[0])
        KQ = 4
        with contextlib.ExitStack() as st:
            nstg = 3
            stg = [self.sb(st, '%s_stg%d' % (tag, i), [128, KQ, SW], F32) for i in range(nstg)]
            slab = [self.sb(st, '%s_slab%d' % (tag, i), [128, kt_n, SW], BF16) for i in range(2)]
            stg_i = 0
            pending_epi = None
            bank_rr = 0
            nb = len(banks)
            for si, segs in enumerate(col_segs):
                sl = slab[si % 2]
                slk = '%s_slab%d' % (tag, si % 2)
                kq0 = 0
                while kq0 < kt_n:
                    kq = min(KQ, kt_n - kq0)
                    sg = stg[stg_i % nstg]
                    sgk = '%s_stg%d' % (tag, stg_i % nstg)
                    stg_i += 1
                    off = 0
                    for (c0, ncol) in segs:
                        src = w[kq0 * 128:(kq0 + kq) * 128, c0:c0 + ncol].rearrange("(k p) c -> p k c", p=128)
                        S.dma('sp', sg[:, 0:kq, off:off + ncol], src, writes=[sgk])
                        off += ncol
                    wk = [(slk, kq0 + j) for j in range(kq)]
                    if self.cast_rr % 2 == 0:
                        S.op('act', lambda e, o=sl[:, kq0:kq0 + kq, :], i=sg[:, 0:kq, :]: e.activation(out=o, in_=i, func=AF.Copy),
                             reads=[sgk], writes=wk)
                    else:
                        S.op('dve', lambda e, o=sl[:, kq0:kq0 + kq, :], i=sg[:, 0:kq, :]: e.tensor_copy(out=o, in_=i),
                             reads=[sgk], writes=wk)
                    self.cast_rr += 1
                    kq0 += kq
                for ch in range(SW // 128):
                    pss = []
                    for (t0, n) in tok_groups:
                        b = banks[bank_rr % nb]
                        bank_rr += 1
                        pk = ('ps', b)
                        for kt in range(kt_n):
                            S.op('pe', lambda e, o=ps[:, b, 0:n], l=sl[:, kt, ch * 128:(ch + 1) * 128], r=act[:, kt, t0:t0 + n], a=(kt == 0), z=(kt == kt_n - 1):
                                 e.matmul(o, lhsT=l, rhs=r, start=a, stop=z),
                                 reads=[(slk, kt), act_key], writes=[pk], inc=(kt == kt_n - 1))
                        pss.append((ps[:, b, 0:n], pk))
                    if pending_epi is not None:
                        pending_epi()
                    pending_epi = (lambda si=si, ch=ch, pss=pss: epilogue(si, ch, pss))
            if pending_epi is not None:
                pending_epi()
            S.barrier()

    def evac(self, out, in_, reads, writes, func=None, **kw):
        S = self.S
        if func is None and not kw:
            self.evac_rr += 1
            if self.evac_rr % 2:
                return S.op('dve', lambda e: e.tensor_copy(out=out, in_=in_), reads=reads, writes=writes)
            func = AF.Copy
        return S.op('act', lambda e: e.activation(out=out, in_=in_, func=func, **kw), reads=reads, writes=writes)


def build(dbg=None, stop_after=99, only_inputs=None):
    P = Prog(dbg, only_inputs)
    nc, S = P.nc, P.S
    o = P.outer
    x_main = P.din("x_main", [1152, D])
    x_pre = P.din("x_pre", [NPFP, D])
    cT = P.din("cT", [128, KT, 2])
    w_ada = P.din("w_ada", [D, 6 * D])
    b_adaT = P.din("b_adaT", [128, 192])
    norm1T = P.din("norm1T", [128, KT])
    norm2T = P.din("norm2T", [128, KT])
    fnormT = P.din("fnormT", [128, KT])
    w_in = P.din("w_in", [D, 28672])
    retlog = P.din("retlog", [128, 2 * H])
    conv5T = P.din("conv5T", [128, KT, 5])
    convbT = P.din("convbT", [128, KT])
    lru_wa = P.din("lru_wa", [2, 16, 256, 256])
    lru_wx = P.din("lru_wx", [2, 16, 256, 256])
    lru_baT = P.din("lru_baT", [128, 2, KT])
    lru_bxT = P.din("lru_bxT", [128, 2, KT])
    lru_lamT = P.din("lru_lamT", [128, 2, KT])
    w_ret_o = P.din("w_ret_o", [D, D])
    w_lru_o = P.din("w_lru_o", [D, D])
    w_out = P.din("w_out", [D, D])
    w_up = P.din("w_up", [D, 2 * FF])
    fconvT = P.din("fconvT", [128, 2 * FFT, 3])
    fconvbT = P.din("fconvbT", [128, 2 * FFT])
    w_down = P.din("w_down", [FF, D])
    ident_in = P.din("ident", [128, 128])
    rmat_in = P.din("rmat", [128, 128])
    ropec = P.din("ropec", [128, 5])
    y_out = nc.dram_tensor("y_out", [1024, D], F32, kind="ExternalOutput").ap()

    xT_m = P.dscr("xT_m", [D, NTP])
    kT_m = P.dscr("kT_m", [2048, NTP], BF16)
    qT_m = P.dscr("qT_m", [2048, NTP], BF16)
    vT_m = P.dscr("vT_m", [D, NTP], BF16)
    xl_m = P.dscr("xl_m", [D, NTP])
    sg_m = P.dscr("sg_m", [D, NTP])
    gg_m = P.dscr("gg_m", [D, NTP])
    sr_m = P.dscr("sr_m", [D, NTP])
    sl_m = P.dscr("sl_m", [D, NTP])
    kT_p = P.dscr("kT_p", [2048, NPFP], BF16)
    vT_p = P.dscr("vT_p", [D, NPFP], BF16)
    xl_p = P.dscr("xl_p", [D, NPFP])
    yl_s = P.dscr("yl_s", [D, NTP])
    yr_s = P.dscr("yr_s", [D, NTP])
    xlatT = P.dscr("xlatT", [D, NTP])
    a_scr = P.dscr("a_scr", [FF, 1024], BF16)
    st_scr = P.dscr("st_scr", [2, H, 128, 256])
    hl_scr = P.dscr("hl_scr", [2, D])

    T = lambda name, shape, dt=F32: P.sb(o, name, shape, dt)
    ps = o.enter_context(nc.psum_tensor("ps", [128, 8, 512], F32))

    ident = T("ident", [128, 128])
    identb = T("identb", [128, 128], BF16)
    rmat = T("rmatb", [128, 128], BF16)
    rtmp = T("rtmp", [128, 128])
    ones = T("ones", [128, 128])
    modT = T("modT", [128, 192, 2])
    A1l = T("A1l", [128, KT]); B1l = T("B1l", [128, KT])
    A1c = T("A1c", [128, KT]); B1c = T("B1c", [128, KT])
    A2 = T("A2", [128, KT]); B2 = T("B2", [128, KT])
    G1 = T("G1", [128, KT]); G2 = T("G2", [128, KT])
    n1 = T("n1", [128, KT]); n2 = T("n2", [128, KT]); bada = T("bada", [128, 192])
    lg = T("lg", [128, 2 * H])
    rc = T("rc", [128, 5])
    epsc = T('epsc', [128, 1])
    rstd2 = T('rstd2', [128, NTP])
    S.op('dve', lambda e: e.memset(epsc[:], EPS), writes=['epsc'])
    S.dma('sp', ident[:], ident_in, writes=['ident'])
    S.dma('sp', rtmp[:], rmat_in, writes=['rtmp'])
    S.dma('sp', n1[:], norm1T, writes=['n1'])
    S.dma('sp', n2[:], norm2T, writes=['n2'])
    S.dma('sp', bada[:], b_adaT, writes=['bada'])
    S.dma('sp', lg[:], retlog, writes=['lg'])
    S.dma('sp', rc[:], ropec, writes=['rc'])
    S.op('dve', lambda e: e.tensor_copy(out=identb[:], in_=ident[:]), reads=['ident'], writes=['identb'])
    S.op('dve', lambda e: e.tensor_copy(out=rmat[:], in_=rtmp[:]), reads=['rtmp'], writes=['rmat'])
    S.op('dve', lambda e: e.memset(ones[:], 1.0), writes=['ones'])
    S.op('act', lambda e: e.activation(out=lg[:], in_=lg[:], func=AF.Exp, scale=-1.0), reads=['lg'], writes=['lg'])
    S.op('act', lambda e: e.activation(out=lg[:], in_=lg[:], func=AF.Ln, bias=1.0), reads=['lg'], writes=['lg'])
    S.op('dve', lambda e: e.tensor_scalar(out=lg[:], in0=lg[:], scalar1=-1.0, scalar2=None, op0=ALU.mult), reads=['lg'], writes=['lg'])

    if 'inj_mod' in P.dbg:
        S.dma('sp', modT[:], P.din('modT_in', [128, 192, 2]), writes=['modT'])
    with contextlib.ExitStack() as st:
      if 'inj_mod' not in P.dbg:
          cf = P.sb(st, "cf", [128, KT, 2], F32)
          cb = P.sb(st, "cb", [128, KT, 2], BF16)
          S.dma('sp', cf[:], cT, writes=['cf'])
          S.op('act', lambda e: e.activation(out=cb[:], in_=cf[:], func=AF.Silu), reads=['cf'], writes=['cb'])

          def epi_ada(si, ch, pss):
              j = si * 4 + ch
              (p0, pk), = pss
              S.op('dve', lambda e: e.tensor_scalar(out=modT[:, j, :], in0=p0, scalar1=bada[:, j:j + 1], scalar2=None, op0=ALU.add),
                   reads=[pk, 'bada'], writes=['modT'])
          P.proj_fm('ada', w_ada, D, [[(s * 512, 512)] for s in range(48)], cb, 'cb', [(0, 2)], epi_ada, [0, 1, 2, 3], ps)
    def ts(out, in0, s1, s2, op0, op1=None, rd=(), wr=()):
        if op1 is None:
            S.op('dve', lambda e: e.tensor_scalar(out=out, in0=in0, scalar1=s1, scalar2=None, op0=op0), reads=rd, writes=wr)
        else:
            S.op('dve', lambda e: e.tensor_scalar(out=out, in0=in0, scalar1=s1, scalar2=s2, op0=op0, op1=op1), reads=rd, writes=wr)

    def tt(out, a, b, op, rd=(), wr=()):
        S.op('dve', lambda e: e.tensor_tensor(out=out, in0=a, in1=b, op=op), reads=rd, writes=wr)
    ts(A1l[:], modT[:, 32:64, 0], 1.0, None, ALU.add, rd=['modT'], wr=['A1l'])
    tt(A1l[:], A1l[:], n1[:], ALU.mult, rd=['A1l', 'n1'], wr=['A1l'])
    ts(A1c[:], modT[:, 32:64, 1], 1.0, None, ALU.add, rd=['modT'], wr=['A1c'])
    tt(A1c[:], A1c[:], n1[:], ALU.mult, rd=['A1c', 'n1'], wr=['A1c'])
    ts(A2[:], modT[:, 128:160, 0], 1.0, None, ALU.add, rd=['modT'], wr=['A2'])
    tt(A2[:], A2[:], n2[:], ALU.mult, rd=['A2', 'n2'], wr=['A2'])
    for dst, lo, col, nm in ((B1l, 0, 0, 'B1l'), (B1c, 0, 1, 'B1c'), (B2, 96, 0, 'B2'), (G1, 64, 0, 'G1'), (G2, 160, 0, 'G2')):
        S.op('dve', lambda e, dst=dst, lo=lo, col=col: e.tensor_copy(out=dst[:], in_=modT[:, lo:lo + 32, col]), reads=['modT'], writes=[nm])
    S.barrier()
    if 'modT' in P.dbg:
        dbg_mod = nc.dram_tensor("dbg_modT", [128, 192, 2], F32, kind="ExternalOutput").ap()
        S.dma('sp', dbg_mod, modT[:], reads=['modT'])
    if stop_after <= 0:
        S.finish(); return nc

    ro = contextlib.ExitStack()
    CQ = P.sb(ro, "CQ", [128, NTP]); SQ = P.sb(ro, "SQ", [128, NTP])
    CKm = P.sb(ro, "CKm", [128, NTP]); SKm = P.sb(ro, "SKm", [128, NTP])
    CKp = P.sb(ro, "CKp", [128, NPFP]); SKp = P.sb(ro, "SKp", [128, NPFP])
    with contextlib.ExitStack() as st:
        NL = 2048
        ri = P.sb(st, "ri", [128, NL], I32); ci = P.sb(st, "ci", [128, NL], I32)
        rf = P.sb(st, "rf", [128, NL]); cfl = P.sb(st, "cfl", [128, NL])
        ang = P.sb(st, "ang", [128, NL]); u = P.sb(st, "u", [128, NL]); ui = P.sb(st, "ui", [128, NL], I32)
        uf = P.sb(st, "uf", [128, NL]); m1 = P.sb(st, "m1", [128, NL])
        cosf = P.sb(st, "cosf", [128, NL]); sinf = P.sb(st, "sinf", [128, NL])
        invf = P.sb(st, "invf", [128, 1])
        S.op('pool', lambda e: e.iota(ri[:], pattern=[[1, 32], [0, 64]], base=0, channel_multiplier=0), writes=['ri'])
        S.op('pool', lambda e: e.iota(ci[:], pattern=[[0, 32], [1, 64]], base=0, channel_multiplier=0), writes=['ci'])
        S.op('dve', lambda e: e.tensor_copy(out=rf[:], in_=ri[:]), reads=['ri'], writes=['rf'])
        S.op('dve', lambda e: e.tensor_copy(out=cfl[:], in_=ci[:]), reads=['ci'], writes=['cfl'])
        ts(rf[:], rf[:], rc[:, 0:1], rc[:, 1:2], ALU.mult, ALU.add, rd=['rf', 'rc'], wr=['rf'])
        ts(cfl[:], cfl[:], rc[:, 0:1], rc[:, 2:3], ALU.mult, ALU.add, rd=['cfl', 'rc'], wr=['cfl'])
        tt(rf[:], rf[:], cfl[:], ALU.subtract, rd=['rf', 'cfl'], wr=['rf'])
        S.op('dve', lambda e: e.scalar_tensor_tensor(out=ang[:], in0=rf[:], scalar=rc[:, 4:5], in1=cfl[:], op0=ALU.mult, op1=ALU.add),
             reads=['rf', 'cfl', 'rc'], writes=['ang'])
        S.op('act', lambda e: e.activation(out=invf[:], in_=rc[:, 3:4], func=AF.Exp, scale=-float(np.log(10000.0) / 32.0)), reads=['rc'], writes=['invf'])
        ts(ang[:], ang[:], invf[:, 0:1], 1.0 / TWO_PI, ALU.mult, ALU.mult, rd=['ang', 'invf'], wr=['ang'])
        for (dst, shift, nm) in ((sinf, 0.0, 'sinf'), (cosf, 0.25, 'cosf')):
            ts(u[:], ang[:], shift, None, ALU.add, rd=['ang'], wr=['u'])
            S.op('dve', lambda e: e.tensor_copy(out=ui[:], in_=u[:]), reads=['u'], writes=['ui'])
            S.op('dve', lambda e: e.tensor_copy(out=uf[:], in_=ui[:]), reads=['ui'], writes=['uf'])
            tt(u[:], u[:], uf[:], ALU.subtract, rd=['u', 'uf'], wr=['u'])
            ts(m1[:], u[:], 0.5, None, ALU.is_gt, rd=['u'], wr=['m1'])
            tt(u[:], u[:], m1[:], ALU.subtract, rd=['u', 'm1'], wr=['u'])
            ts(m1[:], u[:], -0.5, None, ALU.is_lt, rd=['u'], wr=['m1'])
            tt(u[:], u[:], m1[:], ALU.add, rd=['u', 'm1'], wr=['u'])
            S.op('act', lambda e, dst=dst: e.activation(out=dst[:], in_=u[:], func=AF.Sin, scale=TWO_PI), reads=['u'], writes=[nm])
        sc = float(128 ** -0.5)
        S.op('dve', lambda e: e.tensor_copy(out=CQ[:, 0:NT], in_=cosf[:, 0:NT]), reads=['cosf'], writes=['CQ'])
        S.op('dve', lambda e: e.tensor_copy(out=SQ[:, 0:NT], in_=sinf[:, 0:NT]), reads=['sinf'], writes=['SQ'])
        ts(CKm[:, 0:NT], cosf[:, 0:NT], sc, None, ALU.mult, rd=['cosf'], wr=['CKm'])
        ts(SKm[:, 0:NT], sinf[:, 0:NT], sc, None, ALU.mult, rd=['sinf'], wr=['SKm'])
        S.op('dve', lambda e: e.memset(CKp[:], sc), writes=['CKp'])
        S.op('dve', lambda e: e.memset(SKp[:], 0.0), writes=['SKp'])
        ts(CKp[:, 256:NPF], cosf[:, 1025:2048], sc, None, ALU.mult, rd=['cosf', 'CKp'], wr=['CKp'])
        ts(SKp[:, 256:NPF], sinf[:, 1025:2048], sc, None, ALU.mult, rd=['sinf', 'SKp'], wr=['SKp'])
        S.barrier()
        if 'rope' in P.dbg:
            d1 = nc.dram_tensor("dbg_cos", [128, NL], F32, kind="ExternalOutput").ap()
            d2 = nc.dram_tensor("dbg_sin", [128, NL], F32, kind="ExternalOutput").ap()
            S.dma('sp', d1, cosf[:], reads=['cosf']); S.dma('sp', d2, sinf[:], reads=['sinf'])
            S.barrier()
    if stop_after <= 1:
        S.finish(); return nc


    def norm_tm(st, x_dram, ntile, hT, hkey, AB_of_tile, xT_spill):
        xt = [P.sb(st, "xt%d" % i, [128, D], F32) for i in range(2)]
        xs = [P.sb(st, "xs%d" % i, [128, D], F32) for i in range(2)]
        ss = P.sb(st, "ss", [128, 4]); xtb = None
        if xT_spill is not None:
            xtb = [P.sb(st, "xtb%d" % i, [128, KT, 128], F32) for i in range(2)]
        bank = 0
        for i in range(ntile):
            x_, xs_ = xt[i % 2], xs[i % 2]
            xk, xsk = 'xt%d' % (i % 2), 'xs%d' % (i % 2)
            S.dma('sp', x_[:], x_dram[i * 128:(i + 1) * 128, :], writes=[xk])
            S.op('act', lambda e, x_=x_, xs_=xs_: e.activation(out=xs_[:], in_=x_[:], func=AF.Square, accum_out=ss[:, 0:1]), reads=[xk], writes=[xsk, 'ss'])
            ts(ss[:, 1:2], ss[:, 0:1], 1.0 / D, EPS, ALU.mult, ALU.add, rd=['ss'], wr=['ss'])
            S.op('act', lambda e: e.activation(out=ss[:, 2:3], in_=ss[:, 1:2], func=AF.Sqrt), reads=['ss'], writes=['ss'])
            S.op('dve', lambda e: e.reciprocal(out=ss[:, 3:4], in_=ss[:, 2:3]), reads=['ss'], writes=['ss'])
            S.op('act', lambda e, x_=x_, xs_=xs_: e.activation(out=xs_[:], in_=x_[:], func=AF.Identity, scale=ss[:, 3:4]), reads=[xk, 'ss'], writes=[xsk])
            A, B, abk = AB_of_tile(i)
            for k4 in range(KT // 4):
                b = bank % 8; bank += 1
                for q in range(4):
                    kt = k4 * 4 + q
                    S.op('pe', lambda e, o_=ps[:, b, q * 128:(q + 1) * 128], i_=xs_[:, kt * 128:(kt + 1) * 128]: e.transpose(o_, i_, ident[:]),
                         reads=[xsk, 'ident'], writes=[('ps', b)], inc=(q == 3))
                nc_ = min(128, hT.shape[2] - i * 128)
                for q in range(4):
                    kt = k4 * 4 + q
                    S.op('act', lambda e, o_=hT[:, kt, i * 128:i * 128 + nc_], i_=ps[:, b, q * 128:q * 128 + nc_], A=A, B=B, kt=kt:
                         e.activation(out=o_, in_=i_, func=AF.Identity, scale=A[:, kt:kt + 1], bias=B[:, kt:kt + 1]),
                         reads=[('ps', b)] + abk, writes=[hkey])
            if xT_spill is not None:
                xb_ = xtb[i % 2]; xbk = 'xtb%d' % (i % 2)
                for k4 in range(KT // 4):
                    b = bank % 8; bank += 1
                    for q in range(4):
                        kt = k4 * 4 + q
                        S.op('pe', lambda e, o_=ps[:, b, q * 128:(q + 1) * 128], i_=x_[:, kt * 128:(kt + 1) * 128]: e.transpose(o_, i_, ident[:]),
                             reads=[xk, 'ident'], writes=[('ps', b)], inc=(q == 3))
                    S.op('dve', lambda e, o_=xb_[:, k4 * 4:(k4 + 1) * 4, :], i_=ps[:, b, :].rearrange("p (k t) -> p k t", k=4): e.tensor_copy(out=o_, in_=i_),
                         reads=[('ps', b)], writes=[xbk])
                n = min(128, NTP - i * 128)
                S.dma('pool', xT_spill[:, i * 128:i * 128 + n].rearrange("(k p) t -> p k t", p=128), xb_[:, :, 0:n], reads=[xbk], writes=['xT_spill'])
        S.barrier()

    def spill_epi(dst, width, row0_of, func, out_dt, tok_groups, nbuf_tag):
        bufs = [P.sb(cur_st[0], "%s_o%d" % (nbuf_tag, i), [128, width], out_dt) for i in range(2)]
        state = {'i': 0}

        def epi(si, ch, pss, dst=dst, r0=None, func=func):
            i = state['i']; state['i'] += 1
            bt = bufs[i % 2]; bk = "%s_o%d" % (nbuf_tag, i % 2)
            for (t0, n), (pap, pk) in zip(tok_groups, pss):
                P.evac(bt[:, t0:t0 + n], pap, [pk], [bk], func=func)
            if r0 is None:
                r0 = row0_of(si, ch)
            ntok = tok_groups[-1][0] + tok_groups[-1][1]
            S.dma('pool', dst[r0:r0 + 128, 0:ntok], bt[:, 0:ntok], reads=[bk], writes=[('dram', id(dst))])
        return epi

    cur_st = [None]

    def rope_epi(dst, Ct, St, ckeys, width, tok_groups, tag):
        kb = [P.sb(cur_st[0], "%s_kb%d" % (tag, i), [128, width], BF16) for i in range(1)]
        t1 = [P.sb(cur_st[0], "%s_t1%d" % (tag, i), [128, width], F32) for i in range(1)]
        t2 = [P.sb(cur_st[0], "%s_t2%d" % (tag, i), [128, width], F32) for i in range(1)]
        ob = [P.sb(cur_st[0], "%s_ob%d" % (tag, i), [128, width], BF16) for i in range(1)]
        state = {'i': 0}

        def epi(si, ch, pss, head, dst=dst, Ct=Ct, St=St, ckeys=ckeys):
            i = state['i']; state['i'] += 1
            j = 0
            kbk, t1k, t2k, obk = "%s_kb%d" % (tag, j), "%s_t1%d" % (tag, j), "%s_t2%d" % (tag, j), "%s_ob%d" % (tag, j)
            for gi, ((t0, n), (pap, pk)) in enumerate(zip(tok_groups, pss)):
                S.op('act', lambda e, o_=kb[j][:, t0:t0 + n], i_=pap: e.activation(out=o_, in_=i_, func=AF.Copy), reads=[pk], writes=[kbk])
                tt(t1[j][:, t0:t0 + n], pap, Ct[:, t0:t0 + n], ALU.mult, rd=[pk] + ckeys, wr=[t1k])
                rb = 6 + (gi % 2)
                import os
                if os.environ.get('ROPEDBG') == 'noR':
                    tt(t2[j][:, t0:t0 + n], pap, St[:, t0:t0 + n], ALU.mult, rd=[pk] + ckeys, wr=[t2k])
                else:
                    S.op('pe', lambda e, o_=ps[:, rb, 0:n], r_=kb[j][:, t0:t0 + n]: e.matmul(o_, lhsT=rmat[:], rhs=r_, start=True, stop=True),
                         reads=[kbk, 'rmat'], writes=[('ps', rb)])
                    tt(t2[j][:, t0:t0 + n], ps[:, rb, 0:n], St[:, t0:t0 + n], ALU.mult, rd=[('ps', rb)] + ckeys, wr=[t2k])
                tt(ob[j][:, t0:t0 + n], t1[j][:, t0:t0 + n], t2[j][:, t0:t0 + n], ALU.add, rd=[t1k, t2k], wr=[obk])
            ntok = tok_groups[-1][0] + tok_groups[-1][1]
            S.dma('pool', dst[head * 128:(head + 1) * 128, 0:ntok], ob[j][:, 0:ntok], reads=[obk], writes=[('dram', id(dst))])
        return epi

    PB = [0, 1, 2, 3, 4, 5]
    with contextlib.ExitStack() as st:
        cur_st[0] = st
        hTp = P.sb(st, "hTp", [128, KT, NPFP], BF16)
        with contextlib.ExitStack() as st2:
            norm_tm(st2, x_pre, 10, hTp, 'hTp', lambda i: (A1c, B1c, ['A1c', 'B1c']) if i < 2 else (A1l, B1l, ['A1l', 'B1l']), None)
        if 'hTp' in P.dbg:
            dz = nc.dram_tensor('dbg_hTp', [128, KT, NPFP], BF16, kind='ExternalOutput').ap()
            S.dma('sp', dz, hTp[:], reads=['hTp']); S.barrier()
            if stop_after <= 1.5:
                S.finish(); return nc
        tgp = [(0, 428), (428, 426), (854, 425)]
        pin_slabs = [[(s_ * 256, 256)] for s_ in range(40)]
        if 'few' in P.dbg:
            import os; pin_slabs = [pin_slabs[int(i)] for i in os.environ.get('FEW', '0,1,2,8,9,24,25').split(',')]
        e_k = rope_epi(kT_p, CKp, SKp, ['CKp', 'SKp'], NPFP, tgp, 'pk')
        e_v = spill_epi(vT_p, NPFP, lambda si, ch: pin_slabs[si][0][0] + ch * 128 - 2048, None, BF16, tgp, 'pv')
        e_x = spill_epi(xl_p, NPFP, lambda si, ch: pin_slabs[si][0][0] + ch * 128 - 6144, None, F32, tgp, 'px')

        def epi_p(si, ch, pss):
            c0 = pin_slabs[si][0][0] + ch * 128
            if c0 < 2048:
                e_k(si, ch, pss, c0 // 128)
            elif c0 < 6144:
                e_v(si, ch, pss)
            else:
                e_x(si, ch, pss)
        P.proj_fm('pin', w_in, D, pin_slabs, hTp, 'hTp', tgp, epi_p, PB, ps)
    if stop_after <= 2:
        S.finish(); return nc

    tgm = [(0, 342), (342, 342), (684, 341)]
    with contextlib.ExitStack() as st:
        cur_st[0] = st
        hTm = P.sb(st, "hTm", [128, KT, NTP], BF16)
        with contextlib.ExitStack() as st2:
            norm_tm(st2, x_main, 9, hTm, 'hTm', lambda i: (A1l, B1l, ['A1l', 'B1l']), xT_m)
        r_e = rope_epi(None, None, None, None, NTP, tgm, 'mk')
        e_b = spill_epi(None, NTP, None, None, BF16, tgm, 'mvb')
        e_f = spill_epi(None, NTP, None, None, F32, tgm, 'mvf')

        def epi_m(si, ch, pss):
            c0 = si * 256 + ch * 128
            if c0 < 2048:
                r_e(si, ch, pss, c0 // 128, dst=kT_m, Ct=CKm, St=SKm, ckeys=['CKm', 'SKm'])
            elif c0 < 6144:
                e_b(si, ch, pss, dst=vT_m, r0=c0 - 2048, func=None)
            elif c0 < 10240:
                e_f(si, ch, pss, dst=xl_m, r0=c0 - 6144, func=None)
            elif c0 < 12288:
                r_e(si, ch, pss, (c0 - 10240) // 128, dst=qT_m, Ct=CQ, St=SQ, ckeys=['CQ', 'SQ'])
            elif c0 < 16384:
                e_f(si, ch, pss, dst=sg_m, r0=c0 - 12288, func=AF.Silu)
            elif c0 < 20480:
                e_f(si, ch, pss, dst=gg_m, r0=c0 - 16384, func=AF.Gelu)
            elif c0 < 24576:
                e_f(si, ch, pss, dst=sr_m, r0=c0 - 20480, func=AF.Sigmoid)
            else:
                e_f(si, ch, pss, dst=sl_m, r0=c0 - 24576, func=AF.Sigmoid)
        P.proj_fm('min', w_in, D, [[(s * 256, 256)] for s in range(112)], hTm, 'hTm', tgm, epi_m, PB, ps)
    ro.close()
    S.barrier()
    if stop_after <= 3:
        S.finish(); return nc

    with contextlib.ExitStack() as st:
        cz = P.sb(st, "cz", [128, 2, KT]); bat = P.sb(st, "bat", [128, 2, KT]); bxt = P.sb(st, "bxt", [128, 2, KT])
        c5 = P.sb(st, "c5", [128, KT, 5]); cb5 = P.sb(st, "cb5", [128, KT])
        S.dma('sp', cz[:], lru_lamT, writes=['cz'])
        S.dma('sp', bat[:], lru_baT, writes=['bat'])
        S.dma('sp', bxt[:], lru_bxT, writes=['bxt'])
        S.dma('sp', c5[:], conv5T, writes=['c5'])
        S.dma('sp', cb5[:], convbT, writes=['cb5'])
        S.op('act', lambda e: e.activation(out=cz[:], in_=cz[:], func=AF.Exp, scale=-1.0), reads=['cz'], writes=['cz'])
        S.op('act', lambda e: e.activation(out=cz[:], in_=cz[:], func=AF.Ln, bias=1.0), reads=['cz'], writes=['cz'])
        ts(cz[:], cz[:], -8.0, None, ALU.mult, rd=['cz'], wr=['cz'])
        xlat_ = P.sb(st, "xlat_", [128, 2, 2052]); xctx_ = P.sb(st, "xctx_", [128, 2, 260])
        xcl = P.sb(st, "xcl", [128, 2, 2048]); xcc = P.sb(st, "xcc", [128, 2, 256])
        xbl = P.sb(st, "xbl", [128, 2, 2048], BF16); xbc = P.sb(st, "xbc", [128, 2, 256], BF16)
        wf = P.sb(st, "wf", [128, 4, 2, 256]); wbf = P.sb(st, "wbf", [128, 4, 2, 256], BF16)
        aA = P.sb(st, "aA", [128, 256 + NTP]); bA = P.sb(st, "bA", [128, 256 + NTP]); hA = P.sb(st, "hA", [128, 256 + NTP])
        aB = P.sb(st, "aB", [128, 2304]); bB = P.sb(st, "bB", [128, 2304]); hB = P.sb(st, "hB", [128, 2304])
        tr = P.sb(st, "tr", [128, 512]); ti = P.sb(st, "ti", [128, 512]); tq = P.sb(st, "tq", [128, 512])
        ylo = [P.sb(st, "ylo%d" % i, [128, NTP]) for i in range(2)]
        S.op('dve', lambda e: e.memset(xlat_[:], 0.0), writes=['xlat_'])
        S.op('dve', lambda e: e.memset(xctx_[:], 0.0), writes=['xctx_'])
        pb = 0
        for n in range(16):
            for i2 in range(2):
                r0 = (2 * n + i2) * 128
                S.dma('pool', xlat_[:, i2, 2:2 + NT], xl_m[r0:r0 + 128, 0:NT], writes=['xlat_'])
                S.dma('pool', xlat_[:, i2, 2 + NT:2 + 2048], xl_p[r0:r0 + 128, 256:NPF], writes=['xlat_'])
                S.dma('pool', xctx_[:, i2, 2:258], xl_p[r0:r0 + 128, 0:256], writes=['xctx_'])
            for z in range(2):
                for gi, wsrc in enumerate((lru_wa, lru_wx)):
                    S.dma('sp', wf[:, z * 2 + gi, :, :], wsrc[z, n].rearrange("(k p) f -> p k f", p=128), writes=['wf'])
            S.op('act', lambda e: e.activation(out=wbf[:], in_=wf[:], func=AF.Copy), reads=['wf'], writes=['wbf'])
            for i2 in range(2):
                kt = 2 * n + i2
                for (src, dst, dstb, L, sk, dk, dbk) in ((xlat_, xcl, xbl, 2048, 'xlat_', 'xcl', 'xbl'), (xctx_, xcc, xbc, 256, 'xctx_', 'xcc', 'xbc')):
                    S.op('dve', lambda e, src=src, dst=dst, L=L, i2=i2, kt=kt: e.tensor_scalar(out=dst[:, i2, :], in0=src[:, i2, 0:L], scalar1=c5[:, kt, 0:1], scalar2=cb5[:, kt:kt + 1], op0=ALU.mult, op1=ALU.add),
                         reads=[sk, 'c5', 'cb5'], writes=[dk])
                    for j in range(1, 5):
                        S.op('dve', lambda e, src=src, dst=dst, L=L, i2=i2, kt=kt, j=j: e.scalar_tensor_tensor(out=dst[:, i2, :], in0=src[:, i2, j:j + L], scalar=c5[:, kt, j:j + 1], in1=dst[:, i2, :], op0=ALU.mult, op1=ALU.add),
                             reads=[sk, 'c5', dk], writes=[dk])
                    S.op('act', lambda e, dst=dst, dstb=dstb, i2=i2: e.activation(out=dstb[:, i2, :], in_=dst[:, i2, :], func=AF.Copy), reads=[dk], writes=[dbk])
            for j2 in range(2):
                kt = 2 * n + j2
                for z in range(2):
                    if z == 0:
                        segs = [(xcc, xbc, 'xbc', 'xcc', 0, 256, 0), (xcl, xbl, 'xbl', 'xcl', 0, 512, 256), (xcl, xbl, 'xbl', 'xcl', 512, 512, 768), (xcl, xbl, 'xbl', 'xcl', 1024, 2, 1280)]
                        ab, bb_, abk, bbk = aA, bA, 'aA', 'bA'
                    else:
                        segs = [(xcl, xbl, 'xbl', 'xcl', q * 512, 512, q * 512) for q in range(4)] + [(xcc, xbc, 'xbc', 'xcc', 0, 256, 2048)]
                        ab, bb_, abk, bbk = aB, bB, 'aB', 'bB'
                    for (xf, xb_, xbk, xfk, c0, nn, o0) in segs:
                        b_r = pb % 8; b_i = (pb + 1) % 8; pb += 2
                        for gi, bnk in ((0, b_r), (1, b_i)):
                            for i2 in range(2):
                                S.op('pe', lambda e, o_=ps[:, bnk, 0:nn], l=wbf[:, z * 2 + gi, i2, j2 * 128:(j2 + 1) * 128], r=xb_[:, i2, c0:c0 + nn], i2=i2:
                                     e.matmul(o_, lhsT=l, rhs=r, start=(i2 == 0), stop=(i2 == 1)),
                                     reads=['wbf', xbk], writes=[('ps', bnk)], inc=(i2 == 1))
                        S.op('act', lambda e, nn=nn, b_r=b_r, z=z, kt=kt: e.activation(out=tr[:, 0:nn], in_=ps[:, b_r, 0:nn], func=AF.Sigmoid, bias=bat[:, z, kt:kt + 1]),
                             reads=[('ps', b_r), 'bat'], writes=['tr'])
                        S.op('act', lambda e, nn=nn, b_i=b_i, z=z, kt=kt: e.activation(out=ti[:, 0:nn], in_=ps[:, b_i, 0:nn], func=AF.Sigmoid, bias=bxt[:, z, kt:kt + 1]),
                             reads=[('ps', b_i), 'bxt'], writes=['ti'])
                        S.op('act', lambda e, nn=nn, ab=ab, o0=o0, z=z, kt=kt: e.activation(out=ab[:, o0:o0 + nn], in_=tr[:, 0:nn], func=AF.Exp, scale=cz[:, z, kt:kt + 1]),
                             reads=['tr', 'cz'], writes=[abk])
                        tt(tq[:, 0:nn], ab[:, o0:o0 + nn], ab[:, o0:o0 + nn], ALU.mult, rd=[abk], wr=['tq'])
                        S.op('act', lambda e, nn=nn: e.activation(out=tq[:, 0:nn], in_=tq[:, 0:nn], func=AF.Sqrt, scale=-1.0, bias=1.0), reads=['tq'], writes=['tq'])
                        tt(tq[:, 0:nn], tq[:, 0:nn], ti[:, 0:nn], ALU.mult, rd=['tq', 'ti'], wr=['tq'])
                        tt(bb_[:, o0:o0 + nn], tq[:, 0:nn], xf[:, j2, c0:c0 + nn], ALU.mult, rd=['tq', xfk], wr=[bbk])
                LA = 256 + NT
                S.op('dve', lambda e: e.tensor_tensor_scan(out=hA[:, 0:LA], data0=aA[:, 0:LA], data1=bA[:, 0:LA], initial=0.0, op0=ALU.mult, op1=ALU.add),
                     reads=['aA', 'bA'], writes=['hA'])
                S.op('dve', lambda e: e.tensor_tensor_scan(out=hB[:, ::-1], data0=aB[:, ::-1], data1=bB[:, ::-1], initial=0.0, op0=ALU.mult, op1=ALU.add),
                     reads=['aB', 'bB'], writes=['hB'])
                yo = ylo[kt % 2]; yok = 'ylo%d' % (kt % 2)
                tt(yo[:, 0:NT], hA[:, 256:256 + NT], hB[:, 0:NT], ALU.add, rd=['hA', 'hB'], wr=[yok])
                S.dma('pool', yl_s[kt * 128:(kt + 1) * 128, 0:NT], yo[:, 0:NT], reads=[yok], writes=['yl_s'])
        S.barrier()
    if stop_after <= 4:
        S.finish(); return nc

    zst = contextlib.ExitStack()
    zT = P.sb(zst, "zT", [128, KT, NTP], BF16)
    with contextlib.ExitStack() as st:
        jpi = P.sb(st, "jpi", [128, 1], I32); jp = P.sb(st, "jp", [128, 1])
        reli = P.sb(st, "reli", [128, 128], I32); rel = P.sb(st, "rel", [128, 128])
        relp = P.sb(st, "relp", [128, 128]); reln = P.sb(st, "reln", [128, 128])
        indA = P.sb(st, "indA", [128, 128]); indB = P.sb(st, "indB", [128, 128]); mtmp = P.sb(st, "mtmp", [128, 128])
        Mall = P.sb(st, "Mall", [128, H, 128])
        ev = P.sb(st, "ev", [128, 8])
        kdA = P.sb(st, "kdA", [128, H]); kdB = P.sb(st, "kdB", [128, H]); qdA = P.sb(st, "qdA", [128, H]); qdB = P.sb(st, "qdB", [128, H])
        cdA = P.sb(st, "cdA", [128, H]); cdB = P.sb(st, "cdB", [128, H]); g1A = P.sb(st, "g1A", [128, H]); g1B = P.sb(st, "g1B", [128, H])
        eA = P.sb(st, "eA", [128, 2]); eB = P.sb(st, "eB", [128, 10])
        wA = P.sb(st, "wA", [128, H, 2]); wB = P.sb(st, "wB", [128, H, 10])
        onesc = P.sb(st, "onesc", [128, 1])
        S.op('pool', lambda e: e.iota(jpi[:], pattern=[[0, 1]], base=0, channel_multiplier=1), writes=['jpi'])
        S.op('pool', lambda e: e.iota(reli[:], pattern=[[1, 128]], base=0, channel_multiplier=-1), writes=['reli'])
        S.op('dve', lambda e: e.tensor_copy(out=jp[:], in_=jpi[:]), reads=['jpi'], writes=['jp'])
        S.op('dve', lambda e: e.tensor_copy(out=rel[:], in_=reli[:]), reads=['reli'], writes=['rel'])
        S.op('dve', lambda e: e.memset(onesc[:], 1.0), writes=['onesc'])
        ts(relp[:], rel[:], 0.0, None, ALU.max, rd=['rel'], wr=['relp'])
        ts(reln[:], rel[:], -1.0, 0.0, ALU.mult, ALU.max, rd=['rel'], wr=['reln'])
        ts(indA[:], rel[:], 0.0, None, ALU.is_ge, rd=['rel'], wr=['indA'])
        ts(indB[:], rel[:], 0.0, None, ALU.is_le, rd=['rel'], wr=['indB'])
        ts(ev[:, 0:1], jp[:], -1.0, 127.0, ALU.mult, ALU.add, rd=['jp'], wr=['ev'])
        S.op('dve', lambda e: e.tensor_copy(out=ev[:, 1:2], in_=jp[:]), reads=['jp'], writes=['ev'])
        ts(ev[:, 2:3], jp[:], 1.0, None, ALU.add, rd=['jp'], wr=['ev'])
        ts(ev[:, 3:4], jp[:], -1.0, 128.0, ALU.mult, ALU.add, rd=['jp'], wr=['ev'])
        S.op('dve', lambda e: e.memset(ev[:, 4:5], 128.0), writes=['ev'])
        S.op('dve', lambda e: e.memset(ev[:, 5:6], 1.0), writes=['ev'])
        for (dst, z, col, nm) in ((kdA, 0, 0, 'kdA'), (kdB, 1, 1, 'kdB'), (qdA, 0, 2, 'qdA'), (qdB, 1, 3, 'qdB'), (cdA, 0, 4, 'cdA'), (cdB, 1, 4, 'cdB'), (g1A, 0, 5, 'g1A'), (g1B, 1, 5, 'g1B')):
            S.op('act', lambda e, dst=dst, z=z, col=col: e.activation(out=dst[:], in_=lg[:, z * H:(z + 1) * H], func=AF.Exp, scale=ev[:, col:col + 1]),
                 reads=['lg', 'ev'], writes=[nm])
        for c in range(10):
            if c < 2:
                ts(eA[:, c:c + 1], jp[:], -1.0, float(255 - c * 128), ALU.mult, ALU.add, rd=['jp'], wr=['eA'])
                ts(eB[:, c:c + 1], jp[:], float(c * 128 + 1023), None, ALU.add, rd=['jp'], wr=['eB'])
            else:
                ts(eB[:, c:c + 1], jp[:], float(c * 128 - 256), None, ALU.add, rd=['jp'], wr=['eB'])
        for h in range(H):
            S.op('act', lambda e, h=h: e.activation(out=wA[:, h, :], in_=eA[:], func=AF.Exp, scale=lg[:, h:h + 1]), reads=['eA', 'lg'], writes=['wA'])
            S.op('act', lambda e, h=h: e.activation(out=wB[:, h, :], in_=eB[:], func=AF.Exp, scale=lg[:, H + h:H + h + 1]), reads=['eB', 'lg'], writes=['wB'])
            S.op('act', lambda e, h=h: e.activation(out=Mall[:, h, :], in_=relp[:], func=AF.Exp, scale=lg[:, h:h + 1]), reads=['relp', 'lg'], writes=['Mall'])
            tt(Mall[:, h, :], Mall[:, h, :], indA[:], ALU.mult, rd=['Mall', 'indA'], wr=['Mall'])
            S.op('act', lambda e, h=h: e.activation(out=mtmp[:], in_=reln[:], func=AF.Exp, scale=lg[:, H + h:H + h + 1]), reads=['reln', 'lg'], writes=['mtmp'])
            tt(mtmp[:], mtmp[:], indB[:], ALU.mult, rd=['mtmp', 'indB'], wr=['mtmp'])
            tt(Mall[:, h, :], Mall[:, h, :], mtmp[:], ALU.add, rd=['Mall', 'mtmp'], wr=['Mall'])

        kTp = P.sb(st, "kTp", [128, NPFP], BF16); vTp = P.sb(st, "vTp", [128, 2, NPFP], BF16)
        kTm = P.sb(st, "kTm", [128, NTP], BF16); qTm = P.sb(st, "qTm", [128, NTP], BF16); vTm = P.sb(st, "vTm", [128, 2, NTP], BF16)
        sgh = P.sb(st, "sgh", [128, 2, NTP])
        kAc = P.sb(st, "kAc", [128, 128], BF16); kBc = P.sb(st, "kBc", [128, 128], BF16); vc = P.sb(st, "vc", [128, 256], BF16)
        kBall = P.sb(st, "kBall", [128, 9, 128], BF16); vtok = P.sb(st, "vtok", [128, 9, 256], BF16)
        SA = P.sb(st, "SA", [128, 256]); SB = P.sb(st, "SB", [128, 256]); SB16 = P.sb(st, "SB16", [128, 256], BF16)
        SAall = P.sb(st, "SAall", [128, 9, 256], BF16)
        PT = P.sb(st, "PT", [128, 128], BF16); osb = P.sb(st, "osb", [128, 256]); on = P.sb(st, "on", [128, 256])
        bst = P.sb(st, "bst", [128, 6]); mv = P.sb(st, "mv", [128, 4])
        psb = ps[:, :, :].bitcast(BF16)

        def tr_bf(bank, off, in_ap, n, rd):
            S.op('pe', lambda e: e.transpose(psb[0:n, bank, off:off + 128], in_ap, identb[:]), reads=rd + ['identb'], writes=[('ps', bank)])
            return psb[0:n, bank, off:off + 128]

        for h in range(H):
            S.dma('pool', kTp[:], kT_p[h * 128:(h + 1) * 128, :], writes=['kTp'])
            S.dma('pool', vTp[:], vT_p[h * 256:(h + 1) * 256, :].rearrange("(e p) t -> p e t", p=128), writes=['vTp'])
            S.dma('pool', kTm[:], kT_m[h * 128:(h + 1) * 128, :], writes=['kTm'])
            S.dma('pool', qTm[:], qT_m[h * 128:(h + 1) * 128, :], writes=['qTm'])
            S.dma('pool', vTm[:], vT_m[h * 256:(h + 1) * 256, :].rearrange("(e p) t -> p e t", p=128), writes=['vTm'])
            S.dma('pool', sgh[:], sg_m[h * 256:(h + 1) * 256, :].rearrange("(e p) t -> p e t", p=128), writes=['sgh'])
            for c in range(10):
                n = min(128, NPF - c * 128)
                pk_ = tr_bf(0, 0, kTp[:, c * 128:c * 128 + n], n, ['kTp'])
                S.op('act', lambda e, n=n, c=c, h=h, pk_=pk_: e.activation(out=kBc[0:n, :], in_=pk_, func=AF.Copy, scale=wB[0:n, h, c:c + 1]), reads=[('ps', 0), 'wB'], writes=['kBc'])
                if c < 2:
                    S.op('dve', lambda e, n=n, c=c, h=h, pk_=pk_: e.tensor_scalar(out=kAc[0:n, :], in0=pk_, scalar1=wA[0:n, h, c:c + 1], scalar2=None, op0=ALU.mult), reads=[('ps', 0), 'wA'], writes=['kAc'])
                for e2 in range(2):
                    tr_bf(1, e2 * 128, vTp[:, e2, c * 128:c * 128 + n], n, ['vTp'])
                S.op('dve', lambda e, n=n: e.tensor_copy(out=vc[0:n, :], in_=psb[0:n, 1, 0:256]), reads=[('ps', 1)], writes=['vc'])
                S.op('pe', lambda e, n=n, c=c: e.matmul(ps[:, 7, 0:256], lhsT=kBc[0:n, :], rhs=vc[0:n, :], start=(c == 0), stop=(c == 9)),
                     reads=['kBc', 'vc'], writes=[('ps', 7)])
                if c < 2:
                    S.op('pe', lambda e, n=n, c=c: e.matmul(ps[:, 6, 0:256], lhsT=kAc[0:n, :], rhs=vc[0:n, :], start=(c == 0), stop=(c == 1)),
                         reads=['kAc', 'vc'], writes=[('ps', 6)])
            S.op('act', lambda e: e.activation(out=SA[:], in_=ps[:, 6, 0:256], func=AF.Copy), reads=[('ps', 6)], writes=['SA'])
            S.op('dve', lambda e: e.tensor_copy(out=SB[:], in_=ps[:, 7, 0:256]), reads=[('ps', 7)], writes=['SB'])
            S.op('act', lambda e: e.activation(out=SB16[:], in_=SB[:], func=AF.Copy), reads=['SB'], writes=['SB16'])
            for c in range(9):
                n = 128 if c < 8 else 1
                pk_ = tr_bf(0, 0, kTm[:, c * 128:c * 128 + n], n, ['kTm'])
                sA = kdA[0:n, h:h + 1] if c < 8 else onesc[0:n, 0:1]
                S.op('act', lambda e, n=n, pk_=pk_, sA=sA: e.activation(out=kAc[0:n, :], in_=pk_, func=AF.Copy, scale=sA), reads=[('ps', 0), 'kdA', 'onesc'], writes=['kAc'])
                S.op('dve', lambda e, n=n, c=c, h=h, pk_=pk_: e.tensor_scalar(out=kBall[0:n, c, :], in0=pk_, scalar1=kdB[0:n, h:h + 1], scalar2=None, op0=ALU.mult), reads=[('ps', 0), 'kdB'], writes=['kBall'])
                for e2 in range(2):
                    tr_bf(1, e2 * 128, vTm[:, e2, c * 128:c * 128 + n], n, ['vTm'])
                S.op('dve', lambda e, n=n, c=c: e.tensor_copy(out=vtok[0:n, c, :], in_=psb[0:n, 1, 0:256]), reads=[('ps', 1)], writes=['vtok'])
                S.op('act', lambda e, c=c: e.activation(out=SAall[:, c, :], in_=SA[:], func=AF.Copy), reads=['SA'], writes=['SAall'])
                if c < 8:
                    S.op('pe', lambda e, n=n, c=c: e.matmul(ps[:, 2, 0:256], lhsT=kAc[0:n, :], rhs=vtok[0:n, c, :], start=True, stop=True),
                         reads=['kAc', 'vtok'], writes=[('ps', 2)])
                    S.op('dve', lambda e, h=h: e.scalar_tensor_tensor(out=SA[:], in0=SA[:], scalar=cdA[:, h:h + 1], in1=ps[:, 2, 0:256], op0=ALU.mult, op1=ALU.add),
                         reads=['SA', 'cdA', ('ps', 2)], writes=['SA'])
            for c in range(8, -1, -1):
                n = 128 if c < 8 else 1
                t0 = c * 128
                S.op('pe', lambda e, n=n, t0=t0: e.matmul(ps[0:n, 3, 0:n], lhsT=kTm[:, t0:t0 + n], rhs=qTm[:, t0:t0 + n], start=True, stop=True),
                     reads=['kTm', 'qTm'], writes=[('ps', 3)])
                tt(PT[0:n, 0:n], ps[0:n, 3, 0:n], Mall[0:n, h, 0:n], ALU.mult, rd=[('ps', 3), 'Mall'], wr=['PT'])
                S.op('pe', lambda e, n=n, c=c: e.matmul(ps[0:n, 4, 0:256], lhsT=PT[0:n, 0:n], rhs=vtok[0:n, c, :], start=True, stop=True),
                     reads=['PT', 'vtok'], writes=[('ps', 4)])
                S.op('pe', lambda e, n=n, c=c, t0=t0: e.matmul(ps[0:n, 5, 0:256], lhsT=qTm[:, t0:t0 + n], rhs=SAall[:, c, :], start=True, stop=True),
                     reads=['qTm', 'SAall'], writes=[('ps', 5)])
                S.op('pe', lambda e, n=n, t0=t0: e.matmul(ps[0:n, 5, 256:512], lhsT=qTm[:, t0:t0 + n], rhs=SB16[:], start=True, stop=True),
                     reads=['qTm', 'SB16'], writes=[('ps', 5)])
                S.op('act', lambda e, n=n: e.activation(out=osb[0:n, :], in_=ps[0:n, 4, 0:256], func=AF.Copy), reads=[('ps', 4)], writes=['osb'])
                sqB = qdB[0:n, h:h + 1] if c < 8 else g1B[0:n, h:h + 1]
                S.op('dve', lambda e, n=n, h=h: e.scalar_tensor_tensor(out=osb[0:n, :], in0=ps[0:n, 5, 0:256], scalar=qdA[0:n, h:h + 1], in1=osb[0:n, :], op0=ALU.mult, op1=ALU.add),
                     reads=[('ps', 5), 'qdA', 'osb'], writes=['osb'])
                S.op('dve', lambda e, n=n, sqB=sqB: e.scalar_tensor_tensor(out=osb[0:n, :], in0=ps[0:n, 5, 256:512], scalar=sqB, in1=osb[0:n, :], op0=ALU.mult, op1=ALU.add),
                     reads=[('ps', 5), 'qdB', 'g1B', 'osb'], writes=['osb'])
                S.op('dve', lambda e, n=n: e.bn_stats(out=bst[0:n, :], in_=osb[0:n, :]), reads=['osb'], writes=['bst'])
                S.op('dve', lambda e, n=n: e.bn_aggr(out=mv[0:n, 0:2], in_=bst[0:n, :]), reads=['bst'], writes=['mv'])
                ts(mv[0:n, 2:3], mv[0:n, 1:2], EPS, None, ALU.add, rd=['mv'], wr=['mv'])
                S.op('act', lambda e, n=n: e.activation(out=mv[0:n, 2:3], in_=mv[0:n, 2:3], func=AF.Sqrt), reads=['mv'], writes=['mv'])
                S.op('dve', lambda e, n=n: e.reciprocal(out=mv[0:n, 3:4], in_=mv[0:n, 2:3]), reads=['mv'], writes=['mv'])
                ts(on[0:n, :], osb[0:n, :], mv[0:n, 0:1], mv[0:n, 3:4], ALU.subtract, ALU.mult, rd=['osb', 'mv'], wr=['on'])
                for e2 in range(2):
                    S.op('pe', lambda e, n=n, e2=e2: e.transpose(ps[:, 2, e2 * 128:e2 * 128 + n], on[0:n, e2 * 128:(e2 + 1) * 128], ident[0:n, 0:n]),
                         reads=['on', 'ident'], writes=[('ps', 2)])
                    tt(zT[:, 2 * h + e2, t0:t0 + n], ps[:, 2, e2 * 128:e2 * 128 + n], sgh[:, e2, t0:t0 + n], ALU.mult, rd=[('ps', 2), 'sgh'], wr=['zT'])
                if c > 0:
                    S.op('pe', lambda e, n=n, c=c: e.matmul(ps[:, 6, 0:256], lhsT=kBall[0:n, c, :], rhs=vtok[0:n, c, :], start=True, stop=True),
                         reads=['kBall', 'vtok'], writes=[('ps', 6)])
                    scB = cdB[:, h:h + 1] if c < 8 else g1B[:, h:h + 1]
                    S.op('dve', lambda e, scB=scB: e.scalar_tensor_tensor(out=SB[:], in0=SB[:], scalar=scB, in1=ps[:, 6, 0:256], op0=ALU.mult, op1=ALU.add),
                         reads=['SB', 'cdB', 'g1B', ('ps', 6)], writes=['SB'])
                    S.op('act', lambda e: e.activation(out=SB16[:], in_=SB[:], func=AF.Copy), reads=['SB'], writes=['SB16'])
        S.barrier()
    if 'zT' in P.dbg:
        dz = nc.dram_tensor("dbg_zT", [128, KT, NTP], BF16, kind="ExternalOutput").ap()
        S.dma('sp', dz, zT[:], reads=['zT']); S.barrier()
    if stop_after <= 5:
        S.finish(); return nc

    def gated_epi(st, tag, gate_src, add_src, dst_dram, dst_sb, dst_key):
        gb = [P.sb(st, "%s_g%d" % (tag, i), [128, NTP]) for i in range(1)]
        ab_ = [P.sb(st, "%s_a%d" % (tag, i), [128, NTP]) for i in range(1)] if add_src is not None else None
        ob = [P.sb(st, "%s_ob%d" % (tag, i), [128, NTP]) for i in range(1)]
        state = {'i': 0}

        def epi(si, ch, pss):
            i = state['i']; state['i'] += 1
            j = 0
            cidx = si * 2 + ch
            r0 = cidx * 128
            gk, ak, ok = "%s_g%d" % (tag, j), "%s_a%d" % (tag, j), "%s_ob%d" % (tag, j)
            S.dma('pool', gb[j][:, 0:NT], gate_src[r0:r0 + 128, 0:NT], writes=[gk])
            if add_src is not None:
                S.dma('pool', ab_[j][:, 0:NT], add_src[r0:r0 + 128, 0:NT], writes=[ak])
            for (t0, n), (pap, pk) in zip(tgm, pss):
                tt(ob[j][:, t0:t0 + n], pap, gb[j][:, t0:t0 + n], ALU.mult, rd=[pk, gk], wr=[ok])
            if add_src is not None:
                tt(ob[j][:, 0:NT], ob[j][:, 0:NT], ab_[j][:, 0:NT], ALU.add, rd=[ok, ak], wr=[ok])
            if dst_dram is not None:
                S.dma('pool', dst_dram[r0:r0 + 128, 0:NT], ob[j][:, 0:NT], reads=[ok], writes=[('dram', id(dst_dram))])
            else:
                S.op('act', lambda e: e.activation(out=dst_sb[:, cidx, 0:NT], in_=ob[j][:, 0:NT], func=AF.Copy), reads=[ok], writes=[dst_key])
        return epi

    sq_slabs = [[(s_ * 256, 256)] for s_ in range(16)]
    with contextlib.ExitStack() as st:
        P.proj_fm('ro', w_ret_o, D, sq_slabs, zT, 'zT', tgm, gated_epi(st, 'ro', sr_m, None, yr_s, None, None), PB, ps)
    zst.close()
    S.barrier()
    yst = contextlib.ExitStack()
    yT = P.sb(yst, "yT", [128, KT, NTP], BF16)
    with contextlib.ExitStack() as st:
        uT = P.sb(st, "uT", [128, KT, NTP], BF16)
        with contextlib.ExitStack() as st2:
            g_ = [P.sb(st2, "u_g%d" % i, [128, NTP]) for i in range(2)]
            y_ = [P.sb(st2, "u_y%d" % i, [128, NTP]) for i in range(2)]
            for kt in range(KT):
                j = kt % 2
                S.dma('pool', g_[j][:, 0:NT], gg_m[kt * 128:(kt + 1) * 128, 0:NT], writes=['u_g%d' % j])
                S.dma('pool', y_[j][:, 0:NT], yl_s[kt * 128:(kt + 1) * 128, 0:NT], writes=['u_y%d' % j])
                tt(uT[:, kt, 0:NT], g_[j][:, 0:NT], y_[j][:, 0:NT], ALU.mult, rd=['u_g%d' % j, 'u_y%d' % j], wr=['uT'])
            S.barrier()
        P.proj_fm('lo', w_lru_o, D, sq_slabs, uT, 'uT', tgm, gated_epi(st, 'lo', sl_m, yr_s, None, yT, 'yT'), PB, ps)
    S.barrier()
    if 'yT' in P.dbg:
        dz = nc.dram_tensor("dbg_yT", [128, KT, NTP], BF16, kind="ExternalOutput").ap()
        S.dma('sp', dz, yT[:], reads=['yT']); S.barrier()
    if stop_after <= 6:
        S.finish(); return nc
    with contextlib.ExitStack() as st:
        ssacc = P.sb(st, "ssacc", [128, NTP])
        S.op('dve', lambda e: e.memset(ssacc[:], 0.0), writes=['ssacc'])
        xb2 = [P.sb(st, "wo_x%d" % i, [128, NTP]) for i in range(2)]
        ob2 = [P.sb(st, "wo_o%d" % i, [128, NTP]) for i in range(2)]
        sq2 = P.sb(st, "wo_sq", [128, NTP])
        stt = {'i': 0}

        def epi_wo(si, ch, pss):
            i = stt['i']; stt['i'] += 1
            j = i % 2
            cidx = si * 2 + ch
            r0 = cidx * 128
            xk, ok = 'wo_x%d' % j, 'wo_o%d' % j
            S.dma('pool', xb2[j][:, 0:NT], xT_m[r0:r0 + 128, 0:NT], writes=[xk])
            for (t0, n), (pap, pk) in zip(tgm, pss):
                S.op('dve', lambda e, t0=t0, n=n, pap=pap: e.scalar_tensor_tensor(out=ob2[j][:, t0:t0 + n], in0=pap, scalar=G1[:, cidx:cidx + 1], in1=xb2[j][:, t0:t0 + n], op0=ALU.mult, op1=ALU.add),
                     reads=[pk, 'G1', xk], writes=[ok])
            S.op('act', lambda e: e.activation(out=sq2[:, 0:NT], in_=ob2[j][:, 0:NT], func=AF.Square), reads=[ok], writes=['wo_sq'])
            tt(ssacc[:, 0:NT], ssacc[:, 0:NT], sq2[:, 0:NT], ALU.add, rd=['ssacc', 'wo_sq'], wr=['ssacc'])
            S.dma('pool', xlatT[r0:r0 + 128, 0:NT], ob2[j][:, 0:NT], reads=[ok], writes=['xlatT'])
        P.proj_fm('wo', w_out, D, sq_slabs, yT, 'yT', tgm, epi_wo, PB, ps)
        if stop_after <= 6.5:
            S.finish(); return nc
        for gi, (t0, n) in enumerate(tgm):
            S.op('pe', lambda e, gi=gi, t0=t0, n=n: e.matmul(ps[:, gi, 0:n], lhsT=ones[:], rhs=ssacc[:, t0:t0 + n], start=True, stop=True),
                 reads=['ones', 'ssacc'], writes=[('ps', gi)])
            S.op('act', lambda e, gi=gi, t0=t0, n=n: e.activation(out=rstd2[:, t0:t0 + n], in_=ps[:, gi, 0:n], func=AF.Sqrt, scale=1.0 / D, bias=epsc[:, 0:1]),
                 reads=[('ps', gi), 'epsc'], writes=['rstd2'])
        S.op('dve', lambda e: e.reciprocal(out=rstd2[:, 0:NT], in_=rstd2[:, 0:NT]), reads=['rstd2'], writes=['rstd2'])
        S.barrier()
        if stop_after <= 6.7:
            dz = nc.dram_tensor('dbg_rstd2', [128, NTP], F32, kind='ExternalOutput').ap()
            S.dma('sp', dz, rstd2[:], reads=['rstd2']); S.finish(); return nc
    yst.close()
    S.barrier()
    hst = contextlib.ExitStack()
    h2T = P.sb(hst, "h2T", [128, KT, NTP], BF16)
    with contextlib.ExitStack() as st:
        xb3 = [P.sb(st, "n2_x%d" % i, [128, NTP]) for i in range(2)]
        for kt in range(KT):
            j = kt % 2
            xk = 'n2_x%d' % j
            S.dma('pool', xb3[j][:, 0:NT], xlatT[kt * 128:(kt + 1) * 128, 0:NT], writes=[xk])
            tt(xb3[j][:, 0:NT], xb3[j][:, 0:NT], rstd2[:, 0:NT], ALU.mult, rd=[xk, 'rstd2'], wr=[xk])
            S.op('act', lambda e, j=j, kt=kt: e.activation(out=h2T[:, kt, 0:NT], in_=xb3[j][:, 0:NT], func=AF.Identity, scale=A2[:, kt:kt + 1], bias=B2[:, kt:kt + 1]),
                 reads=[xk, 'A2', 'B2'], writes=['h2T'])
        S.barrier()
    if 'h2T' in P.dbg:
        dz = nc.dram_tensor("dbg_h2T", [128, KT, NTP], BF16, kind="ExternalOutput").ap()
        S.dma('sp', dz, h2T[:], reads=['h2T']); S.barrier()
    if stop_after <= 7:
        S.finish(); return nc
    with contextlib.ExitStack() as st:
        fcw = P.sb(st, "fcw", [128, 2 * FFT, 3]); fcb = P.sb(st, "fcb", [128, 2 * FFT])
        S.dma('sp', fcw[:], fconvT, writes=['fcw'])
        S.dma('sp', fcb[:], fconvbT, writes=['fcb'])
        ub = [P.sb(st, "ub%d" % i, [128, NTP + 2]) for i in range(2)]
        acc = [P.sb(st, "acc%d" % i, [128, 1024]) for i in range(2)]
        sgt = [P.sb(st, "sgt%d" % i, [128, 1024]) for i in range(2)]
        ao = [P.sb(st, "ao%d" % i, [128, 1024], BF16) for i in range(2)]
        for i in range(2):
            S.op('dve', lambda e, i=i: e.memset(ub[i][:], 0.0), writes=['ub%d' % i])
        stt9 = {'i': 0}

        def epi_up(si, ch, pss):
            i = stt9['i']; stt9['i'] += 1
            j = i % 2
            isv = ch >= 2
            gch = 2 * si + (ch % 2)
            cw = gch + (FFT if isv else 0)
            ubk, acck = 'ub%d' % j, 'acc%d' % j
            for (t0, n), (pap, pk) in zip(tgm, pss):
                S.op('act', lambda e, t0=t0, n=n, pap=pap: e.activation(out=ub[j][:, 1 + t0:1 + t0 + n], in_=pap, func=AF.Copy), reads=[pk], writes=[ubk])
            S.op('dve', lambda e: e.tensor_scalar(out=acc[j][:], in0=ub[j][:, 0:1024], scalar1=fcw[:, cw, 0:1], scalar2=fcb[:, cw:cw + 1], op0=ALU.mult, op1=ALU.add),
                 reads=[ubk, 'fcw', 'fcb'], writes=[acck])
            for tap in (1, 2):
                S.op('dve', lambda e, tap=tap: e.scalar_tensor_tensor(out=acc[j][:], in0=ub[j][:, tap:tap + 1024], scalar=fcw[:, cw, tap:tap + 1], in1=acc[j][:], op0=ALU.mult, op1=ALU.add),
                     reads=[ubk, 'fcw', acck], writes=[acck])
            if not isv:
                S.op('act', lambda e: e.activation(out=sgt[ch][:], in_=acc[j][:], func=AF.Silu), reads=[acck], writes=['sgt%d' % ch])
            else:
                c2 = ch - 2
                tt(ao[c2][:], acc[j][:], sgt[c2][:], ALU.mult, rd=[acck, 'sgt%d' % c2], wr=['ao%d' % c2])
                S.dma('pool', a_scr[gch * 128:(gch + 1) * 128, :], ao[c2][:], reads=['ao%d' % c2], writes=['a_scr'])
        up_slabs = [[(s_ * 256, 256), (FF + s_ * 256, 256)] for s_ in range(43)]
        P.proj_fm('up', w_up, D, up_slabs, h2T, 'h2T', tgm, epi_up, PB, ps)
    hst.close()
    S.barrier()
    if 'a_scr' in P.dbg and stop_after <= 8:
        S.finish(); return nc
    with contextlib.ExitStack() as st:
        fnT_ = P.sb(st, "fnT_", [128, KT])
        S.dma('sp', fnT_[:], fnormT, writes=['fnT_'])
        KQ = 4
        wst = [P.sb(st, "dn_ws%d" % i, [128, KQ, 384]) for i in range(2)]
        wbb = [P.sb(st, "dn_wb%d" % i, [128, KQ, 384], BF16) for i in range(2)]
        abuf = [P.sb(st, "dn_a%d" % i, [128, KQ, 1024], BF16) for i in range(2)]
        xl4 = [P.sb(st, "dn_x%d" % i, [128, 1024]) for i in range(2)]
        xo4 = [P.sb(st, "dn_o%d" % i, [128, 1024]) for i in range(2)]
        xf4 = [P.sb(st, "dn_f%d" % i, [128, 1024]) for i in range(2)]
        sq4 = P.sb(st, "dn_sq", [128, 1024]); ssa = P.sb(st, "dn_ssa", [128, 1024])
        ot = [P.sb(st, "dn_ot%d" % i, [128, 8, 128]) for i in range(2)]
        rs3 = P.sb(st, "dn_rs3", [128, 8])
        big = [P.sb(st, "dn_big%d" % i, [128, D]) for i in range(2)]
        S.op('dve', lambda e: e.memset(ssa[:], 0.0), writes=['dn_ssa'])
        y_v = y_out.rearrange("(i p) f -> p i f", p=128)
        grp = 0
        ei = 0
        for cg in range(11):
            c0 = cg * 384
            ncol = min(384, D - c0)
            nch = ncol // 128
            nkq = (FFT + KQ - 1) // KQ
            for kq in range(nkq):
                k0 = kq * KQ
                kn = min(KQ, FFT - k0)
                j = grp % 2; grp += 1
                S.dma('sp', wst[j][:, 0:kn, 0:ncol], w_down[k0 * 128:(k0 + kn) * 128, c0:c0 + ncol].rearrange("(k p) c -> p k c", p=128), writes=['dn_ws%d' % j])
                S.dma('pool', abuf[j][:, 0:kn, :], a_scr[k0 * 128:(k0 + kn) * 128, :].rearrange("(k p) t -> p k t", p=128), reads=['a_scr'], writes=['dn_a%d' % j])
                if grp % 2:
                    S.op('act', lambda e, j=j, kn=kn, ncol=ncol: e.activation(out=wbb[j][:, 0:kn, 0:ncol], in_=wst[j][:, 0:kn, 0:ncol], func=AF.Copy), reads=['dn_ws%d' % j], writes=['dn_wb%d' % j])
                else:
                    S.op('dve', lambda e, j=j, kn=kn, ncol=ncol: e.tensor_copy(out=wbb[j][:, 0:kn, 0:ncol], in_=wst[j][:, 0:kn, 0:ncol]), reads=['dn_ws%d' % j], writes=['dn_wb%d' % j])
                for q in range(kn):
                    kt = k0 + q
                    for ch in range(nch):
                        for g in range(2):
                            b = ch * 2 + g
                            last = (q == kn - 1 and ch == nch - 1 and g == 1)
                            S.op('pe', lambda e, b=b, j=j, q=q, ch=ch, g=g, kt=kt: e.matmul(ps[:, b, :], lhsT=wbb[j][:, q, ch * 128:(ch + 1) * 128], rhs=abuf[j][:, q, g * 512:(g + 1) * 512], start=(kt == 0), stop=(kt == FFT - 1)),
                                 reads=['dn_wb%d' % j, 'dn_a%d' % j], writes=[('ps', b)], inc=(kt == FFT - 1 or last))
            for ch in range(nch):
                cidx = cg * 3 + ch
                j = ei % 2; ei += 1
                xk, ok, fk, otk = 'dn_x%d' % j, 'dn_o%d' % j, 'dn_f%d' % j, 'dn_ot%d' % j
                S.dma('pool', xl4[j][:], xlatT[cidx * 128:(cidx + 1) * 128, 0:1024], writes=[xk])
                for g in range(2):
                    b = ch * 2 + g
                    S.op('dve', lambda e, b=b, g=g, j=j, cidx=cidx: e.scalar_tensor_tensor(out=xo4[j][:, g * 512:(g + 1) * 512], in0=ps[:, b, :], scalar=G2[:, cidx:cidx + 1], in1=xl4[j][:, g * 512:(g + 1) * 512], op0=ALU.mult, op1=ALU.add),
                         reads=[('ps', b), 'G2', xk], writes=[ok])
                S.op('act', lambda e, j=j: e.activation(out=sq4[:], in_=xo4[j][:], func=AF.Square), reads=[ok], writes=['dn_sq'])
                tt(ssa[:], ssa[:], sq4[:], ALU.add, rd=['dn_ssa', 'dn_sq'], wr=['dn_ssa'])
                S.op('act', lambda e, j=j, cidx=cidx: e.activation(out=xf4[j][:], in_=xo4[j][:], func=AF.Identity, scale=fnT_[:, cidx:cidx + 1]), reads=[ok, 'fnT_'], writes=[fk])
                for tb in range(2):
                    bk = 6 + tb
                    for q in range(4):
                        ti_ = tb * 4 + q
                        S.op('pe', lambda e, bk=bk, q=q, ti_=ti_, j=j: e.transpose(ps[:, bk, q * 128:(q + 1) * 128], xf4[j][:, ti_ * 128:(ti_ + 1) * 128], ident[:]),
                             reads=[fk, 'ident'], writes=[('ps', bk)], inc=(q == 3))
                    P.evac(ot[j][:, tb * 4:(tb + 1) * 4, :], ps[:, bk, :].rearrange("p (q f) -> p q f", q=4), [('ps', bk)], [otk])
                S.dma('pool', y_v[:, :, cidx * 128:(cidx + 1) * 128], ot[j][:], reads=[otk], writes=['y_out'])
        for i in range(8):
            S.op('pe', lambda e, i=i: e.matmul(ps[:, 0, i:i + 1], lhsT=ssa[:, i * 128:(i + 1) * 128], rhs=ones[:, 0:1], start=True, stop=True),
                 reads=['dn_ssa', 'ones'], writes=[('ps', 0)])
        S.op('act', lambda e: e.activation(out=rs3[:], in_=ps[:, 0, 0:8], func=AF.Sqrt, scale=1.0 / D, bias=epsc[:, 0:1]), reads=[('ps', 0), 'epsc'], writes=['dn_rs3'])
        S.op('dve', lambda e: e.reciprocal(out=rs3[:], in_=rs3[:]), reads=['dn_rs3'], writes=['dn_rs3'])
        for i in range(8):
            j = i % 2
            bk_ = 'dn_big%d' % j
            S.dma('sp', big[j][:], y_out[i * 128:(i + 1) * 128, :], reads=['y_out'], writes=[bk_])
            S.op('act', lambda e, i=i, j=j: e.activation(out=big[j][:], in_=big[j][:], func=AF.Identity, scale=rs3[:, i:i + 1]), reads=[bk_, 'dn_rs3'], writes=[bk_])
            S.dma('sp', y_out[i * 128:(i + 1) * 128, :], big[j][:], reads=[bk_], writes=['y_out2'])
        S.barrier()
    S.finish()
    return nc


def _pT(v, nchunk):
    return np.ascontiguousarray(np.asarray(v, np.float32).reshape(nchunk, 128).T)


def prep_shared(inp):
    sh = {}
    sh["w_ada"] = np.ascontiguousarray(inp["w_ada"][0])
    sh["b_adaT"] = _pT(inp["b_ada"][0], 192)
    sh["norm1T"] = _pT(inp["norm1"][0], KT)
    sh["norm2T"] = _pT(inp["norm2"][0], KT)
    sh["fnormT"] = _pT(inp["final_norm"], KT)
    sh["w_in"] = np.ascontiguousarray(inp["w_in"][0])
    sh["convbT"] = _pT(inp["lru_conv_b"][0], KT)
    sh["w_ret_o"] = np.ascontiguousarray(inp["w_ret_o"][0])
    sh["w_lru_o"] = np.ascontiguousarray(inp["w_lru_o"][0])
    sh["w_out"] = np.ascontiguousarray(inp["w_out"][0])
    sh["w_up"] = np.ascontiguousarray(inp["w_up"][0])
    sh["fconvbT"] = _pT(inp["ffn_conv_b"][0], 2 * FFT)
    sh["w_down"] = np.ascontiguousarray(inp["w_down"][0])
    sh["ident"] = np.eye(128, dtype=np.float32)
    r = np.zeros((128, 128), np.float32)
    for d in range(64):
        r[d + 64, d] = -1.0
        r[d, d + 64] = 1.0
    sh["rmat"] = r
    return sh


def prep_core(inp, sh, b, half):
    flip = (half == 1)
    m = dict(sh)
    xb = inp["x"][b]
    cx = inp["ctx"][b]
    if flip:
        xb = xb[::-1]
        cx = cx[::-1]
    xm = np.zeros((1152, D), np.float32)
    xm[:NT] = xb[:NT]
    xp = np.zeros((NPFP, D), np.float32)
    xp[:256] = cx
    xp[256:NPF] = xb[NT:2048]
    m["x_main"] = xm
    m["x_pre"] = xp
    cT = np.stack([_pT(inp["c"][b], KT), _pT(inp["c_ctx"], KT)], axis=-1)
    m["cT"] = np.ascontiguousarray(cT)
    zs = [1, 0] if flip else [0, 1]
    m["retlog"] = np.ascontiguousarray(np.broadcast_to(inp["ret_decay_logit"][0][zs].reshape(1, 2 * H), (128, 2 * H)))
    cw = inp["lru_conv_w"][0]
    c5 = np.zeros((5, D), np.float32)
    if flip:
        c5[1:5] = cw[::-1]
    else:
        c5[0:4] = cw
    m["conv5T"] = np.ascontiguousarray(np.stack([_pT(c5[j], KT) for j in range(5)], axis=-1))
    m["lru_wa"] = np.ascontiguousarray(inp["lru_wa"][0][zs])
    m["lru_wx"] = np.ascontiguousarray(inp["lru_wx"][0][zs])
    for nm, key in (("lru_baT", "lru_ba"), ("lru_bxT", "lru_bx"), ("lru_lamT", "lru_lambda")):
        a = inp[key][0][zs]
        m[nm] = np.ascontiguousarray(np.stack([_pT(a[0], KT), _pT(a[1], KT)], axis=1))
    fw = inp["ffn_conv_w"][0]
    if flip:
        fw = fw[::-1]
    m["fconvT"] = np.ascontiguousarray(np.stack([_pT(fw[j], 2 * FFT) for j in range(3)], axis=-1))
    d = np.arange(128)
    rc = np.zeros((128, 5), np.float32)
    rc[:, 0] = -1.0 if flip else 1.0
    rc[:, 1] = 31.0 if flip else 0.0
    rc[:, 2] = 63.0 if flip else 0.0
    rc[:, 3] = d % 32
    rc[:, 4] = ((d % 64) < 32).astype(np.float32)
    m["ropec"] = rc
    return m


def kernel(**inputs):
    inp = {k: np.asarray(v) for k, v in inputs.items()}
    nc = build()
    sh = prep_shared(inp)
    maps = []
    for core in range(8):
        maps.append(prep_core(inp, sh, core // 2, core % 2))
    res = run_bass_kernel_spmd(nc, maps, core_ids=list(range(8)))
    out = np.empty((4, 2048, D), np.float32)
    for core in range(8):
        b, half = core // 2, core % 2
        y = np.asarray(res.results[core]["y_out"], np.float32)
        if half == 0:
            out[b, 0:1024] = y
        else:
            out[b, 1024:2048] = y[::-1]
    return out
```

```python
import contextlib
import numpy as np
import concourse.bass as bass
import concourse.mybir as mybir
from concourse.bass_utils import run_bass_kernel_spmd

F32 = mybir.dt.float32
BF16 = mybir.dt.bfloat16
I32 = mybir.dt.int32
AF = mybir.ActivationFunctionType
ALU = mybir.AluOpType
AX = mybir.AxisListType

D = 4096
KT = 32
NT = 1025
NTP = 1026
NPF = 1279
NPFP = 1280
H = 16
FF = 11008
FFT = 86
EPS = 1e-6
TWO_PI = 6.283185307179586


class Sched:
    ENG = ('pe', 'act', 'dve', 'pool', 'sp')

    def __init__(self, nc, stack, n_dma_sems=(('sp', 14), ('pool', 14))):
        self.nc = nc
        self.sem = {}
        for e in self.ENG:
            self.sem['c_' + e] = stack.enter_context(nc.semaphore('c_' + e))
        self.cnt = {e: 0 for e in self.ENG}
        self.pending = {e: False for e in self.ENG}
        self.seen = {e: {} for e in self.ENG}
        self.prog = {e: [] for e in self.ENG}
        self.last_write = {}
        self.readers = {}
        self.dma_sems = {}
        self.dma_uses = {}
        self.dma_rr = {}
        for q, n in n_dma_sems:
            names = []
            for i in range(n):
                nm = 'd_%s_%d' % (q, i)
                self.sem[nm] = stack.enter_context(nc.semaphore(nm))
                self.dma_uses[nm] = 0
                names.append(nm)
            self.dma_sems[q] = names
            self.dma_rr[q] = 0

    def _deps(self, eng, reads, writes):
        deps = {}

        def add(tok):
            if tok is None:
                return
            s, v = tok
            if deps.get(s, 0) < v:
                deps[s] = v
        for r in reads:
            add(self.last_write.get(r))
            if isinstance(r, tuple) and r[0] == 'ps':
                for s, v in self.readers.get(r, {}).items():
                    if s != 'c_' + eng:
                        add((s, v))
        for r in writes:
            add(self.last_write.get(r))
            for s, v in self.readers.get(r, {}).items():
                add((s, v))
        waits = []
        for s, v in deps.items():
            if eng == 'pe' and s == 'c_pe':
                continue
            if self.seen[eng].get(s, 0) < v:
                self.seen[eng][s] = v
                waits.append((s, v))
        return waits

    def _commit(self, tok, reads, writes):
        for r in writes:
            self.last_write[r] = tok
            self.readers[r] = {}
        for r in reads:
            d = self.readers.setdefault(r, {})
            if d.get(tok[0], 0) < tok[1]:
                d[tok[0]] = tok[1]

    def op(self, eng, emit, reads=(), writes=(), inc=True):
        waits = self._deps(eng, reads, writes)
        if inc:
            self.cnt[eng] += 1
            self.pending[eng] = False
            tok = ('c_' + eng, self.cnt[eng])
        else:
            self.pending[eng] = True
            tok = ('c_' + eng, self.cnt[eng] + 1)
        self.prog[eng].append((waits, emit, ('c_' + eng, 1) if inc else None))
        self._commit(tok, reads, writes)
        return tok

    def dma(self, q, out, in_, reads=(), writes=(), **kw):
        names = self.dma_sems[q]
        nm = names[self.dma_rr[q] % len(names)]
        self.dma_rr[q] += 1
        waits = self._deps(q, reads, writes)
        prev = self.dma_uses[nm] * 16
        if prev and self.seen[q].get(nm, 0) < prev:
            self.seen[q][nm] = prev
            waits.append((nm, prev))
        self.dma_uses[nm] += 1
        tok = (nm, self.dma_uses[nm] * 16)

        def emit(e, out=out, in_=in_, kw=kw):
            return e.dma_start(out=out, in_=in_, **kw)
        self.prog[q].append((waits, emit, (nm, 16)))
        self._commit(tok, reads, writes)
        return tok

    def barrier(self):
        toks = []
        for e in ('pe', 'act', 'dve', 'pool'):
            if self.pending[e]:
                raise RuntimeError('pending non-inc op on %s at barrier' % e)
            if self.cnt[e]:
                toks.append(('c_' + e, self.cnt[e]))
        for nm, u in self.dma_uses.items():
            if u:
                toks.append((nm, u * 16))
        for e in self.ENG:
            waits = []
            for s, v in toks:
                if self.seen[e].get(s, 0) < v:
                    self.seen[e][s] = v
                    waits.append((s, v))
            if waits:
                self.prog[e].append((waits, None, None))
        self.last_write = {}
        self.readers = {}

    def finish(self):
        nc = self.nc
        self.barrier()
        sem = self.sem
        prog = self.prog
        with nc.Block() as block:
            def run(e):
                def body(engobj):
                    for waits, emit, inc in prog[e]:
                        for s, v in waits:
                            engobj.wait_ge(sem[s], v)
                        if emit is None:
                            continue
                        ins = emit(engobj)
                        if inc is not None:
                            ins.then_inc(sem[inc[0]], inc[1])
                return body
            block.tensor(run('pe'))
            block.scalar(run('act'))
            block.vector(run('dve'))
            block.gpsimd(run('pool'))
            block.sync(run('sp'))


class Prog:
    def __init__(self, dbg=None, only_inputs=None):
        self.dbg = dbg or {}
        self.only_inputs = only_inputs
        self.nc = bass.Bass("TRN2", target_bir_lowering=False)
        self.outer = contextlib.ExitStack()
        self.S = Sched(self.nc, self.outer)
        self.cast_rr = 0
        self.evac_rr = 0

    def din(self, name, shape, dt=F32):
        if self.only_inputs is not None and name not in self.only_inputs:
            return None
        return self.nc.dram_tensor(name, list(shape), dt, kind="ExternalInput").ap()

    def dscr(self, name, shape, dt=F32):
        kind = "ExternalOutput" if name in self.dbg else "Internal"
        return self.nc.dram_tensor(name, list(shape), dt, kind=kind).ap()

    def sb(self, st, name, shape, dt=F32):
        self._uid = getattr(self, "_uid", 0) + 1
        return st.enter_context(self.nc.sbuf_tensor("s%d_%s" % (self._uid, name), list(shape), dt))

    def proj_fm(self, tag, w, K, col_segs, act, act_key, tok_groups, epilogue, banks, ps, nstg=3):
        S = self.S
        nc = self.nc
        kt_n = K // 128
        SW = sum(n for _, n in col_segs[0])
        KQ = 4
        with contextlib.ExitStack() as st:
            stg = [self.sb(st, '%s_stg%d' % (tag, i), [128, KQ, SW], F32) for i in range(nstg)]
            slab = [self.sb(st, '%s_slab%d' % (tag, i), [128, kt_n, SW], BF16) for i in range(2)]
            stg_i = 0
            pending_epi = None
            bank_rr = 0
            nb = len(banks)
            for si, segs in enumerate(col_segs):
                sl = slab[si % 2]
                slk = '%s_slab%d' % (tag, si % 2)
                kq0 = 0
                while kq0 < kt_n:
                    kq = min(KQ, kt_n - kq0)
                    sg = stg[stg_i % nstg]
                    sgk = '%s_stg%d' % (tag, stg_i % nstg)
                    stg_i += 1
                    off = 0
                    for (c0, ncol) in segs:
                        src = w[kq0 * 128:(kq0 + kq) * 128, c0:c0 + ncol].rearrange("(k p) c -> p k c", p=128)
                        S.dma('sp', sg[:, 0:kq, off:off + ncol], src, writes=[sgk])
                        off += ncol
                    wk = [(slk, kq0 + j) for j in range(kq)]
                    if self.cast_rr % 2 == 0:
                        S.op('act', lambda e, o=sl[:, kq0:kq0 + kq, :], i=sg[:, 0:kq, :]: e.activation(out=o, in_=i, func=AF.Copy),
                             reads=[sgk], writes=wk)
                    else:
                        S.op('dve', lambda e, o=sl[:, kq0:kq0 + kq, :], i=sg[:, 0:kq, :]: e.tensor_copy(out=o, in_=i),
                             reads=[sgk], writes=wk)
                    self.cast_rr += 1
                    kq0 += kq
                for ch in range(SW // 128):
                    pss = []
                    for (t0, n) in tok_groups:
                        b = banks[bank_rr % nb]
                        bank_rr += 1
                        pk = ('ps', b)
                        for kt in range(kt_n):
                            S.op('pe', lambda e, o=ps[:, b, 0:n], l=sl[:, kt, ch * 128:(ch + 1) * 128], r=act[:, kt, t0:t0 + n], a=(kt == 0), z=(kt == kt_n - 1):
                                 e.matmul(o, lhsT=l, rhs=r, start=a, stop=z),
                                 reads=[(slk, kt), act_key], writes=[pk], inc=(kt == kt_n - 1))
                        pss.append((ps[:, b, 0:n], pk))
                    if pending_epi is not None:
                        pending_epi()
                    pending_epi = (lambda si=si, ch=ch, pss=pss: epilogue(si, ch, pss))
            if pending_epi is not None:
                pending_epi()
            S.barrier()

    def evac(self, out, in_, reads, writes, func=None, **kw):
        S = self.S
        if func is None and not kw:
            self.evac_rr += 1
            if self.evac_rr % 2:
                return S.op('dve', lambda e: e.tensor_copy(out=out, in_=in_), reads=reads, writes=writes)
            func = AF.Copy
        return S.op('act', lambda e: e.activation(out=out, in_=in_, func=func, **kw), reads=reads, writes=writes)


def build(dbg=None, stop_after=99, only_inputs=None):
    P = Prog(dbg, only_inputs)
    nc, S = P.nc, P.S
    o = P.outer
    x_main = P.din("x_main", [1152, D])
    x_pre = P.din("x_pre", [NPFP, D])
    cT = P.din("cT", [128, KT, 2])
    w_ada = P.din("w_ada", [D, 6 * D])
    b_adaT = P.din("b_adaT", [128, 192])
    norm1T = P.din("norm1T", [128, KT])
    norm2T = P.din("norm2T", [128, KT])
    fnormT = P.din("fnormT", [128, KT])
    w_in = P.din("w_in", [D, 28672])
    retlog = P.din("retlog", [128, 2 * H])
    conv5T = P.din("conv5T", [128, KT, 5])
    convbT = P.din("convbT", [128, KT])
    lru_wa = P.din("lru_wa", [2, 16, 256, 256])
    lru_wx = P.din("lru_wx", [2, 16, 256, 256])
    lru_baT = P.din("lru_baT", [128, 2, KT])
    lru_bxT = P.din("lru_bxT", [128, 2, KT])
    lru_lamT = P.din("lru_lamT", [128, 2, KT])
    w_ret_o = P.din("w_ret_o", [D, D])
    w_lru_o = P.din("w_lru_o", [D, D])
    w_out = P.din("w_out", [D, D])
    w_up = P.din("w_up", [D, 2 * FF])
    fconvT = P.din("fconvT", [128, 2 * FFT, 3])
    fconvbT = P.din("fconvbT", [128, 2 * FFT])
    w_down = P.din("w_down", [FF, D])
    ident_in = P.din("ident", [128, 128])
    rmat_in = P.din("rmat", [128, 128])
    ropec = P.din("ropec", [128, 5])
    y_out = nc.dram_tensor("y_out", [1024, D], F32, kind="ExternalOutput").ap()

    xT_m = P.dscr("xT_m", [D, NTP])
    kT_m = P.dscr("kT_m", [2048, NTP], BF16)
    qT_m = P.dscr("qT_m", [2048, NTP], BF16)
    vT_m = P.dscr("vT_m", [D, NTP], BF16)
    xl_m = P.dscr("xl_m", [D, NTP])
    sg_m = P.dscr("sg_m", [D, NTP])
    gg_m = P.dscr("gg_m", [D, NTP])
    sr_m = P.dscr("sr_m", [D, NTP])
    sl_m = P.dscr("sl_m", [D, NTP])
    kT_p = P.dscr("kT_p", [2048, NPFP], BF16)
    vT_p = P.dscr("vT_p", [D, NPFP], BF16)
    xl_p = P.dscr("xl_p", [D, NPFP])
    yl_s = P.dscr("yl_s", [D, NTP])
    yr_s = P.dscr("yr_s", [D, NTP])
    xlatT = P.dscr("xlatT", [D, NTP])
    a_scr = P.dscr("a_scr", [FF, 1024], BF16)
    st_scr = P.dscr("st_scr", [2, H, 128, 256])
    hl_scr = P.dscr("hl_scr", [2, D])

    T = lambda name, shape, dt=F32: P.sb(o, name, shape, dt)
    ps = o.enter_context(nc.psum_tensor("ps", [128, 8, 512], F32))

    ident = T("ident", [128, 128])
    identb = T("identb", [128, 128], BF16)
    rmat = T("rmatb", [128, 128], BF16)
    rtmp = T("rtmp", [128, 128])
    ones = T("ones", [128, 128])
    modT = T("modT", [128, 192, 2])
    A1l = T("A1l", [128, KT]); B1l = T("B1l", [128, KT])
    A1c = T("A1c", [128, KT]); B1c = T("B1c", [128, KT])
    A2 = T("A2", [128, KT]); B2 = T("B2", [128, KT])
    G1 = T("G1", [128, KT]); G2 = T("G2", [128, KT])
    n1 = T("n1", [128, KT]); n2 = T("n2", [128, KT]); bada = T("bada", [128, 192])
    lg = T("lg", [128, 2 * H])
    rc = T("rc", [128, 5])
    epsc = T('epsc', [128, 1])
    rstd2 = T('rstd2', [128, NTP])
    S.op('dve', lambda e: e.memset(epsc[:], EPS), writes=['epsc'])
    S.dma('sp', ident[:], ident_in, writes=['ident'])
    S.dma('sp', rtmp[:], rmat_in, writes=['rtmp'])
    S.dma('sp', n1[:], norm1T, writes=['n1'])
    S.dma('sp', n2[:], norm2T, writes=['n2'])
    S.dma('sp', bada[:], b_adaT, writes=['bada'])
    S.dma('sp', lg[:], retlog, writes=['lg'])
    S.dma('sp', rc[:], ropec, writes=['rc'])
    S.op('dve', lambda e: e.tensor_copy(out=identb[:], in_=ident[:]), reads=['ident'], writes=['identb'])
    S.op('dve', lambda e: e.tensor_copy(out=rmat[:], in_=rtmp[:]), reads=['rtmp'], writes=['rmat'])
    S.op('dve', lambda e: e.memset(ones[:], 1.0), writes=['ones'])
    S.op('act', lambda e: e.activation(out=lg[:], in_=lg[:], func=AF.Exp, scale=-1.0), reads=['lg'], writes=['lg'])
    S.op('act', lambda e: e.activation(out=lg[:], in_=lg[:], func=AF.Ln, bias=1.0), reads=['lg'], writes=['lg'])
    S.op('dve', lambda e: e.tensor_scalar(out=lg[:], in0=lg[:], scalar1=-1.0, scalar2=None, op0=ALU.mult), reads=['lg'], writes=['lg'])

    if 'inj_mod' in P.dbg:
        S.dma('sp', modT[:], P.din('modT_in', [128, 192, 2]), writes=['modT'])
    with contextlib.ExitStack() as st:
      if 'inj_mod' not in P.dbg:
          cf = P.sb(st, "cf", [128, KT, 2], F32)
          cb = P.sb(st, "cb", [128, KT, 2], BF16)
          S.dma('sp', cf[:], cT, writes=['cf'])
          S.op('act', lambda e: e.activation(out=cb[:], in_=cf[:], func=AF.Silu), reads=['cf'], writes=['cb'])

          def epi_ada(si, ch, pss):
              j = si * 4 + ch
              (p0, pk), = pss
              S.op('dve', lambda e: e.tensor_scalar(out=modT[:, j, :], in0=p0, scalar1=bada[:, j:j + 1], scalar2=None, op0=ALU.add),
                   reads=[pk, 'bada'], writes=['modT'])
          P.proj_fm('ada', w_ada, D, [[(s * 512, 512)] for s in range(48)], cb, 'cb', [(0, 2)], epi_ada, [0, 1, 2, 3], ps, nstg=8)
    def ts(out, in0, s1, s2, op0, op1=None, rd=(), wr=()):
        if op1 is None:
            S.op('dve', lambda e: e.tensor_scalar(out=out, in0=in0, scalar1=s1, scalar2=None, op0=op0), reads=rd, writes=wr)
        else:
            S.op('dve', lambda e: e.tensor_scalar(out=out, in0=in0, scalar1=s1, scalar2=s2, op0=op0, op1=op1), reads=rd, writes=wr)

    def tt(out, a, b, op, rd=(), wr=()):
        S.op('dve', lambda e: e.tensor_tensor(out=out, in0=a, in1=b, op=op), reads=rd, writes=wr)
    ts(A1l[:], modT[:, 32:64, 0], 1.0, None, ALU.add, rd=['modT'], wr=['A1l'])
    tt(A1l[:], A1l[:], n1[:], ALU.mult, rd=['A1l', 'n1'], wr=['A1l'])
    ts(A1c[:], modT[:, 32:64, 1], 1.0, None, ALU.add, rd=['modT'], wr=['A1c'])
    tt(A1c[:], A1c[:], n1[:], ALU.mult, rd=['A1c', 'n1'], wr=['A1c'])
    ts(A2[:], modT[:, 128:160, 0], 1.0, None, ALU.add, rd=['modT'], wr=['A2'])
    tt(A2[:], A2[:], n2[:], ALU.mult, rd=['A2', 'n2'], wr=['A2'])
    for dst, lo, col, nm in ((B1l, 0, 0, 'B1l'), (B1c, 0, 1, 'B1c'), (B2, 96, 0, 'B2'), (G1, 64, 0, 'G1'), (G2, 160, 0, 'G2')):
        S.op('dve', lambda e, dst=dst, lo=lo, col=col: e.tensor_copy(out=dst[:], in_=modT[:, lo:lo + 32, col]), reads=['modT'], writes=[nm])
    S.barrier()
    if 'modT' in P.dbg:
        dbg_mod = nc.dram_tensor("dbg_modT", [128, 192, 2], F32, kind="ExternalOutput").ap()
        S.dma('sp', dbg_mod, modT[:], reads=['modT'])
    if stop_after <= 0:
        S.finish(); return nc

    ro = contextlib.ExitStack()
    CQ = P.sb(ro, "CQ", [128, NTP]); SQ = P.sb(ro, "SQ", [128, NTP])
    CKm = P.sb(ro, "CKm", [128, NTP]); SKm = P.sb(ro, "SKm", [128, NTP])
    CKp = P.sb(ro, "CKp", [128, NPFP]); SKp = P.sb(ro, "SKp", [128, NPFP])
    with contextlib.ExitStack() as st:
        NL = 2048
        ri = P.sb(st, "ri", [128, NL], I32); ci = P.sb(st, "ci", [128, NL], I32)
        rf = P.sb(st, "rf", [128, NL]); cfl = P.sb(st, "cfl", [128, NL])
        ang = P.sb(st, "ang", [128, NL]); u = P.sb(st, "u", [128, NL]); ui = P.sb(st, "ui", [128, NL], I32)
        uf = P.sb(st, "uf", [128, NL]); m1 = P.sb(st, "m1", [128, NL])
        cosf = P.sb(st, "cosf", [128, NL]); sinf = P.sb(st, "sinf", [128, NL])
        invf = P.sb(st, "invf", [128, 1])
        S.op('pool', lambda e: e.iota(ri[:], pattern=[[1, 32], [0, 64]], base=0, channel_multiplier=0), writes=['ri'])
        S.op('pool', lambda e: e.iota(ci[:], pattern=[[0, 32], [1, 64]], base=0, channel_multiplier=0), writes=['ci'])
        S.op('dve', lambda e: e.tensor_copy(out=rf[:], in_=ri[:]), reads=['ri'], writes=['rf'])
        S.op('dve', lambda e: e.tensor_copy(out=cfl[:], in_=ci[:]), reads=['ci'], writes=['cfl'])
        ts(rf[:], rf[:], rc[:, 0:1], rc[:, 1:2], ALU.mult, ALU.add, rd=['rf', 'rc'], wr=['rf'])
        ts(cfl[:], cfl[:], rc[:, 0:1], rc[:, 2:3], ALU.mult, ALU.add, rd=['cfl', 'rc'], wr=['cfl'])
        tt(rf[:], rf[:], cfl[:], ALU.subtract, rd=['rf', 'cfl'], wr=['rf'])
        S.op('dve', lambda e: e.scalar_tensor_tensor(out=ang[:], in0=rf[:], scalar=rc[:, 4:5], in1=cfl[:], op0=ALU.mult, op1=ALU.add),
             reads=['rf', 'cfl', 'rc'], writes=['ang'])
        S.op('act', lambda e: e.activation(out=invf[:], in_=rc[:, 3:4], func=AF.Exp, scale=-float(np.log(10000.0) / 32.0)), reads=['rc'], writes=['invf'])
        ts(ang[:], ang[:], invf[:, 0:1], 1.0 / TWO_PI, ALU.mult, ALU.mult, rd=['ang', 'invf'], wr=['ang'])
        for (dst, shift, nm) in ((sinf, 0.0, 'sinf'), (cosf, 0.25, 'cosf')):
            ts(u[:], ang[:], shift, None, ALU.add, rd=['ang'], wr=['u'])
            S.op('dve', lambda e: e.tensor_copy(out=ui[:], in_=u[:]), reads=['u'], writes=['ui'])
            S.op('dve', lambda e: e.tensor_copy(out=uf[:], in_=ui[:]), reads=['ui'], writes=['uf'])
            tt(u[:], u[:], uf[:], ALU.subtract, rd=['u', 'uf'], wr=['u'])
            ts(m1[:], u[:], 0.5, None, ALU.is_gt, rd=['u'], wr=['m1'])
            tt(u[:], u[:], m1[:], ALU.subtract, rd=['u', 'm1'], wr=['u'])
            ts(m1[:], u[:], -0.5, None, ALU.is_lt, rd=['u'], wr=['m1'])
            tt(u[:], u[:], m1[:], ALU.add, rd=['u', 'm1'], wr=['u'])
            S.op('act', lambda e, dst=dst: e.activation(out=dst[:], in_=u[:], func=AF.Sin, scale=TWO_PI), reads=['u'], writes=[nm])
        sc = float(128 ** -0.5)
        S.op('dve', lambda e: e.tensor_copy(out=CQ[:, 0:NT], in_=cosf[:, 0:NT]), reads=['cosf'], writes=['CQ'])
        S.op('dve', lambda e: e.tensor_copy(out=SQ[:, 0:NT], in_=sinf[:, 0:NT]), reads=['sinf'], writes=['SQ'])
        ts(CKm[:, 0:NT], cosf[:, 0:NT], sc, None, ALU.mult, rd=['cosf'], wr=['CKm'])
        ts(SKm[:, 0:NT], sinf[:, 0:NT], sc, None, ALU.mult, rd=['sinf'], wr=['SKm'])
        S.op('dve', lambda e: e.memset(CKp[:], sc), writes=['CKp'])
        S.op('dve', lambda e: e.memset(SKp[:], 0.0), writes=['SKp'])
        ts(CKp[:, 256:NPF], cosf[:, 1025:2048], sc, None, ALU.mult, rd=['cosf', 'CKp'], wr=['CKp'])
        ts(SKp[:, 256:NPF], sinf[:, 1025:2048], sc, None, ALU.mult, rd=['sinf', 'SKp'], wr=['SKp'])
        S.barrier()
        if 'rope' in P.dbg:
            d1 = nc.dram_tensor("dbg_cos", [128, NL], F32, kind="ExternalOutput").ap()
            d2 = nc.dram_tensor("dbg_sin", [128, NL], F32, kind="ExternalOutput").ap()
            S.dma('sp', d1, cosf[:], reads=['cosf']); S.dma('sp', d2, sinf[:], reads=['sinf'])
            S.barrier()
    if stop_after <= 1:
        S.finish(); return nc


    def norm_tm(st, x_dram, ntile, hT, hkey, AB_of_tile, xT_spill):
        xt = [P.sb(st, "xt%d" % i, [128, D], F32) for i in range(2)]
        xs = [P.sb(st, "xs%d" % i, [128, D], F32) for i in range(2)]
        ss = P.sb(st, "ss", [128, 4]); xtb = None
        if xT_spill is not None:
            xtb = [P.sb(st, "xtb%d" % i, [128, KT, 128], F32) for i in range(2)]
        bank = 0
        for i in range(ntile):
            x_, xs_ = xt[i % 2], xs[i % 2]
            xk, xsk = 'xt%d' % (i % 2), 'xs%d' % (i % 2)
            S.dma('sp', x_[:], x_dram[i * 128:(i + 1) * 128, :], writes=[xk])
            S.op('act', lambda e, x_=x_, xs_=xs_: e.activation(out=xs_[:], in_=x_[:], func=AF.Square, accum_out=ss[:, 0:1]), reads=[xk], writes=[xsk, 'ss'])
            ts(ss[:, 1:2], ss[:, 0:1], 1.0 / D, EPS, ALU.mult, ALU.add, rd=['ss'], wr=['ss'])
            S.op('act', lambda e: e.activation(out=ss[:, 2:3], in_=ss[:, 1:2], func=AF.Sqrt), reads=['ss'], writes=['ss'])
            S.op('dve', lambda e: e.reciprocal(out=ss[:, 3:4], in_=ss[:, 2:3]), reads=['ss'], writes=['ss'])
            S.op('act', lambda e, x_=x_, xs_=xs_: e.activation(out=xs_[:], in_=x_[:], func=AF.Identity, scale=ss[:, 3:4]), reads=[xk, 'ss'], writes=[xsk])
            A, B, abk = AB_of_tile(i)
            for k4 in range(KT // 4):
                b = bank % 8; bank += 1
                for q in range(4):
                    kt = k4 * 4 + q
                    S.op('pe', lambda e, o_=ps[:, b, q * 128:(q + 1) * 128], i_=xs_[:, kt * 128:(kt + 1) * 128]: e.transpose(o_, i_, ident[:]),
                         reads=[xsk, 'ident'], writes=[('ps', b)], inc=(q == 3))
                nc_ = min(128, hT.shape[2] - i * 128)
                for q in range(4):
                    kt = k4 * 4 + q
                    S.op('act', lambda e, o_=hT[:, kt, i * 128:i * 128 + nc_], i_=ps[:, b, q * 128:q * 128 + nc_], A=A, B=B, kt=kt:
                         e.activation(out=o_, in_=i_, func=AF.Identity, scale=A[:, kt:kt + 1], bias=B[:, kt:kt + 1]),
                         reads=[('ps', b)] + abk, writes=[hkey])
            if xT_spill is not None:
                xb_ = xtb[i % 2]; xbk = 'xtb%d' % (i % 2)
                for k4 in range(KT // 4):
                    b = bank % 8; bank += 1
                    for q in range(4):
                        kt = k4 * 4 + q
                        S.op('pe', lambda e, o_=ps[:, b, q * 128:(q + 1) * 128], i_=x_[:, kt * 128:(kt + 1) * 128]: e.transpose(o_, i_, ident[:]),
                             reads=[xk, 'ident'], writes=[('ps', b)], inc=(q == 3))
                    S.op('dve', lambda e, o_=xb_[:, k4 * 4:(k4 + 1) * 4, :], i_=ps[:, b, :].rearrange("p (k t) -> p k t", k=4): e.tensor_copy(out=o_, in_=i_),
                         reads=[('ps', b)], writes=[xbk])
                n = min(128, NTP - i * 128)
                S.dma('sp', xT_spill[:, i * 128:i * 128 + n].rearrange("(k p) t -> p k t", p=128), xb_[:, :, 0:n], reads=[xbk], writes=['xT_spill'])
        S.barrier()

    def spill_epi(dst, width, row0_of, func, out_dt, tok_groups, nbuf_tag):
        bufs = [P.sb(cur_st[0], "%s_o%d" % (nbuf_tag, i), [128, width], out_dt) for i in range(2)]
        state = {'i': 0}

        def epi(si, ch, pss, dst=dst, r0=None, func=func):
            i = state['i']; state['i'] += 1
            bt = bufs[i % 2]; bk = "%s_o%d" % (nbuf_tag, i % 2)
            for (t0, n), (pap, pk) in zip(tok_groups, pss):
                P.evac(bt[:, t0:t0 + n], pap, [pk], [bk], func=func)
            if r0 is None:
                r0 = row0_of(si, ch)
            ntok = tok_groups[-1][0] + tok_groups[-1][1]
            S.dma('pool', dst[r0:r0 + 128, 0:ntok], bt[:, 0:ntok], reads=[bk], writes=[('dram', id(dst))])
        return epi

    cur_st = [None]

    def rope_epi(dst, Ct, St, ckeys, width, tok_groups, tag):
        kb = [P.sb(cur_st[0], "%s_kb%d" % (tag, i), [128, width], BF16) for i in range(1)]
        t1 = [P.sb(cur_st[0], "%s_t1%d" % (tag, i), [128, width], F32) for i in range(1)]
        t2 = [P.sb(cur_st[0], "%s_t2%d" % (tag, i), [128, width], F32) for i in range(1)]
        ob = [P.sb(cur_st[0], "%s_ob%d" % (tag, i), [128, width], BF16) for i in range(1)]
        state = {'i': 0}

        def epi(si, ch, pss, head, dst=dst, Ct=Ct, St=St, ckeys=ckeys):
            i = state['i']; state['i'] += 1
            j = 0
            kbk, t1k, t2k, obk = "%s_kb%d" % (tag, j), "%s_t1%d" % (tag, j), "%s_t2%d" % (tag, j), "%s_ob%d" % (tag, j)
            for gi, ((t0, n), (pap, pk)) in enumerate(zip(tok_groups, pss)):
                S.op('act', lambda e, o_=kb[j][:, t0:t0 + n], i_=pap: e.activation(out=o_, in_=i_, func=AF.Copy), reads=[pk], writes=[kbk])
                tt(t1[j][:, t0:t0 + n], pap, Ct[:, t0:t0 + n], ALU.mult, rd=[pk] + ckeys, wr=[t1k])
                rb = 6 + (gi % 2)
                import os
                if os.environ.get('ROPEDBG') == 'noR':
                    tt(t2[j][:, t0:t0 + n], pap, St[:, t0:t0 + n], ALU.mult, rd=[pk] + ckeys, wr=[t2k])
                else:
                    S.op('pe', lambda e, o_=ps[:, rb, 0:n], r_=kb[j][:, t0:t0 + n]: e.matmul(o_, lhsT=rmat[:], rhs=r_, start=True, stop=True),
                         reads=[kbk, 'rmat'], writes=[('ps', rb)])
                    tt(t2[j][:, t0:t0 + n], ps[:, rb, 0:n], St[:, t0:t0 + n], ALU.mult, rd=[('ps', rb)] + ckeys, wr=[t2k])
                tt(ob[j][:, t0:t0 + n], t1[j][:, t0:t0 + n], t2[j][:, t0:t0 + n], ALU.add, rd=[t1k, t2k], wr=[obk])
            ntok = tok_groups[-1][0] + tok_groups[-1][1]
            S.dma('pool', dst[head * 128:(head + 1) * 128, 0:ntok], ob[j][:, 0:ntok], reads=[obk], writes=[('dram', id(dst))])
        return epi

    PB = [0, 1, 2, 3, 4, 5]
    with contextlib.ExitStack() as st:
        cur_st[0] = st
        hTp = P.sb(st, "hTp", [128, KT, NPFP], BF16)
        with contextlib.ExitStack() as st2:
            norm_tm(st2, x_pre, 10, hTp, 'hTp', lambda i: (A1c, B1c, ['A1c', 'B1c']) if i < 2 else (A1l, B1l, ['A1l', 'B1l']), None)
        if 'hTp' in P.dbg:
            dz = nc.dram_tensor('dbg_hTp', [128, KT, NPFP], BF16, kind='ExternalOutput').ap()
            S.dma('sp', dz, hTp[:], reads=['hTp']); S.barrier()
            if stop_after <= 1.5:
                S.finish(); return nc
        tgp = [(0, 428), (428, 426), (854, 425)]
        pin_slabs = [[(s_ * 256, 256)] for s_ in range(40)]
        if 'few' in P.dbg:
            import os; pin_slabs = [pin_slabs[int(i)] for i in os.environ.get('FEW', '0,1,2,8,9,24,25').split(',')]
        e_k = rope_epi(kT_p, CKp, SKp, ['CKp', 'SKp'], NPFP, tgp, 'pk')
        e_v = spill_epi(vT_p, NPFP, lambda si, ch: pin_slabs[si][0][0] + ch * 128 - 2048, None, BF16, tgp, 'pv')
        e_x = spill_epi(xl_p, NPFP, lambda si, ch: pin_slabs[si][0][0] + ch * 128 - 6144, None, F32, tgp, 'px')

        def epi_p(si, ch, pss):
            c0 = pin_slabs[si][0][0] + ch * 128
            if c0 < 2048:
                e_k(si, ch, pss, c0 // 128)
            elif c0 < 6144:
                e_v(si, ch, pss)
            else:
                e_x(si, ch, pss)
        P.proj_fm('pin', w_in, D, pin_slabs, hTp, 'hTp', tgp, epi_p, PB, ps)
    if stop_after <= 2:
        S.finish(); return nc

    tgm = [(0, 342), (342, 342), (684, 341)]
    with contextlib.ExitStack() as st:
        cur_st[0] = st
        hTm = P.sb(st, "hTm", [128, KT, NTP], BF16)
        with contextlib.ExitStack() as st2:
            norm_tm(st2, x_main, 9, hTm, 'hTm', lambda i: (A1l, B1l, ['A1l', 'B1l']), xT_m)
        r_e = rope_epi(None, None, None, None, NTP, tgm, 'mk')
        e_b = spill_epi(None, NTP, None, None, BF16, tgm, 'mvb')
        e_f = spill_epi(None, NTP, None, None, F32, tgm, 'mvf')

        def epi_m(si, ch, pss):
            c0 = si * 256 + ch * 128
            if c0 < 2048:
                r_e(si, ch, pss, c0 // 128, dst=kT_m, Ct=CKm, St=SKm, ckeys=['CKm', 'SKm'])
            elif c0 < 6144:
                e_b(si, ch, pss, dst=vT_m, r0=c0 - 2048, func=None)
            elif c0 < 10240:
                e_f(si, ch, pss, dst=xl_m, r0=c0 - 6144, func=None)
            elif c0 < 12288:
                r_e(si, ch, pss, (c0 - 10240) // 128, dst=qT_m, Ct=CQ, St=SQ, ckeys=['CQ', 'SQ'])
            elif c0 < 16384:
                e_f(si, ch, pss, dst=sg_m, r0=c0 - 12288, func=AF.Silu)
            elif c0 < 20480:
                e_f(si, ch, pss, dst=gg_m, r0=c0 - 16384, func=AF.Gelu)
            elif c0 < 24576:
                e_f(si, ch, pss, dst=sr_m, r0=c0 - 20480, func=AF.Sigmoid)
            else:
                e_f(si, ch, pss, dst=sl_m, r0=c0 - 24576, func=AF.Sigmoid)
        P.proj_fm('min', w_in, D, [[(s * 256, 256)] for s in range(112)], hTm, 'hTm', tgm, epi_m, PB, ps)
    ro.close()
    S.barrier()
    if stop_after <= 3:
        S.finish(); return nc

    with contextlib.ExitStack() as st:
        cz = P.sb(st, "cz", [128, 2, KT]); bat = P.sb(st, "bat", [128, 2, KT]); bxt = P.sb(st, "bxt", [128, 2, KT])
        c5 = P.sb(st, "c5", [128, KT, 5]); cb5 = P.sb(st, "cb5", [128, KT])
        S.dma('sp', cz[:], lru_lamT, writes=['cz'])
        S.dma('sp', bat[:], lru_baT, writes=['bat'])
        S.dma('sp', bxt[:], lru_bxT, writes=['bxt'])
        S.dma('sp', c5[:], conv5T, writes=['c5'])
        S.dma('sp', cb5[:], convbT, writes=['cb5'])
        S.op('act', lambda e: e.activation(out=cz[:], in_=cz[:], func=AF.Exp, scale=-1.0), reads=['cz'], writes=['cz'])
        S.op('act', lambda e: e.activation(out=cz[:], in_=cz[:], func=AF.Ln, bias=1.0), reads=['cz'], writes=['cz'])
        ts(cz[:], cz[:], -8.0, None, ALU.mult, rd=['cz'], wr=['cz'])
        xlat_ = P.sb(st, "xlat_", [128, 2, 2052]); xctx_ = P.sb(st, "xctx_", [128, 2, 260])
        xcl = P.sb(st, "xcl", [128, 2, 2048]); xcc = P.sb(st, "xcc", [128, 2, 256])
        xbl = P.sb(st, "xbl", [128, 2, 2048], BF16); xbc = P.sb(st, "xbc", [128, 2, 256], BF16)
        wf = P.sb(st, "wf", [128, 4, 2, 256]); wbf = P.sb(st, "wbf", [128, 4, 2, 256], BF16)
        aA = P.sb(st, "aA", [128, 256 + NTP]); bA = P.sb(st, "bA", [128, 256 + NTP]); hA = P.sb(st, "hA", [128, 256 + NTP])
        aB = P.sb(st, "aB", [128, 2304]); bB = P.sb(st, "bB", [128, 2304]); hB = P.sb(st, "hB", [128, 2304])
        tqA = P.sb(st, "tqA", [128, 256 + NTP]); tqB = P.sb(st, "tqB", [128, 2304])
        ylo = [P.sb(st, "ylo%d" % i, [128, NTP]) for i in range(2)]
        S.op('dve', lambda e: e.memset(xlat_[:], 0.0), writes=['xlat_'])
        S.op('dve', lambda e: e.memset(xctx_[:], 0.0), writes=['xctx_'])
        pb = 0
        for n in range(16):
            for i2 in range(2):
                r0 = (2 * n + i2) * 128
                S.dma('pool', xlat_[:, i2, 2:2 + NT], xl_m[r0:r0 + 128, 0:NT], writes=['xlat_'])
                S.dma('pool', xlat_[:, i2, 2 + NT:2 + 2048], xl_p[r0:r0 + 128, 256:NPF], writes=['xlat_'])
                S.dma('pool', xctx_[:, i2, 2:258], xl_p[r0:r0 + 128, 0:256], writes=['xctx_'])
            for z in range(2):
                for gi, wsrc in enumerate((lru_wa, lru_wx)):
                    S.dma('sp', wf[:, z * 2 + gi, :, :], wsrc[z, n].rearrange("(k p) f -> p k f", p=128), writes=['wf'])
            S.op('pool', lambda e: e.tensor_copy(out=wbf[:], in_=wf[:]), reads=['wf'], writes=['wbf'])
            for i2 in range(2):
                kt = 2 * n + i2
                for (src, dst, dstb, L, sk, dk, dbk) in ((xlat_, xcl, xbl, 2048, 'xlat_', 'xcl', 'xbl'), (xctx_, xcc, xbc, 256, 'xctx_', 'xcc', 'xbc')):
                    S.op('dve', lambda e, src=src, dst=dst, L=L, i2=i2, kt=kt: e.tensor_scalar(out=dst[:, i2, :], in0=src[:, i2, 0:L], scalar1=c5[:, kt, 0:1], scalar2=cb5[:, kt:kt + 1], op0=ALU.mult, op1=ALU.add),
                         reads=[sk, 'c5', 'cb5'], writes=[dk])
                    for j in range(1, 5):
                        S.op('dve', lambda e, src=src, dst=dst, L=L, i2=i2, kt=kt, j=j: e.scalar_tensor_tensor(out=dst[:, i2, :], in0=src[:, i2, j:j + L], scalar=c5[:, kt, j:j + 1], in1=dst[:, i2, :], op0=ALU.mult, op1=ALU.add),
                             reads=[sk, 'c5', dk], writes=[dk])
                    S.op('pool', lambda e, dst=dst, dstb=dstb, i2=i2: e.tensor_copy(out=dstb[:, i2, :], in_=dst[:, i2, :]), reads=[dk], writes=[dbk])
            for j2 in range(2):
                kt = 2 * n + j2
                zinfo = []
                for z in range(2):
                    if z == 0:
                        segs = [(xbc, 'xbc', 0, 256, 0), (xbl, 'xbl', 0, 512, 256), (xbl, 'xbl', 512, 512, 768), (xbl, 'xbl', 1024, 2, 1280)]
                        ab, bb_, tqz, abk, bbk, tqk, L = aA, bA, tqA, 'aA', 'bA', 'tqA', 1282
                    else:
                        segs = [(xbl, 'xbl', q * 512, 512, q * 512) for q in range(4)] + [(xbc, 'xbc', 0, 256, 2048)]
                        ab, bb_, tqz, abk, bbk, tqk, L = aB, bB, tqB, 'aB', 'bB', 'tqB', 2304
                    zinfo.append((ab, bb_, tqz, abk, bbk, tqk, L))
                    for (xb_, xbk, c0, nn, o0) in segs:
                        b_r = pb % 8; b_i = (pb + 1) % 8; pb += 2
                        for gi, bnk in ((0, b_r), (1, b_i)):
                            for i2 in range(2):
                                S.op('pe', lambda e, o_=ps[:, bnk, 0:nn], l=wbf[:, z * 2 + gi, i2, j2 * 128:(j2 + 1) * 128], r=xb_[:, i2, c0:c0 + nn], i2=i2:
                                     e.matmul(o_, lhsT=l, rhs=r, start=(i2 == 0), stop=(i2 == 1)),
                                     reads=['wbf', xbk], writes=[('ps', bnk)], inc=(i2 == 1))
                        S.op('act', lambda e, nn=nn, b_r=b_r, z=z, kt=kt, ab=ab, o0=o0: e.activation(out=ab[:, o0:o0 + nn], in_=ps[:, b_r, 0:nn], func=AF.Sigmoid, bias=bat[:, z, kt:kt + 1]),
                             reads=[('ps', b_r), 'bat'], writes=[abk])
                        S.op('act', lambda e, nn=nn, b_i=b_i, z=z, kt=kt, bb_=bb_, o0=o0: e.activation(out=bb_[:, o0:o0 + nn], in_=ps[:, b_i, 0:nn], func=AF.Sigmoid, bias=bxt[:, z, kt:kt + 1]),
                             reads=[('ps', b_i), 'bxt'], writes=[bbk])
                for z, (ab, bb_, tqz, abk, bbk, tqk, L) in enumerate(zinfo):
                    S.op('act', lambda e, ab=ab, L=L, z=z, kt=kt: e.activation(out=ab[:, 0:L], in_=ab[:, 0:L], func=AF.Exp, scale=cz[:, z, kt:kt + 1]),
                         reads=[abk, 'cz'], writes=[abk])
                    tt(tqz[:, 0:L], ab[:, 0:L], ab[:, 0:L], ALU.mult, rd=[abk], wr=[tqk])
                for z, (ab, bb_, tqz, abk, bbk, tqk, L) in enumerate(zinfo):
                    S.op('act', lambda e, tqz=tqz, L=L: e.activation(out=tqz[:, 0:L], in_=tqz[:, 0:L], func=AF.Sqrt, scale=-1.0, bias=1.0), reads=[tqk], writes=[tqk])
                    tt(bb_[:, 0:L], bb_[:, 0:L], tqz[:, 0:L], ALU.mult, rd=[bbk, tqk], wr=[bbk])
                tt(bA[:, 0:256], bA[:, 0:256], xcc[:, j2, :], ALU.mult, rd=['bA', 'xcc'], wr=['bA'])
                tt(bA[:, 256:1282], bA[:, 256:1282], xcl[:, j2, 0:1026], ALU.mult, rd=['bA', 'xcl'], wr=['bA'])
                tt(bB[:, 0:2048], bB[:, 0:2048], xcl[:, j2, :], ALU.mult, rd=['bB', 'xcl'], wr=['bB'])
                tt(bB[:, 2048:2304], bB[:, 2048:2304], xcc[:, j2, :], ALU.mult, rd=['bB', 'xcc'], wr=['bB'])
                LA = 256 + NT
                S.op('dve', lambda e: e.tensor_tensor_scan(out=hA[:, 0:LA], data0=aA[:, 0:LA], data1=bA[:, 0:LA], initial=0.0, op0=ALU.mult, op1=ALU.add),
                     reads=['aA', 'bA'], writes=['hA'])
                S.op('dve', lambda e: e.tensor_tensor_scan(out=hB[:, ::-1], data0=aB[:, ::-1], data1=bB[:, ::-1], initial=0.0, op0=ALU.mult, op1=ALU.add),
                     reads=['aB', 'bB'], writes=['hB'])
                yo = ylo[kt % 2]; yok = 'ylo%d' % (kt % 2)
                tt(yo[:, 0:NT], hA[:, 256:256 + NT], hB[:, 0:NT], ALU.add, rd=['hA', 'hB'], wr=[yok])
                S.dma('pool', yl_s[kt * 128:(kt + 1) * 128, 0:NT], yo[:, 0:NT], reads=[yok], writes=['yl_s'])
        S.barrier()
    if stop_after <= 4:
        S.finish(); return nc

    zst = contextlib.ExitStack()
    zT = P.sb(zst, "zT", [128, KT, NTP], BF16)
    with contextlib.ExitStack() as st:
        jpi = P.sb(st, "jpi", [128, 1], I32); jp = P.sb(st, "jp", [128, 1])
        reli = P.sb(st, "reli", [128, 128], I32); rel = P.sb(st, "rel", [128, 128])
        relp = P.sb(st, "relp", [128, 128]); reln = P.sb(st, "reln", [128, 128])
        indA = P.sb(st, "indA", [128, 128]); indB = P.sb(st, "indB", [128, 128]); mtmp = P.sb(st, "mtmp", [128, 128])
        Mall = P.sb(st, "Mall", [128, H, 128])
        ev = P.sb(st, "ev", [128, 8])
        kdA = P.sb(st, "kdA", [128, H]); kdB = P.sb(st, "kdB", [128, H]); qdA = P.sb(st, "qdA", [128, H]); qdB = P.sb(st, "qdB", [128, H])
        cdA = P.sb(st, "cdA", [128, H]); cdB = P.sb(st, "cdB", [128, H]); g1A = P.sb(st, "g1A", [128, H]); g1B = P.sb(st, "g1B", [128, H])
        eA = P.sb(st, "eA", [128, 2]); eB = P.sb(st, "eB", [128, 10])
        wA = P.sb(st, "wA", [128, H, 2]); wB = P.sb(st, "wB", [128, H, 10])
        onesc = P.sb(st, "onesc", [128, 1])
        S.op('pool', lambda e: e.iota(jpi[:], pattern=[[0, 1]], base=0, channel_multiplier=1), writes=['jpi'])
        S.op('pool', lambda e: e.iota(reli[:], pattern=[[1, 128]], base=0, channel_multiplier=-1), writes=['reli'])
        S.op('dve', lambda e: e.tensor_copy(out=jp[:], in_=jpi[:]), reads=['jpi'], writes=['jp'])
        S.op('dve', lambda e: e.tensor_copy(out=rel[:], in_=reli[:]), reads=['reli'], writes=['rel'])
        S.op('dve', lambda e: e.memset(onesc[:], 1.0), writes=['onesc'])
        ts(relp[:], rel[:], 0.0, None, ALU.max, rd=['rel'], wr=['relp'])
        ts(reln[:], rel[:], -1.0, 0.0, ALU.mult, ALU.max, rd=['rel'], wr=['reln'])
        ts(indA[:], rel[:], 0.0, None, ALU.is_ge, rd=['rel'], wr=['indA'])
        ts(indB[:], rel[:], 0.0, None, ALU.is_le, rd=['rel'], wr=['indB'])
        ts(ev[:, 0:1], jp[:], -1.0, 127.0, ALU.mult, ALU.add, rd=['jp'], wr=['ev'])
        S.op('dve', lambda e: e.tensor_copy(out=ev[:, 1:2], in_=jp[:]), reads=['jp'], writes=['ev'])
        ts(ev[:, 2:3], jp[:], 1.0, None, ALU.add, rd=['jp'], wr=['ev'])
        ts(ev[:, 3:4], jp[:], -1.0, 128.0, ALU.mult, ALU.add, rd=['jp'], wr=['ev'])
        S.op('dve', lambda e: e.memset(ev[:, 4:5], 128.0), writes=['ev'])
        S.op('dve', lambda e: e.memset(ev[:, 5:6], 1.0), writes=['ev'])
        for (dst, z, col, nm) in ((kdA, 0, 0, 'kdA'), (kdB, 1, 1, 'kdB'), (qdA, 0, 2, 'qdA'), (qdB, 1, 3, 'qdB'), (cdA, 0, 4, 'cdA'), (cdB, 1, 4, 'cdB'), (g1A, 0, 5, 'g1A'), (g1B, 1, 5, 'g1B')):
            S.op('act', lambda e, dst=dst, z=z, col=col: e.activation(out=dst[:], in_=lg[:, z * H:(z + 1) * H], func=AF.Exp, scale=ev[:, col:col + 1]),
                 reads=['lg', 'ev'], writes=[nm])
        for c in range(10):
            if c < 2:
                ts(eA[:, c:c + 1], jp[:], -1.0, float(255 - c * 128), ALU.mult, ALU.add, rd=['jp'], wr=['eA'])
                ts(eB[:, c:c + 1], jp[:], float(c * 128 + 1023), None, ALU.add, rd=['jp'], wr=['eB'])
            else:
                ts(eB[:, c:c + 1], jp[:], float(c * 128 - 256), None, ALU.add, rd=['jp'], wr=['eB'])
        for h in range(H):
            S.op('act', lambda e, h=h: e.activation(out=wA[:, h, :], in_=eA[:], func=AF.Exp, scale=lg[:, h:h + 1]), reads=['eA', 'lg'], writes=['wA'])
            S.op('act', lambda e, h=h: e.activation(out=wB[:, h, :], in_=eB[:], func=AF.Exp, scale=lg[:, H + h:H + h + 1]), reads=['eB', 'lg'], writes=['wB'])
            S.op('act', lambda e, h=h: e.activation(out=Mall[:, h, :], in_=relp[:], func=AF.Exp, scale=lg[:, h:h + 1]), reads=['relp', 'lg'], writes=['Mall'])
            tt(Mall[:, h, :], Mall[:, h, :], indA[:], ALU.mult, rd=['Mall', 'indA'], wr=['Mall'])
            S.op('act', lambda e, h=h: e.activation(out=mtmp[:], in_=reln[:], func=AF.Exp, scale=lg[:, H + h:H + h + 1]), reads=['reln', 'lg'], writes=['mtmp'])
            tt(mtmp[:], mtmp[:], indB[:], ALU.mult, rd=['mtmp', 'indB'], wr=['mtmp'])
            tt(Mall[:, h, :], Mall[:, h, :], mtmp[:], ALU.add, rd=['Mall', 'mtmp'], wr=['Mall'])

        kTp = P.sb(st, "kTp", [128, NPFP], BF16); vTp = P.sb(st, "vTp", [128, 2, NPFP], BF16)
        kTm = P.sb(st, "kTm", [128, NTP], BF16); qTm = P.sb(st, "qTm", [128, NTP], BF16); vTm = P.sb(st, "vTm", [128, 2, NTP], BF16)
        sgh = P.sb(st, "sgh", [128, 2, NTP])
        kAc = P.sb(st, "kAc", [128, 128], BF16); kBc = P.sb(st, "kBc", [128, 128], BF16); vc = P.sb(st, "vc", [128, 256], BF16)
        kBall = P.sb(st, "kBall", [128, 9, 128], BF16); vtok = P.sb(st, "vtok", [128, 9, 256], BF16)
        SA = P.sb(st, "SA", [128, 256]); SB = P.sb(st, "SB", [128, 256]); SB16 = P.sb(st, "SB16", [128, 256], BF16)
        SAall = P.sb(st, "SAall", [128, 9, 256], BF16)
        PT = P.sb(st, "PT", [128, 128], BF16); osb = P.sb(st, "osb", [128, 256]); on = P.sb(st, "on", [128, 256])
        bst = P.sb(st, "bst", [128, 6]); mv = P.sb(st, "mv", [128, 4])
        psb = ps[:, :, :].bitcast(BF16)

        def tr_bf(bank, off, in_ap, n, rd):
            S.op('pe', lambda e: e.transpose(psb[0:n, bank, off:off + 128], in_ap, identb[:]), reads=rd + ['identb'], writes=[('ps', bank)])
            return psb[0:n, bank, off:off + 128]

        for h in range(H):
            S.dma('pool', kTp[:], kT_p[h * 128:(h + 1) * 128, :], writes=['kTp'])
            S.dma('pool', vTp[:], vT_p[h * 256:(h + 1) * 256, :].rearrange("(e p) t -> p e t", p=128), writes=['vTp'])
            S.dma('pool', kTm[:], kT_m[h * 128:(h + 1) * 128, :], writes=['kTm'])
            S.dma('pool', qTm[:], qT_m[h * 128:(h + 1) * 128, :], writes=['qTm'])
            S.dma('pool', vTm[:], vT_m[h * 256:(h + 1) * 256, :].rearrange("(e p) t -> p e t", p=128), writes=['vTm'])
            S.dma('pool', sgh[:], sg_m[h * 256:(h + 1) * 256, :].rearrange("(e p) t -> p e t", p=128), writes=['sgh'])
            for c in range(10):
                n = min(128, NPF - c * 128)
                pk_ = tr_bf(0, 0, kTp[:, c * 128:c * 128 + n], n, ['kTp'])
                S.op('act', lambda e, n=n, c=c, h=h, pk_=pk_: e.activation(out=kBc[0:n, :], in_=pk_, func=AF.Copy, scale=wB[0:n, h, c:c + 1]), reads=[('ps', 0), 'wB'], writes=['kBc'])
                if c < 2:
                    S.op('dve', lambda e, n=n, c=c, h=h, pk_=pk_: e.tensor_scalar(out=kAc[0:n, :], in0=pk_, scalar1=wA[0:n, h, c:c + 1], scalar2=None, op0=ALU.mult), reads=[('ps', 0), 'wA'], writes=['kAc'])
                for e2 in range(2):
                    tr_bf(1, e2 * 128, vTp[:, e2, c * 128:c * 128 + n], n, ['vTp'])
                S.op('dve', lambda e, n=n: e.tensor_copy(out=vc[0:n, :], in_=psb[0:n, 1, 0:256]), reads=[('ps', 1)], writes=['vc'])
                S.op('pe', lambda e, n=n, c=c: e.matmul(ps[:, 7, 0:256], lhsT=kBc[0:n, :], rhs=vc[0:n, :], start=(c == 0), stop=(c == 9)),
                     reads=['kBc', 'vc'], writes=[('ps', 7)])
                if c < 2:
                    S.op('pe', lambda e, n=n, c=c: e.matmul(ps[:, 6, 0:256], lhsT=kAc[0:n, :], rhs=vc[0:n, :], start=(c == 0), stop=(c == 1)),
                         reads=['kAc', 'vc'], writes=[('ps', 6)])
            S.op('act', lambda e: e.activation(out=SA[:], in_=ps[:, 6, 0:256], func=AF.Copy), reads=[('ps', 6)], writes=['SA'])
            S.op('dve', lambda e: e.tensor_copy(out=SB[:], in_=ps[:, 7, 0:256]), reads=[('ps', 7)], writes=['SB'])
            S.op('act', lambda e: e.activation(out=SB16[:], in_=SB[:], func=AF.Copy), reads=['SB'], writes=['SB16'])
            for c in range(9):
                n = 128 if c < 8 else 1
                pk_ = tr_bf(0, 0, kTm[:, c * 128:c * 128 + n], n, ['kTm'])
                sA = kdA[0:n, h:h + 1] if c < 8 else onesc[0:n, 0:1]
                S.op('act', lambda e, n=n, pk_=pk_, sA=sA: e.activation(out=kAc[0:n, :], in_=pk_, func=AF.Copy, scale=sA), reads=[('ps', 0), 'kdA', 'onesc'], writes=['kAc'])
                S.op('dve', lambda e, n=n, c=c, h=h, pk_=pk_: e.tensor_scalar(out=kBall[0:n, c, :], in0=pk_, scalar1=kdB[0:n, h:h + 1], scalar2=None, op0=ALU.mult), reads=[('ps', 0), 'kdB'], writes=['kBall'])
                for e2 in range(2):
                    tr_bf(1, e2 * 128, vTm[:, e2, c * 128:c * 128 + n], n, ['vTm'])
                S.op('dve', lambda e, n=n, c=c: e.tensor_copy(out=vtok[0:n, c, :], in_=psb[0:n, 1, 0:256]), reads=[('ps', 1)], writes=['vtok'])
                S.op('act', lambda e, c=c: e.activation(out=SAall[:, c, :], in_=SA[:], func=AF.Copy), reads=['SA'], writes=['SAall'])
                if c < 8:
                    S.op('pe', lambda e, n=n, c=c: e.matmul(ps[:, 2, 0:256], lhsT=kAc[0:n, :], rhs=vtok[0:n, c, :], start=True, stop=True),
                         reads=['kAc', 'vtok'], writes=[('ps', 2)])
                    S.op('dve', lambda e, h=h: e.scalar_tensor_tensor(out=SA[:], in0=SA[:], scalar=cdA[:, h:h + 1], in1=ps[:, 2, 0:256], op0=ALU.mult, op1=ALU.add),
                         reads=['SA', 'cdA', ('ps', 2)], writes=['SA'])
            for c in range(8, -1, -1):
                n = 128 if c < 8 else 1
                t0 = c * 128
                S.op('pe', lambda e, n=n, t0=t0: e.matmul(ps[0:n, 3, 0:n], lhsT=kTm[:, t0:t0 + n], rhs=qTm[:, t0:t0 + n], start=True, stop=True),
                     reads=['kTm', 'qTm'], writes=[('ps', 3)])
                tt(PT[0:n, 0:n], ps[0:n, 3, 0:n], Mall[0:n, h, 0:n], ALU.mult, rd=[('ps', 3), 'Mall'], wr=['PT'])
                S.op('pe', lambda e, n=n, c=c: e.matmul(ps[0:n, 4, 0:256], lhsT=PT[0:n, 0:n], rhs=vtok[0:n, c, :], start=True, stop=True),
                     reads=['PT', 'vtok'], writes=[('ps', 4)])
                S.op('pe', lambda e, n=n, c=c, t0=t0: e.matmul(ps[0:n, 5, 0:256], lhsT=qTm[:, t0:t0 + n], rhs=SAall[:, c, :], start=True, stop=True),
                     reads=['qTm', 'SAall'], writes=[('ps', 5)])
                S.op('pe', lambda e, n=n, t0=t0: e.matmul(ps[0:n, 5, 256:512], lhsT=qTm[:, t0:t0 + n], rhs=SB16[:], start=True, stop=True),
                     reads=['qTm', 'SB16'], writes=[('ps', 5)])
                S.op('act', lambda e, n=n: e.activation(out=osb[0:n, :], in_=ps[0:n, 4, 0:256], func=AF.Copy), reads=[('ps', 4)], writes=['osb'])
                sqB = qdB[0:n, h:h + 1] if c < 8 else g1B[0:n, h:h + 1]
                S.op('dve', lambda e, n=n, h=h: e.scalar_tensor_tensor(out=osb[0:n, :], in0=ps[0:n, 5, 0:256], scalar=qdA[0:n, h:h + 1], in1=osb[0:n, :], op0=ALU.mult, op1=ALU.add),
                     reads=[('ps', 5), 'qdA', 'osb'], writes=['osb'])
                S.op('dve', lambda e, n=n, sqB=sqB: e.scalar_tensor_tensor(out=osb[0:n, :], in0=ps[0:n, 5, 256:512], scalar=sqB, in1=osb[0:n, :], op0=ALU.mult, op1=ALU.add),
                     reads=[('ps', 5), 'qdB', 'g1B', 'osb'], writes=['osb'])
                S.op('dve', lambda e, n=n: e.bn_stats(out=bst[0:n, :], in_=osb[0:n, :]), reads=['osb'], writes=['bst'])
                S.op('dve', lambda e, n=n: e.bn_aggr(out=mv[0:n, 0:2], in_=bst[0:n, :]), reads=['bst'], writes=['mv'])
                ts(mv[0:n, 2:3], mv[0:n, 1:2], EPS, None, ALU.add, rd=['mv'], wr=['mv'])
                S.op('act', lambda e, n=n: e.activation(out=mv[0:n, 2:3], in_=mv[0:n, 2:3], func=AF.Sqrt), reads=['mv'], writes=['mv'])
                S.op('dve', lambda e, n=n: e.reciprocal(out=mv[0:n, 3:4], in_=mv[0:n, 2:3]), reads=['mv'], writes=['mv'])
                ts(on[0:n, :], osb[0:n, :], mv[0:n, 0:1], mv[0:n, 3:4], ALU.subtract, ALU.mult, rd=['osb', 'mv'], wr=['on'])
                for e2 in range(2):
                    S.op('pe', lambda e, n=n, e2=e2: e.transpose(ps[:, 2, e2 * 128:e2 * 128 + n], on[0:n, e2 * 128:(e2 + 1) * 128], ident[0:n, 0:n]),
                         reads=['on', 'ident'], writes=[('ps', 2)])
                    tt(zT[:, 2 * h + e2, t0:t0 + n], ps[:, 2, e2 * 128:e2 * 128 + n], sgh[:, e2, t0:t0 + n], ALU.mult, rd=[('ps', 2), 'sgh'], wr=['zT'])
                if c > 0:
                    S.op('pe', lambda e, n=n, c=c: e.matmul(ps[:, 6, 0:256], lhsT=kBall[0:n, c, :], rhs=vtok[0:n, c, :], start=True, stop=True),
                         reads=['kBall', 'vtok'], writes=[('ps', 6)])
                    scB = cdB[:, h:h + 1] if c < 8 else g1B[:, h:h + 1]
                    S.op('dve', lambda e, scB=scB: e.scalar_tensor_tensor(out=SB[:], in0=SB[:], scalar=scB, in1=ps[:, 6, 0:256], op0=ALU.mult, op1=ALU.add),
                         reads=['SB', 'cdB', 'g1B', ('ps', 6)], writes=['SB'])
                    S.op('act', lambda e: e.activation(out=SB16[:], in_=SB[:], func=AF.Copy), reads=['SB'], writes=['SB16'])
        S.barrier()
    if 'zT' in P.dbg:
        dz = nc.dram_tensor("dbg_zT", [128, KT, NTP], BF16, kind="ExternalOutput").ap()
        S.dma('sp', dz, zT[:], reads=['zT']); S.barrier()
    if stop_after <= 5:
        S.finish(); return nc

    def gated_epi(st, tag, gate_src, add_src, dst_dram, dst_sb, dst_key):
        gb = [P.sb(st, "%s_g%d" % (tag, i), [128, NTP]) for i in range(1)]
        ab_ = [P.sb(st, "%s_a%d" % (tag, i), [128, NTP]) for i in range(1)] if add_src is not None else None
        ob = [P.sb(st, "%s_ob%d" % (tag, i), [128, NTP]) for i in range(1)]
        state = {'i': 0}

        def epi(si, ch, pss):
            i = state['i']; state['i'] += 1
            j = 0
            cidx = si * 2 + ch
            r0 = cidx * 128
            gk, ak, ok = "%s_g%d" % (tag, j), "%s_a%d" % (tag, j), "%s_ob%d" % (tag, j)
            S.dma('pool', gb[j][:, 0:NT], gate_src[r0:r0 + 128, 0:NT], writes=[gk])
            if add_src is not None:
                S.dma('pool', ab_[j][:, 0:NT], add_src[r0:r0 + 128, 0:NT], writes=[ak])
            for (t0, n), (pap, pk) in zip(tgm, pss):
                tt(ob[j][:, t0:t0 + n], pap, gb[j][:, t0:t0 + n], ALU.mult, rd=[pk, gk], wr=[ok])
            if add_src is not None:
                tt(ob[j][:, 0:NT], ob[j][:, 0:NT], ab_[j][:, 0:NT], ALU.add, rd=[ok, ak], wr=[ok])
            if dst_dram is not None:
                S.dma('pool', dst_dram[r0:r0 + 128, 0:NT], ob[j][:, 0:NT], reads=[ok], writes=[('dram', id(dst_dram))])
            else:
                S.op('act', lambda e: e.activation(out=dst_sb[:, cidx, 0:NT], in_=ob[j][:, 0:NT], func=AF.Copy), reads=[ok], writes=[dst_key])
        return epi

    sq_slabs = [[(s_ * 256, 256)] for s_ in range(16)]
    with contextlib.ExitStack() as st:
        P.proj_fm('ro', w_ret_o, D, sq_slabs, zT, 'zT', tgm, gated_epi(st, 'ro', sr_m, None, yr_s, None, None), PB, ps)
    zst.close()
    S.barrier()
    yst = contextlib.ExitStack()
    yT = P.sb(yst, "yT", [128, KT, NTP], BF16)
    with contextlib.ExitStack() as st:
        uT = P.sb(st, "uT", [128, KT, NTP], BF16)
        with contextlib.ExitStack() as st2:
            g_ = [P.sb(st2, "u_g%d" % i, [128, NTP]) for i in range(2)]
            y_ = [P.sb(st2, "u_y%d" % i, [128, NTP]) for i in range(2)]
            for kt in range(KT):
                j = kt % 2
                S.dma('pool', g_[j][:, 0:NT], gg_m[kt * 128:(kt + 1) * 128, 0:NT], writes=['u_g%d' % j])
                S.dma('pool', y_[j][:, 0:NT], yl_s[kt * 128:(kt + 1) * 128, 0:NT], writes=['u_y%d' % j])
                tt(uT[:, kt, 0:NT], g_[j][:, 0:NT], y_[j][:, 0:NT], ALU.mult, rd=['u_g%d' % j, 'u_y%d' % j], wr=['uT'])
            S.barrier()
        P.proj_fm('lo', w_lru_o, D, sq_slabs, uT, 'uT', tgm, gated_epi(st, 'lo', sl_m, yr_s, None, yT, 'yT'), PB, ps)
    S.barrier()
    if 'yT' in P.dbg:
        dz = nc.dram_tensor("dbg_yT", [128, KT, NTP], BF16, kind="ExternalOutput").ap()
        S.dma('sp', dz, yT[:], reads=['yT']); S.barrier()
    if stop_after <= 6:
        S.finish(); return nc
    with contextlib.ExitStack() as st:
        ssacc = P.sb(st, "ssacc", [128, NTP])
        S.op('dve', lambda e: e.memset(ssacc[:], 0.0), writes=['ssacc'])
        xb2 = [P.sb(st, "wo_x%d" % i, [128, NTP]) for i in range(2)]
        ob2 = [P.sb(st, "wo_o%d" % i, [128, NTP]) for i in range(2)]
        sq2 = P.sb(st, "wo_sq", [128, NTP])
        stt = {'i': 0}

        def epi_wo(si, ch, pss):
            i = stt['i']; stt['i'] += 1
            j = i % 2
            cidx = si * 2 + ch
            r0 = cidx * 128
            xk, ok = 'wo_x%d' % j, 'wo_o%d' % j
            S.dma('pool', xb2[j][:, 0:NT], xT_m[r0:r0 + 128, 0:NT], writes=[xk])
            for (t0, n), (pap, pk) in zip(tgm, pss):
                S.op('dve', lambda e, t0=t0, n=n, pap=pap: e.scalar_tensor_tensor(out=ob2[j][:, t0:t0 + n], in0=pap, scalar=G1[:, cidx:cidx + 1], in1=xb2[j][:, t0:t0 + n], op0=ALU.mult, op1=ALU.add),
                     reads=[pk, 'G1', xk], writes=[ok])
            S.op('act', lambda e: e.activation(out=sq2[:, 0:NT], in_=ob2[j][:, 0:NT], func=AF.Square), reads=[ok], writes=['wo_sq'])
            tt(ssacc[:, 0:NT], ssacc[:, 0:NT], sq2[:, 0:NT], ALU.add, rd=['ssacc', 'wo_sq'], wr=['ssacc'])
            S.dma('pool', xlatT[r0:r0 + 128, 0:NT], ob2[j][:, 0:NT], reads=[ok], writes=['xlatT'])
        P.proj_fm('wo', w_out, D, sq_slabs, yT, 'yT', tgm, epi_wo, PB, ps)
        if stop_after <= 6.5:
            S.finish(); return nc
        for gi, (t0, n) in enumerate(tgm):
            S.op('pe', lambda e, gi=gi, t0=t0, n=n: e.matmul(ps[:, gi, 0:n], lhsT=ones[:], rhs=ssacc[:, t0:t0 + n], start=True, stop=True),
                 reads=['ones', 'ssacc'], writes=[('ps', gi)])
            S.op('act', lambda e, gi=gi, t0=t0, n=n: e.activation(out=rstd2[:, t0:t0 + n], in_=ps[:, gi, 0:n], func=AF.Sqrt, scale=1.0 / D, bias=epsc[:, 0:1]),
                 reads=[('ps', gi), 'epsc'], writes=['rstd2'])
        S.op('dve', lambda e: e.reciprocal(out=rstd2[:, 0:NT], in_=rstd2[:, 0:NT]), reads=['rstd2'], writes=['rstd2'])
        S.barrier()
        if stop_after <= 6.7:
            dz = nc.dram_tensor('dbg_rstd2', [128, NTP], F32, kind='ExternalOutput').ap()
            S.dma('sp', dz, rstd2[:], reads=['rstd2']); S.finish(); return nc
    yst.close()
    S.barrier()
    hst = contextlib.ExitStack()
    h2T = P.sb(hst, "h2T", [128, KT, NTP], BF16)
    with contextlib.ExitStack() as st:
        xb3 = [P.sb(st, "n2_x%d" % i, [128, NTP]) for i in range(2)]
        for kt in range(KT):
            j = kt % 2
            xk = 'n2_x%d' % j
            S.dma('pool', xb3[j][:, 0:NT], xlatT[kt * 128:(kt + 1) * 128, 0:NT], writes=[xk])
            tt(xb3[j][:, 0:NT], xb3[j][:, 0:NT], rstd2[:, 0:NT], ALU.mult, rd=[xk, 'rstd2'], wr=[xk])
            S.op('act', lambda e, j=j, kt=kt: e.activation(out=h2T[:, kt, 0:NT], in_=xb3[j][:, 0:NT], func=AF.Identity, scale=A2[:, kt:kt + 1], bias=B2[:, kt:kt + 1]),
                 reads=[xk, 'A2', 'B2'], writes=['h2T'])
        S.barrier()
    if 'h2T' in P.dbg:
        dz = nc.dram_tensor("dbg_h2T", [128, KT, NTP], BF16, kind="ExternalOutput").ap()
        S.dma('sp', dz, h2T[:], reads=['h2T']); S.barrier()
    if stop_after <= 7:
        S.finish(); return nc
    with contextlib.ExitStack() as st:
        fcw = P.sb(st, "fcw", [128, 2 * FFT, 3]); fcb = P.sb(st, "fcb", [128, 2 * FFT])
        S.dma('sp', fcw[:], fconvT, writes=['fcw'])
        S.dma('sp', fcb[:], fconvbT, writes=['fcb'])
        ub = [P.sb(st, "ub%d" % i, [128, NTP + 2]) for i in range(2)]
        acc = [P.sb(st, "acc%d" % i, [128, 1024]) for i in range(2)]
        sgt = [P.sb(st, "sgt%d" % i, [128, 1024]) for i in range(2)]
        ao = [P.sb(st, "ao%d" % i, [128, 1024], BF16) for i in range(2)]
        for i in range(2):
            S.op('dve', lambda e, i=i: e.memset(ub[i][:], 0.0), writes=['ub%d' % i])
        stt9 = {'i': 0}

        def epi_up(si, ch, pss):
            i = stt9['i']; stt9['i'] += 1
            j = i % 2
            isv = ch >= 2
            gch = 2 * si + (ch % 2)
            cw = gch + (FFT if isv else 0)
            ubk, acck = 'ub%d' % j, 'acc%d' % j
            for (t0, n), (pap, pk) in zip(tgm, pss):
                S.op('act', lambda e, t0=t0, n=n, pap=pap: e.activation(out=ub[j][:, 1 + t0:1 + t0 + n], in_=pap, func=AF.Copy), reads=[pk], writes=[ubk])
            S.op('dve', lambda e: e.tensor_scalar(out=acc[j][:], in0=ub[j][:, 0:1024], scalar1=fcw[:, cw, 0:1], scalar2=fcb[:, cw:cw + 1], op0=ALU.mult, op1=ALU.add),
                 reads=[ubk, 'fcw', 'fcb'], writes=[acck])
            for tap in (1, 2):
                S.op('dve', lambda e, tap=tap: e.scalar_tensor_tensor(out=acc[j][:], in0=ub[j][:, tap:tap + 1024], scalar=fcw[:, cw, tap:tap + 1], in1=acc[j][:], op0=ALU.mult, op1=ALU.add),
                     reads=[ubk, 'fcw', acck], writes=[acck])
            if not isv:
                S.op('act', lambda e: e.activation(out=sgt[ch][:], in_=acc[j][:], func=AF.Silu), reads=[acck], writes=['sgt%d' % ch])
            else:
                c2 = ch - 2
                tt(ao[c2][:], acc[j][:], sgt[c2][:], ALU.mult, rd=[acck, 'sgt%d' % c2], wr=['ao%d' % c2])
                S.dma('pool', a_scr[gch * 128:(gch + 1) * 128, :], ao[c2][:], reads=['ao%d' % c2], writes=['a_scr'])
        up_slabs = [[(s_ * 256, 256), (FF + s_ * 256, 256)] for s_ in range(43)]
        P.proj_fm('up', w_up, D, up_slabs, h2T, 'h2T', tgm, epi_up, PB, ps)
    hst.close()
    S.barrier()
    if 'a_scr' in P.dbg and stop_after <= 8:
        S.finish(); return nc
    with contextlib.ExitStack() as st:
        fnT_ = P.sb(st, "fnT_", [128, KT])
        S.dma('sp', fnT_[:], fnormT, writes=['fnT_'])
        KQ = 4
        wst = [P.sb(st, "dn_ws%d" % i, [128, KQ, 384]) for i in range(2)]
        wbb = [P.sb(st, "dn_wb%d" % i, [128, KQ, 384], BF16) for i in range(2)]
        abuf = [P.sb(st, "dn_a%d" % i, [128, KQ, 1024], BF16) for i in range(2)]
        xl4 = [P.sb(st, "dn_x%d" % i, [128, 1024]) for i in range(2)]
        xo4 = [P.sb(st, "dn_o%d" % i, [128, 1024]) for i in range(2)]
        xf4 = [P.sb(st, "dn_f%d" % i, [128, 1024]) for i in range(2)]
        sq4 = P.sb(st, "dn_sq", [128, 1024]); ssa = P.sb(st, "dn_ssa", [128, 1024])
        ot = [P.sb(st, "dn_ot%d" % i, [128, 8, 128]) for i in range(2)]
        rs3 = P.sb(st, "dn_rs3", [128, 8])
        big = [P.sb(st, "dn_big%d" % i, [128, D]) for i in range(2)]
        S.op('dve', lambda e: e.memset(ssa[:], 0.0), writes=['dn_ssa'])
        y_v = y_out.rearrange("(i p) f -> p i f", p=128)
        grp = 0
        ei = 0
        for cg in range(11):
            c0 = cg * 384
            ncol = min(384, D - c0)
            nch = ncol // 128
            nkq = (FFT + KQ - 1) // KQ
            for kq in range(nkq):
                k0 = kq * KQ
                kn = min(KQ, FFT - k0)
                j = grp % 2; grp += 1
                S.dma('sp', wst[j][:, 0:kn, 0:ncol], w_down[k0 * 128:(k0 + kn) * 128, c0:c0 + ncol].rearrange("(k p) c -> p k c", p=128), writes=['dn_ws%d' % j])
                S.dma('pool', abuf[j][:, 0:kn, :], a_scr[k0 * 128:(k0 + kn) * 128, :].rearrange("(k p) t -> p k t", p=128), reads=['a_scr'], writes=['dn_a%d' % j])
                if grp % 2:
                    S.op('act', lambda e, j=j, kn=kn, ncol=ncol: e.activation(out=wbb[j][:, 0:kn, 0:ncol], in_=wst[j][:, 0:kn, 0:ncol], func=AF.Copy), reads=['dn_ws%d' % j], writes=['dn_wb%d' % j])
                else:
                    S.op('dve', lambda e, j=j, kn=kn, ncol=ncol: e.tensor_copy(out=wbb[j][:, 0:kn, 0:ncol], in_=wst[j][:, 0:kn, 0:ncol]), reads=['dn_ws%d' % j], writes=['dn_wb%d' % j])
                for q in range(kn):
                    kt = k0 + q
                    for ch in range(nch):
                        for g in range(2):
                            b = ch * 2 + g
                            last = (q == kn - 1 and ch == nch - 1 and g == 1)
                            S.op('pe', lambda e, b=b, j=j, q=q, ch=ch, g=g, kt=kt: e.matmul(ps[:, b, :], lhsT=wbb[j][:, q, ch * 128:(ch + 1) * 128], rhs=abuf[j][:, q, g * 512:(g + 1) * 512], start=(kt == 0), stop=(kt == FFT - 1)),
                                 reads=['dn_wb%d' % j, 'dn_a%d' % j], writes=[('ps', b)], inc=(kt == FFT - 1 or last))
            for ch in range(nch):
                cidx = cg * 3 + ch
                j = ei % 2; ei += 1
                xk, ok, fk, otk = 'dn_x%d' % j, 'dn_o%d' % j, 'dn_f%d' % j, 'dn_ot%d' % j
                S.dma('pool', xl4[j][:], xlatT[cidx * 128:(cidx + 1) * 128, 0:1024], writes=[xk])
                for g in range(2):
                    b = ch * 2 + g
                    S.op('dve', lambda e, b=b, g=g, j=j, cidx=cidx: e.scalar_tensor_tensor(out=xo4[j][:, g * 512:(g + 1) * 512], in0=ps[:, b, :], scalar=G2[:, cidx:cidx + 1], in1=xl4[j][:, g * 512:(g + 1) * 512], op0=ALU.mult, op1=ALU.add),
                         reads=[('ps', b), 'G2', xk], writes=[ok])
                S.op('act', lambda e, j=j: e.activation(out=sq4[:], in_=xo4[j][:], func=AF.Square), reads=[ok], writes=['dn_sq'])
                tt(ssa[:], ssa[:], sq4[:], ALU.add, rd=['dn_ssa', 'dn_sq'], wr=['dn_ssa'])
                S.op('act', lambda e, j=j, cidx=cidx: e.activation(out=xf4[j][:], in_=xo4[j][:], func=AF.Identity, scale=fnT_[:, cidx:cidx + 1]), reads=[ok, 'fnT_'], writes=[fk])
                for tb in range(2):
                    bk = 6 + tb
                    for q in range(4):
                        ti_ = tb * 4 + q
                        S.op('pe', lambda e, bk=bk, q=q, ti_=ti_, j=j: e.transpose(ps[:, bk, q * 128:(q + 1) * 128], xf4[j][:, ti_ * 128:(ti_ + 1) * 128], ident[:]),
                             reads=[fk, 'ident'], writes=[('ps', bk)], inc=(q == 3))
                    P.evac(ot[j][:, tb * 4:(tb + 1) * 4, :], ps[:, bk, :].rearrange("p (q f) -> p q f", q=4), [('ps', bk)], [otk])
                S.dma('pool', y_v[:, :, cidx * 128:(cidx + 1) * 128], ot[j][:], reads=[otk], writes=['y_out'])
        for i in range(8):
            S.op('pe', lambda e, i=i: e.matmul(ps[:, 0, i:i + 1], lhsT=ssa[:, i * 128:(i + 1) * 128], rhs=ones[:, 0:1], start=True, stop=True),
                 reads=['dn_ssa', 'ones'], writes=[('ps', 0)])
        S.op('act', lambda e: e.activation(out=rs3[:], in_=ps[:, 0, 0:8], func=AF.Sqrt, scale=1.0 / D, bias=epsc[:, 0:1]), reads=[('ps', 0), 'epsc'], writes=['dn_rs3'])
        S.op('dve', lambda e: e.reciprocal(out=rs3[:], in_=rs3[:]), reads=['dn_rs3'], writes=['dn_rs3'])
        for i in range(8):
            j = i % 2
            bk_ = 'dn_big%d' % j
            S.dma('sp', big[j][:], y_out[i * 128:(i + 1) * 128, :], reads=['y_out'], writes=[bk_])
            S.op('act', lambda e, i=i, j=j: e.activation(out=big[j][:], in_=big[j][:], func=AF.Identity, scale=rs3[:, i:i + 1]), reads=[bk_, 'dn_rs3'], writes=[bk_])
            S.dma('sp', y_out[i * 128:(i + 1) * 128, :], big[j][:], reads=[bk_], writes=['y_out2'])
        S.barrier()
    S.finish()
    return nc


def _pT(v, nchunk):
    return np.ascontiguousarray(np.asarray(v, np.float32).reshape(nchunk, 128).T)


def prep_shared(inp):
    sh = {}
    sh["w_ada"] = np.ascontiguousarray(inp["w_ada"][0])
    sh["b_adaT"] = _pT(inp["b_ada"][0], 192)
    sh["norm1T"] = _pT(inp["norm1"][0], KT)
    sh["norm2T"] = _pT(inp["norm2"][0], KT)
    sh["fnormT"] = _pT(inp["final_norm"], KT)
    sh["w_in"] = np.ascontiguousarray(inp["w_in"][0])
    sh["convbT"] = _pT(inp["lru_conv_b"][0], KT)
    sh["w_ret_o"] = np.ascontiguousarray(inp["w_ret_o"][0])
    sh["w_lru_o"] = np.ascontiguousarray(inp["w_lru_o"][0])
    sh["w_out"] = np.ascontiguousarray(inp["w_out"][0])
    sh["w_up"] = np.ascontiguousarray(inp["w_up"][0])
    sh["fconvbT"] = _pT(inp["ffn_conv_b"][0], 2 * FFT)
    sh["w_down"] = np.ascontiguousarray(inp["w_down"][0])
    sh["ident"] = np.eye(128, dtype=np.float32)
    r = np.zeros((128, 128), np.float32)
    for d in range(64):
        r[d + 64, d] = -1.0
        r[d, d + 64] = 1.0
    sh["rmat"] = r
    return sh


def prep_core(inp, sh, b, half):
    flip = (half == 1)
    m = dict(sh)
    xb = inp["x"][b]
    cx = inp["ctx"][b]
    if flip:
        xb = xb[::-1]
        cx = cx[::-1]
    xm = np.zeros((1152, D), np.float32)
    xm[:NT] = xb[:NT]
    xp = np.zeros((NPFP, D), np.float32)
    xp[:256] = cx
    xp[256:NPF] = xb[NT:2048]
    m["x_main"] = xm
    m["x_pre"] = xp
    cT = np.stack([_pT(inp["c"][b], KT), _pT(inp["c_ctx"], KT)], axis=-1)
    m["cT"] = np.ascontiguousarray(cT)
    zs = [1, 0] if flip else [0, 1]
    m["retlog"] = np.ascontiguousarray(np.broadcast_to(inp["ret_decay_logit"][0][zs].reshape(1, 2 * H), (128, 2 * H)))
    cw = inp["lru_conv_w"][0]
    c5 = np.zeros((5, D), np.float32)
    if flip:
        c5[1:5] = cw[::-1]
    else:
        c5[0:4] = cw
    m["conv5T"] = np.ascontiguousarray(np.stack([_pT(c5[j], KT) for j in range(5)], axis=-1))
    m["lru_wa"] = np.ascontiguousarray(inp["lru_wa"][0][zs])
    m["lru_wx"] = np.ascontiguousarray(inp["lru_wx"][0][zs])
    for nm, key in (("lru_baT", "lru_ba"), ("lru_bxT", "lru_bx"), ("lru_lamT", "lru_lambda")):
        a = inp[key][0][zs]
        m[nm] = np.ascontiguousarray(np.stack([_pT(a[0], KT), _pT(a[1], KT)], axis=1))
    fw = inp["ffn_conv_w"][0]
    if flip:
        fw = fw[::-1]
    m["fconvT"] = np.ascontiguousarray(np.stack([_pT(fw[j], 2 * FFT) for j in range(3)], axis=-1))
    d = np.arange(128)
    rc = np.zeros((128, 5), np.float32)
    rc[:, 0] = -1.0 if flip else 1.0
    rc[:, 1] = 31.0 if flip else 0.0
    rc[:, 2] = 63.0 if flip else 0.0
    rc[:, 3] = d % 32
    rc[:, 4] = ((d % 64) < 32).astype(np.float32)
    m["ropec"] = rc
    return m


def kernel(**inputs):
    inp = {k: np.asarray(v) for k, v in inputs.items()}
    nc = build()
    sh = prep_shared(inp)
    maps = []
    for core in range(8):
        maps.append(prep_core(inp, sh, core // 2, core % 2))
    res = run_bass_kernel_spmd(nc, maps, core_ids=list(range(8)))
    out = np.empty((4, 2048, D), np.float32)
    for core in range(8):
        b, half = core // 2, core % 2
        y = np.asarray(res.results[core]["y_out"], np.float32)
        if half == 0:
            out[b, 0:1024] = y
        else:
            out[b, 1024:2048] = y[::-1]
    return out
```

```python
import contextlib
import numpy as np
import concourse.bass as bass
import concourse.mybir as mybir
from concourse.bass_utils import run_bass_kernel_spmd

F32 = mybir.dt.float32
BF16 = mybir.dt.bfloat16
I32 = mybir.dt.int32
AF = mybir.ActivationFunctionType
ALU = mybir.AluOpType
AX = mybir.AxisListType

D = 4096
KT = 32
NT = 1025
NTP = 1026
NPF = 1279
NPFP = 1280
H = 16
FF = 11008
FFT = 86
EPS = 1e-6
TWO_PI = 6.283185307179586


class Sched:
    ENG = ('pe', 'act', 'dve', 'pool', 'sp')

    def __init__(self, nc, stack, n_dma_sems=(('sp', 14), ('pool', 14))):
        self.nc = nc
        self.sem = {}
        for e in self.ENG:
            self.sem['c_' + e] = stack.enter_context(nc.semaphore('c_' + e))
        self.cnt = {e: 0 for e in self.ENG}
        self.pending = {e: False for e in self.ENG}
        self.seen = {e: {} for e in self.ENG}
        self.prog = {e: [] for e in self.ENG}
        self.last_write = {}
        self.readers = {}
        self.dma_sems = {}
        self.dma_uses = {}
        self.dma_rr = {}
        for q, n in n_dma_sems:
            names = []
            for i in range(n):
                nm = 'd_%s_%d' % (q, i)
                self.sem[nm] = stack.enter_context(nc.semaphore(nm))
                self.dma_uses[nm] = 0
                names.append(nm)
            self.dma_sems[q] = names
            self.dma_rr[q] = 0

    def _deps(self, eng, reads, writes):
        deps = {}

        def add(tok):
            if tok is None:
                return
            s, v = tok
            if deps.get(s, 0) < v:
                deps[s] = v
        for r in reads:
            add(self.last_write.get(r))
            if isinstance(r, tuple) and r[0] == 'ps':
                for s, v in self.readers.get(r, {}).items():
                    if s != 'c_' + eng:
                        add((s, v))
        for r in writes:
            add(self.last_write.get(r))
            for s, v in self.readers.get(r, {}).items():
                add((s, v))
        waits = []
        for s, v in deps.items():
            if eng == 'pe' and s == 'c_pe':
                continue
            if self.seen[eng].get(s, 0) < v:
                self.seen[eng][s] = v
                waits.append((s, v))
        return waits

    def _commit(self, tok, reads, writes):
        for r in writes:
            self.last_write[r] = tok
            self.readers[r] = {}
        for r in reads:
            d = self.readers.setdefault(r, {})
            if d.get(tok[0], 0) < tok[1]:
                d[tok[0]] = tok[1]

    def op(self, eng, emit, reads=(), writes=(), inc=True):
        waits = self._deps(eng, reads, writes)
        if inc:
            self.cnt[eng] += 1
            self.pending[eng] = False
            tok = ('c_' + eng, self.cnt[eng])
        else:
            self.pending[eng] = True
            tok = ('c_' + eng, self.cnt[eng] + 1)
        self.prog[eng].append((waits, emit, ('c_' + eng, 1) if inc else None))
        self._commit(tok, reads, writes)
        return tok

    def dma(self, q, out, in_, reads=(), writes=(), **kw):
        names = self.dma_sems[q]
        nm = names[self.dma_rr[q] % len(names)]
        self.dma_rr[q] += 1
        waits = self._deps(q, reads, writes)
        prev = self.dma_uses[nm] * 16
        if prev and self.seen[q].get(nm, 0) < prev:
            self.seen[q][nm] = prev
            waits.append((nm, prev))
        self.dma_uses[nm] += 1
        tok = (nm, self.dma_uses[nm] * 16)

        def emit(e, out=out, in_=in_, kw=kw):
            return e.dma_start(out=out, in_=in_, **kw)
        self.prog[q].append((waits, emit, (nm, 16)))
        self._commit(tok, reads, writes)
        return tok

    def barrier(self):
        toks = []
        for e in ('pe', 'act', 'dve', 'pool'):
            if self.pending[e]:
                raise RuntimeError('pending non-inc op on %s at barrier' % e)
            if self.cnt[e]:
                toks.append(('c_' + e, self.cnt[e]))
        for nm, u in self.dma_uses.items():
            if u:
                toks.append((nm, u * 16))
        for e in self.ENG:
            waits = []
            for s, v in toks:
                if self.seen[e].get(s, 0) < v:
                    self.seen[e][s] = v
                    waits.append((s, v))
            if waits:
                self.prog[e].append((waits, None, None))
        self.last_write = {}
        self.readers = {}

    def finish(self):
        nc = self.nc
        self.barrier()
        sem = self.sem
        prog = self.prog
        with nc.Block() as block:
            def run(e):
                def body(engobj):
                    for waits, emit, inc in prog[e]:
                        for s, v in waits:
                            engobj.wait_ge(sem[s], v)
                        if emit is None:
                            continue
                        ins = emit(engobj)
                        if inc is not None:
                            ins.then_inc(sem[inc[0]], inc[1])
                return body
            block.tensor(run('pe'))
            block.scalar(run('act'))
            block.vector(run('dve'))
            block.gpsimd(run('pool'))
            block.sync(run('sp'))


class Prog:
    def __init__(self, dbg=None, only_inputs=None):
        self.dbg = dbg or {}
        self.only_inputs = only_inputs
        self.nc = bass.Bass("TRN2", target_bir_lowering=False)
        self.outer = contextlib.ExitStack()
        self.S = Sched(self.nc, self.outer)
        self.cast_rr = 0
        self.evac_rr = 0

    def din(self, name, shape, dt=F32):
        if self.only_inputs is not None and name not in self.only_inputs:
            return None
        return self.nc.dram_tensor(name, list(shape), dt, kind="ExternalInput").ap()

    def dscr(self, name, shape, dt=F32):
        kind = "ExternalOutput" if name in self.dbg else "Internal"
        return self.nc.dram_tensor(name, list(shape), dt, kind=kind).ap()

    def sb(self, st, name, shape, dt=F32):
        self._uid = getattr(self, "_uid", 0) + 1
        return st.enter_context(self.nc.sbuf_tensor("s%d_%s" % (self._uid, name), list(shape), dt))

    def proj_fm(self, *a, **kw):
        for _ in self.proj_gen(*a, **kw):
            pass

    def proj_gen(self, tag, w, K, col_segs, act, act_key, tok_groups, epilogue, banks, ps, nstg=3, flush=False):
        S = self.S
        nc = self.nc
        kt_n = K // 128
        SW = sum(n for _, n in col_segs[0])
        KQ = 4
        with contextlib.ExitStack() as st:
            stg = [self.sb(st, '%s_stg%d' % (tag, i), [128, KQ, SW], F32) for i in range(nstg)]
            slab = [self.sb(st, '%s_slab%d' % (tag, i), [128, kt_n, SW], BF16) for i in range(2)]
            stg_i = 0
            pending_epi = None
            bank_rr = 0
            nb = len(banks)
            for si, segs in enumerate(col_segs):
                sl = slab[si % 2]
                slk = '%s_slab%d' % (tag, si % 2)
                kq0 = 0
                while kq0 < kt_n:
                    kq = min(KQ, kt_n - kq0)
                    sg = stg[stg_i % nstg]
                    sgk = '%s_stg%d' % (tag, stg_i % nstg)
                    stg_i += 1
                    off = 0
                    for (c0, ncol) in segs:
                        src = w[kq0 * 128:(kq0 + kq) * 128, c0:c0 + ncol].rearrange("(k p) c -> p k c", p=128)
                        S.dma('sp', sg[:, 0:kq, off:off + ncol], src, writes=[sgk])
                        off += ncol
                    wk = [(slk, kq0 + j) for j in range(kq)]
                    if self.cast_rr % 2 == 0:
                        S.op('act', lambda e, o=sl[:, kq0:kq0 + kq, :], i=sg[:, 0:kq, :]: e.activation(out=o, in_=i, func=AF.Copy),
                             reads=[sgk], writes=wk)
                    else:
                        S.op('dve', lambda e, o=sl[:, kq0:kq0 + kq, :], i=sg[:, 0:kq, :]: e.tensor_copy(out=o, in_=i),
                             reads=[sgk], writes=wk)
                    self.cast_rr += 1
                    kq0 += kq
                for ch in range(SW // 128):
                    pss = []
                    for (t0, n) in tok_groups:
                        b = banks[bank_rr % nb]
                        bank_rr += 1
                        pk = ('ps', b)
                        for kt in range(kt_n):
                            S.op('pe', lambda e, o=ps[:, b, 0:n], l=sl[:, kt, ch * 128:(ch + 1) * 128], r=act[:, kt, t0:t0 + n], a=(kt == 0), z=(kt == kt_n - 1):
                                 e.matmul(o, lhsT=l, rhs=r, start=a, stop=z),
                                 reads=[(slk, kt), act_key], writes=[pk], inc=(kt == kt_n - 1))
                        pss.append((ps[:, b, 0:n], pk))
                    if pending_epi is not None:
                        pending_epi()
                    pending_epi = (lambda si=si, ch=ch, pss=pss: epilogue(si, ch, pss))
                if flush and pending_epi is not None:
                    pending_epi()
                    pending_epi = None
                yield si
            if pending_epi is not None:
                pending_epi()
            S.barrier()

    def evac(self, out, in_, reads, writes, func=None, **kw):
        S = self.S
        if func is None and not kw:
            self.evac_rr += 1
            if self.evac_rr % 2:
                return S.op('dve', lambda e: e.tensor_copy(out=out, in_=in_), reads=reads, writes=writes)
            func = AF.Copy
        return S.op('act', lambda e: e.activation(out=out, in_=in_, func=func, **kw), reads=reads, writes=writes)


def build(dbg=None, stop_after=99, only_inputs=None):
    P = Prog(dbg, only_inputs)
    nc, S = P.nc, P.S
    o = P.outer
    x_main = P.din("x_main", [1152, D])
    x_pre = P.din("x_pre", [NPFP, D])
    cT = P.din("cT", [128, KT, 2])
    w_ada = P.din("w_ada", [D, 6 * D])
    b_adaT = P.din("b_adaT", [128, 192])
    norm1T = P.din("norm1T", [128, KT])
    norm2T = P.din("norm2T", [128, KT])
    fnormT = P.din("fnormT", [128, KT])
    w_in = P.din("w_in", [D, 28672])
    retlog = P.din("retlog", [128, 2 * H])
    conv5T = P.din("conv5T", [128, KT, 5])
    convbT = P.din("convbT", [128, KT])
    lru_wa = P.din("lru_wa", [2, 16, 256, 256])
    lru_wx = P.din("lru_wx", [2, 16, 256, 256])
    lru_baT = P.din("lru_baT", [128, 2, KT])
    lru_bxT = P.din("lru_bxT", [128, 2, KT])
    lru_lamT = P.din("lru_lamT", [128, 2, KT])
    w_ret_o = P.din("w_ret_o", [D, D])
    w_lru_o = P.din("w_lru_o", [D, D])
    w_out = P.din("w_out", [D, D])
    w_up = P.din("w_up", [D, 2 * FF])
    fconvT = P.din("fconvT", [128, 2 * FFT, 3])
    fconvbT = P.din("fconvbT", [128, 2 * FFT])
    w_down = P.din("w_down", [FF, D])
    ident_in = P.din("ident", [128, 128])
    rmat_in = P.din("rmat", [128, 128])
    ropec = P.din("ropec", [128, 5])
    y_out = nc.dram_tensor("y_out", [1024, D], F32, kind="ExternalOutput").ap()

    xT_m = P.dscr("xT_m", [D, NTP])
    kT_m = P.dscr("kT_m", [2048, NTP], BF16)
    qT_m = P.dscr("qT_m", [2048, NTP], BF16)
    vT_m = P.dscr("vT_m", [D, NTP], BF16)
    xl_m = P.dscr("xl_m", [D, NTP])
    sg_m = P.dscr("sg_m", [D, NTP])
    gg_m = P.dscr("gg_m", [D, NTP])
    sr_m = P.dscr("sr_m", [D, NTP])
    sl_m = P.dscr("sl_m", [D, NTP])
    kT_p = P.dscr("kT_p", [2048, NPFP], BF16)
    vT_p = P.dscr("vT_p", [D, NPFP], BF16)
    xl_p = P.dscr("xl_p", [D, NPFP])
    yl_s = P.dscr("yl_s", [D, NTP])
    yr_s = P.dscr("yr_s", [D, NTP])
    xlatT = P.dscr("xlatT", [D, NTP])
    a_scr = P.dscr("a_scr", [FF, 1024], BF16)
    st_scr = P.dscr("st_scr", [2, H, 128, 256])
    hl_scr = P.dscr("hl_scr", [2, D])

    T = lambda name, shape, dt=F32: P.sb(o, name, shape, dt)
    ps = o.enter_context(nc.psum_tensor("ps", [128, 8, 512], F32))

    ident = T("ident", [128, 128])
    identb = T("identb", [128, 128], BF16)
    rmat = T("rmatb", [128, 128], BF16)
    rtmp = T("rtmp", [128, 128])
    ones = T("ones", [128, 128])
    modT = T("modT", [128, 192, 2])
    A1l = T("A1l", [128, KT]); B1l = T("B1l", [128, KT])
    A1c = T("A1c", [128, KT]); B1c = T("B1c", [128, KT])
    A2 = T("A2", [128, KT]); B2 = T("B2", [128, KT])
    G1 = T("G1", [128, KT]); G2 = T("G2", [128, KT])
    n1 = T("n1", [128, KT]); n2 = T("n2", [128, KT]); bada = T("bada", [128, 192])
    lg = T("lg", [128, 2 * H])
    rc = T("rc", [128, 5])
    epsc = T('epsc', [128, 1])
    rstd2 = T('rstd2', [128, NTP])
    S.op('dve', lambda e: e.memset(epsc[:], EPS), writes=['epsc'])
    S.dma('sp', ident[:], ident_in, writes=['ident'])
    S.dma('sp', rtmp[:], rmat_in, writes=['rtmp'])
    S.dma('sp', n1[:], norm1T, writes=['n1'])
    S.dma('sp', n2[:], norm2T, writes=['n2'])
    S.dma('sp', bada[:], b_adaT, writes=['bada'])
    S.dma('sp', lg[:], retlog, writes=['lg'])
    S.dma('sp', rc[:], ropec, writes=['rc'])
    S.op('dve', lambda e: e.tensor_copy(out=identb[:], in_=ident[:]), reads=['ident'], writes=['identb'])
    S.op('dve', lambda e: e.tensor_copy(out=rmat[:], in_=rtmp[:]), reads=['rtmp'], writes=['rmat'])
    S.op('dve', lambda e: e.memset(ones[:], 1.0), writes=['ones'])
    S.op('act', lambda e: e.activation(out=lg[:], in_=lg[:], func=AF.Exp, scale=-1.0), reads=['lg'], writes=['lg'])
    S.op('act', lambda e: e.activation(out=lg[:], in_=lg[:], func=AF.Ln, bias=1.0), reads=['lg'], writes=['lg'])
    S.op('dve', lambda e: e.tensor_scalar(out=lg[:], in0=lg[:], scalar1=-1.0, scalar2=None, op0=ALU.mult), reads=['lg'], writes=['lg'])

    if 'inj_mod' in P.dbg:
        S.dma('sp', modT[:], P.din('modT_in', [128, 192, 2]), writes=['modT'])
    cf = T("cf", [128, KT, 2]); cb = T("cb", [128, KT, 2], BF16)
    ada_gen = None
    if 'inj_mod' not in P.dbg:
        S.dma('sp', cf[:], cT, writes=['cf'])
        S.op('act', lambda e: e.activation(out=cb[:], in_=cf[:], func=AF.Silu), reads=['cf'], writes=['cb'])

        def mk_epi_ada(slabs):
            def epi_ada(si, ch, pss):
                j = slabs[si][0][0] // 128 + ch
                (p0, pk), = pss
                S.op('dve', lambda e: e.tensor_scalar(out=modT[:, j, :], in0=p0, scalar1=bada[:, j:j + 1], scalar2=None, op0=ALU.add),
                     reads=[pk, 'bada'], writes=['modT'])
            return epi_ada
        early = [[(s_ * 512, 512)] for s_ in range(16)]
        late = [[(8192 + s_ * 256, 256)] for s_ in range(64)]
        P.proj_fm('ada', w_ada, D, early, cb, 'cb', [(0, 2)], mk_epi_ada(early), [0, 1, 2, 3], ps, nstg=8)
        ada_gen = P.proj_gen('ada2', w_ada, D, late, cb, 'cb', [(0, 2)], mk_epi_ada(late), [3, 4], ps, nstg=3, flush=True)
    def ts(out, in0, s1, s2, op0, op1=None, rd=(), wr=()):
        if op1 is None:
            S.op('dve', lambda e: e.tensor_scalar(out=out, in0=in0, scalar1=s1, scalar2=None, op0=op0), reads=rd, writes=wr)
        else:
            S.op('dve', lambda e: e.tensor_scalar(out=out, in0=in0, scalar1=s1, scalar2=s2, op0=op0, op1=op1), reads=rd, writes=wr)

    def tt(out, a, b, op, rd=(), wr=()):
        S.op('dve', lambda e: e.tensor_tensor(out=out, in0=a, in1=b, op=op), reads=rd, writes=wr)
    ts(A1l[:], modT[:, 32:64, 0], 1.0, None, ALU.add, rd=['modT'], wr=['A1l'])
    tt(A1l[:], A1l[:], n1[:], ALU.mult, rd=['A1l', 'n1'], wr=['A1l'])
    ts(A1c[:], modT[:, 32:64, 1], 1.0, None, ALU.add, rd=['modT'], wr=['A1c'])
    tt(A1c[:], A1c[:], n1[:], ALU.mult, rd=['A1c', 'n1'], wr=['A1c'])
    for dst, lo, col, nm in ((B1l, 0, 0, 'B1l'), (B1c, 0, 1, 'B1c')):
        S.op('dve', lambda e, dst=dst, lo=lo, col=col: e.tensor_copy(out=dst[:], in_=modT[:, lo:lo + 32, col]), reads=['modT'], writes=[nm])

    def late_tables():
        ts(A2[:], modT[:, 128:160, 0], 1.0, None, ALU.add, rd=['modT'], wr=['A2'])
        tt(A2[:], A2[:], n2[:], ALU.mult, rd=['A2', 'n2'], wr=['A2'])
        for dst, lo, col, nm in ((B2, 96, 0, 'B2'), (G1, 64, 0, 'G1'), (G2, 160, 0, 'G2')):
            S.op('dve', lambda e, dst=dst, lo=lo, col=col: e.tensor_copy(out=dst[:], in_=modT[:, lo:lo + 32, col]), reads=['modT'], writes=[nm])
        S.barrier()
    S.barrier()
    if 'modT' in P.dbg:
        dbg_mod = nc.dram_tensor("dbg_modT", [128, 192, 2], F32, kind="ExternalOutput").ap()
        S.dma('sp', dbg_mod, modT[:], reads=['modT'])
    if stop_after <= 0:
        S.finish(); return nc

    ro = contextlib.ExitStack()
    CQ = P.sb(ro, "CQ", [128, NTP]); SQ = P.sb(ro, "SQ", [128, NTP])
    CKm = P.sb(ro, "CKm", [128, NTP]); SKm = P.sb(ro, "SKm", [128, NTP])
    CKp = P.sb(ro, "CKp", [128, NPFP]); SKp = P.sb(ro, "SKp", [128, NPFP])
    with contextlib.ExitStack() as st:
        NL = 2048
        ri = P.sb(st, "ri", [128, NL], I32); ci = P.sb(st, "ci", [128, NL], I32)
        rf = P.sb(st, "rf", [128, NL]); cfl = P.sb(st, "cfl", [128, NL])
        ang = P.sb(st, "ang", [128, NL]); u = P.sb(st, "u", [128, NL]); ui = P.sb(st, "ui", [128, NL], I32)
        uf = P.sb(st, "uf", [128, NL]); m1 = P.sb(st, "m1", [128, NL])
        cosf = P.sb(st, "cosf", [128, NL]); sinf = P.sb(st, "sinf", [128, NL])
        invf = P.sb(st, "invf", [128, 1])
        S.op('pool', lambda e: e.iota(ri[:], pattern=[[1, 32], [0, 64]], base=0, channel_multiplier=0), writes=['ri'])
        S.op('pool', lambda e: e.iota(ci[:], pattern=[[0, 32], [1, 64]], base=0, channel_multiplier=0), writes=['ci'])
        S.op('dve', lambda e: e.tensor_copy(out=rf[:], in_=ri[:]), reads=['ri'], writes=['rf'])
        S.op('dve', lambda e: e.tensor_copy(out=cfl[:], in_=ci[:]), reads=['ci'], writes=['cfl'])
        ts(rf[:], rf[:], rc[:, 0:1], rc[:, 1:2], ALU.mult, ALU.add, rd=['rf', 'rc'], wr=['rf'])
        ts(cfl[:], cfl[:], rc[:, 0:1], rc[:, 2:3], ALU.mult, ALU.add, rd=['cfl', 'rc'], wr=['cfl'])
        tt(rf[:], rf[:], cfl[:], ALU.subtract, rd=['rf', 'cfl'], wr=['rf'])
        S.op('dve', lambda e: e.scalar_tensor_tensor(out=ang[:], in0=rf[:], scalar=rc[:, 4:5], in1=cfl[:], op0=ALU.mult, op1=ALU.add),
             reads=['rf', 'cfl', 'rc'], writes=['ang'])
        S.op('act', lambda e: e.activation(out=invf[:], in_=rc[:, 3:4], func=AF.Exp, scale=-float(np.log(10000.0) / 32.0)), reads=['rc'], writes=['invf'])
        ts(ang[:], ang[:], invf[:, 0:1], 1.0 / TWO_PI, ALU.mult, ALU.mult, rd=['ang', 'invf'], wr=['ang'])
        for (dst, shift, nm) in ((sinf, 0.0, 'sinf'), (cosf, 0.25, 'cosf')):
            ts(u[:], ang[:], shift, None, ALU.add, rd=['ang'], wr=['u'])
            S.op('dve', lambda e: e.tensor_copy(out=ui[:], in_=u[:]), reads=['u'], writes=['ui'])
            S.op('dve', lambda e: e.tensor_copy(out=uf[:], in_=ui[:]), reads=['ui'], writes=['uf'])
            tt(u[:], u[:], uf[:], ALU.subtract, rd=['u', 'uf'], wr=['u'])
            ts(m1[:], u[:], 0.5, None, ALU.is_gt, rd=['u'], wr=['m1'])
            tt(u[:], u[:], m1[:], ALU.subtract, rd=['u', 'm1'], wr=['u'])
            ts(m1[:], u[:], -0.5, None, ALU.is_lt, rd=['u'], wr=['m1'])
            tt(u[:], u[:], m1[:], ALU.add, rd=['u', 'm1'], wr=['u'])
            S.op('act', lambda e, dst=dst: e.activation(out=dst[:], in_=u[:], func=AF.Sin, scale=TWO_PI), reads=['u'], writes=[nm])
        sc = float(128 ** -0.5)
        S.op('dve', lambda e: e.tensor_copy(out=CQ[:, 0:NT], in_=cosf[:, 0:NT]), reads=['cosf'], writes=['CQ'])
        S.op('dve', lambda e: e.tensor_copy(out=SQ[:, 0:NT], in_=sinf[:, 0:NT]), reads=['sinf'], writes=['SQ'])
        ts(CKm[:, 0:NT], cosf[:, 0:NT], sc, None, ALU.mult, rd=['cosf'], wr=['CKm'])
        ts(SKm[:, 0:NT], sinf[:, 0:NT], sc, None, ALU.mult, rd=['sinf'], wr=['SKm'])
        S.op('dve', lambda e: e.memset(CKp[:], sc), writes=['CKp'])
        S.op('dve', lambda e: e.memset(SKp[:], 0.0), writes=['SKp'])
        ts(CKp[:, 256:NPF], cosf[:, 1025:2048], sc, None, ALU.mult, rd=['cosf', 'CKp'], wr=['CKp'])
        ts(SKp[:, 256:NPF], sinf[:, 1025:2048], sc, None, ALU.mult, rd=['sinf', 'SKp'], wr=['SKp'])
        S.barrier()
        if 'rope' in P.dbg:
            d1 = nc.dram_tensor("dbg_cos", [128, NL], F32, kind="ExternalOutput").ap()
            d2 = nc.dram_tensor("dbg_sin", [128, NL], F32, kind="ExternalOutput").ap()
            S.dma('sp', d1, cosf[:], reads=['cosf']); S.dma('sp', d2, sinf[:], reads=['sinf'])
            S.barrier()
    if stop_after <= 1:
        S.finish(); return nc


    def norm_tm(st, x_dram, ntile, hT, hkey, AB_of_tile, xT_spill):
        xt = [P.sb(st, "xt%d" % i, [128, D], F32) for i in range(2)]
        xs = [P.sb(st, "xs%d" % i, [128, D], F32) for i in range(2)]
        ss = P.sb(st, "ss", [128, 4]); xtb = None
        if xT_spill is not None:
            xtb = [P.sb(st, "xtb%d" % i, [128, KT, 128], F32) for i in range(2)]
        bank = 0
        for i in range(ntile):
            x_, xs_ = xt[i % 2], xs[i % 2]
            xk, xsk = 'xt%d' % (i % 2), 'xs%d' % (i % 2)
            S.dma('sp', x_[:], x_dram[i * 128:(i + 1) * 128, :], writes=[xk])
            S.op('act', lambda e, x_=x_, xs_=xs_: e.activation(out=xs_[:], in_=x_[:], func=AF.Square, accum_out=ss[:, 0:1]), reads=[xk], writes=[xsk, 'ss'])
            ts(ss[:, 1:2], ss[:, 0:1], 1.0 / D, EPS, ALU.mult, ALU.add, rd=['ss'], wr=['ss'])
            S.op('act', lambda e: e.activation(out=ss[:, 2:3], in_=ss[:, 1:2], func=AF.Sqrt), reads=['ss'], writes=['ss'])
            S.op('dve', lambda e: e.reciprocal(out=ss[:, 3:4], in_=ss[:, 2:3]), reads=['ss'], writes=['ss'])
            S.op('act', lambda e, x_=x_, xs_=xs_: e.activation(out=xs_[:], in_=x_[:], func=AF.Identity, scale=ss[:, 3:4]), reads=[xk, 'ss'], writes=[xsk])
            A, B, abk = AB_of_tile(i)
            for k4 in range(KT // 4):
                b = bank % 8; bank += 1
                for q in range(4):
                    kt = k4 * 4 + q
                    S.op('pe', lambda e, o_=ps[:, b, q * 128:(q + 1) * 128], i_=xs_[:, kt * 128:(kt + 1) * 128]: e.transpose(o_, i_, ident[:]),
                         reads=[xsk, 'ident'], writes=[('ps', b)], inc=(q == 3))
                nc_ = min(128, hT.shape[2] - i * 128)
                for q in range(4):
                    kt = k4 * 4 + q
                    S.op('act', lambda e, o_=hT[:, kt, i * 128:i * 128 + nc_], i_=ps[:, b, q * 128:q * 128 + nc_], A=A, B=B, kt=kt:
                         e.activation(out=o_, in_=i_, func=AF.Identity, scale=A[:, kt:kt + 1], bias=B[:, kt:kt + 1]),
                         reads=[('ps', b)] + abk, writes=[hkey])
            if xT_spill is not None:
                xb_ = xtb[i % 2]; xbk = 'xtb%d' % (i % 2)
                for k4 in range(KT // 4):
                    b = bank % 8; bank += 1
                    for q in range(4):
                        kt = k4 * 4 + q
                        S.op('pe', lambda e, o_=ps[:, b, q * 128:(q + 1) * 128], i_=x_[:, kt * 128:(kt + 1) * 128]: e.transpose(o_, i_, ident[:]),
                             reads=[xk, 'ident'], writes=[('ps', b)], inc=(q == 3))
                    S.op('dve', lambda e, o_=xb_[:, k4 * 4:(k4 + 1) * 4, :], i_=ps[:, b, :].rearrange("p (k t) -> p k t", k=4): e.tensor_copy(out=o_, in_=i_),
                         reads=[('ps', b)], writes=[xbk])
                n = min(128, NTP - i * 128)
                S.dma('sp', xT_spill[:, i * 128:i * 128 + n].rearrange("(k p) t -> p k t", p=128), xb_[:, :, 0:n], reads=[xbk], writes=['xT_spill'])
        S.barrier()

    def spill_epi(dst, width, row0_of, func, out_dt, tok_groups, nbuf_tag):
        bufs = [P.sb(cur_st[0], "%s_o%d" % (nbuf_tag, i), [128, width], out_dt) for i in range(2)]
        state = {'i': 0}

        def epi(si, ch, pss, dst=dst, r0=None, func=func):
            i = state['i']; state['i'] += 1
            bt = bufs[i % 2]; bk = "%s_o%d" % (nbuf_tag, i % 2)
            for (t0, n), (pap, pk) in zip(tok_groups, pss):
                P.evac(bt[:, t0:t0 + n], pap, [pk], [bk], func=func)
            if r0 is None:
                r0 = row0_of(si, ch)
            ntok = tok_groups[-1][0] + tok_groups[-1][1]
            S.dma('pool', dst[r0:r0 + 128, 0:ntok], bt[:, 0:ntok], reads=[bk], writes=[('dram', id(dst))])
        return epi

    cur_st = [None]

    def rope_epi(dst, Ct, St, ckeys, width, tok_groups, tag):
        kb = [P.sb(cur_st[0], "%s_kb%d" % (tag, i), [128, width], BF16) for i in range(1)]
        t1 = [P.sb(cur_st[0], "%s_t1%d" % (tag, i), [128, width], F32) for i in range(1)]
        t2 = [P.sb(cur_st[0], "%s_t2%d" % (tag, i), [128, width], F32) for i in range(1)]
        ob = [P.sb(cur_st[0], "%s_ob%d" % (tag, i), [128, width], BF16) for i in range(1)]
        state = {'i': 0}

        def epi(si, ch, pss, head, dst=dst, Ct=Ct, St=St, ckeys=ckeys):
            i = state['i']; state['i'] += 1
            j = 0
            kbk, t1k, t2k, obk = "%s_kb%d" % (tag, j), "%s_t1%d" % (tag, j), "%s_t2%d" % (tag, j), "%s_ob%d" % (tag, j)
            for gi, ((t0, n), (pap, pk)) in enumerate(zip(tok_groups, pss)):
                S.op('act', lambda e, o_=kb[j][:, t0:t0 + n], i_=pap: e.activation(out=o_, in_=i_, func=AF.Copy), reads=[pk], writes=[kbk])
                tt(t1[j][:, t0:t0 + n], pap, Ct[:, t0:t0 + n], ALU.mult, rd=[pk] + ckeys, wr=[t1k])
                rb = 6 + (gi % 2)
                import os
                if os.environ.get('ROPEDBG') == 'noR':
                    tt(t2[j][:, t0:t0 + n], pap, St[:, t0:t0 + n], ALU.mult, rd=[pk] + ckeys, wr=[t2k])
                else:
                    S.op('pe', lambda e, o_=ps[:, rb, 0:n], r_=kb[j][:, t0:t0 + n]: e.matmul(o_, lhsT=rmat[:], rhs=r_, start=True, stop=True),
                         reads=[kbk, 'rmat'], writes=[('ps', rb)])
                    tt(t2[j][:, t0:t0 + n], ps[:, rb, 0:n], St[:, t0:t0 + n], ALU.mult, rd=[('ps', rb)] + ckeys, wr=[t2k])
                tt(ob[j][:, t0:t0 + n], t1[j][:, t0:t0 + n], t2[j][:, t0:t0 + n], ALU.add, rd=[t1k, t2k], wr=[obk])
            ntok = tok_groups[-1][0] + tok_groups[-1][1]
            S.dma('pool', dst[head * 128:(head + 1) * 128, 0:ntok], ob[j][:, 0:ntok], reads=[obk], writes=[('dram', id(dst))])
        return epi

    PB = [0, 1, 2, 3, 4, 5]
    with contextlib.ExitStack() as st:
        cur_st[0] = st
        hTp = P.sb(st, "hTp", [128, KT, NPFP], BF16)
        with contextlib.ExitStack() as st2:
            norm_tm(st2, x_pre, 10, hTp, 'hTp', lambda i: (A1c, B1c, ['A1c', 'B1c']) if i < 2 else (A1l, B1l, ['A1l', 'B1l']), None)
        if 'hTp' in P.dbg:
            dz = nc.dram_tensor('dbg_hTp', [128, KT, NPFP], BF16, kind='ExternalOutput').ap()
            S.dma('sp', dz, hTp[:], reads=['hTp']); S.barrier()
            if stop_after <= 1.5:
                S.finish(); return nc
        tgp = [(0, 428), (428, 426), (854, 425)]
        pin_slabs = [[(s_ * 256, 256)] for s_ in range(40)]
        if 'few' in P.dbg:
            import os; pin_slabs = [pin_slabs[int(i)] for i in os.environ.get('FEW', '0,1,2,8,9,24,25').split(',')]
        e_k = rope_epi(kT_p, CKp, SKp, ['CKp', 'SKp'], NPFP, tgp, 'pk')
        e_v = spill_epi(vT_p, NPFP, lambda si, ch: pin_slabs[si][0][0] + ch * 128 - 2048, None, BF16, tgp, 'pv')
        e_x = spill_epi(xl_p, NPFP, lambda si, ch: pin_slabs[si][0][0] + ch * 128 - 6144, None, F32, tgp, 'px')

        def epi_p(si, ch, pss):
            c0 = pin_slabs[si][0][0] + ch * 128
            if c0 < 2048:
                e_k(si, ch, pss, c0 // 128)
            elif c0 < 6144:
                e_v(si, ch, pss)
            else:
                e_x(si, ch, pss)
        P.proj_fm('pin', w_in, D, pin_slabs, hTp, 'hTp', tgp, epi_p, PB, ps)
    if stop_after <= 2:
        S.finish(); return nc

    tgm = [(0, 342), (342, 342), (684, 341)]
    with contextlib.ExitStack() as st:
        cur_st[0] = st
        hTm = P.sb(st, "hTm", [128, KT, NTP], BF16)
        with contextlib.ExitStack() as st2:
            norm_tm(st2, x_main, 9, hTm, 'hTm', lambda i: (A1l, B1l, ['A1l', 'B1l']), xT_m)
        r_e = rope_epi(None, None, None, None, NTP, tgm, 'mk')
        e_b = spill_epi(None, NTP, None, None, BF16, tgm, 'mvb')
        e_f = spill_epi(None, NTP, None, None, F32, tgm, 'mvf')

        def epi_m(si, ch, pss):
            c0 = si * 256 + ch * 128
            if c0 < 2048:
                r_e(si, ch, pss, c0 // 128, dst=kT_m, Ct=CKm, St=SKm, ckeys=['CKm', 'SKm'])
            elif c0 < 6144:
                e_b(si, ch, pss, dst=vT_m, r0=c0 - 2048, func=None)
            elif c0 < 10240:
                e_f(si, ch, pss, dst=xl_m, r0=c0 - 6144, func=None)
            elif c0 < 12288:
                r_e(si, ch, pss, (c0 - 10240) // 128, dst=qT_m, Ct=CQ, St=SQ, ckeys=['CQ', 'SQ'])
            elif c0 < 16384:
                e_f(si, ch, pss, dst=sg_m, r0=c0 - 12288, func=AF.Silu)
            elif c0 < 20480:
                e_f(si, ch, pss, dst=gg_m, r0=c0 - 16384, func=AF.Gelu)
            elif c0 < 24576:
                e_f(si, ch, pss, dst=sr_m, r0=c0 - 20480, func=AF.Sigmoid)
            else:
                e_f(si, ch, pss, dst=sl_m, r0=c0 - 24576, func=AF.Sigmoid)
        P.proj_fm('min', w_in, D, [[(s * 256, 256)] for s in range(112)], hTm, 'hTm', tgm, epi_m, PB, ps)
    ro.close()
    S.barrier()
    if stop_after <= 3:
        S.finish(); return nc

    with contextlib.ExitStack() as st:
        cz = P.sb(st, "cz", [128, 2, KT]); bat = P.sb(st, "bat", [128, 2, KT]); bxt = P.sb(st, "bxt", [128, 2, KT])
        c5 = P.sb(st, "c5", [128, KT, 5]); cb5 = P.sb(st, "cb5", [128, KT])
        S.dma('sp', cz[:], lru_lamT, writes=['cz'])
        S.dma('sp', bat[:], lru_baT, writes=['bat'])
        S.dma('sp', bxt[:], lru_bxT, writes=['bxt'])
        S.dma('sp', c5[:], conv5T, writes=['c5'])
        S.dma('sp', cb5[:], convbT, writes=['cb5'])
        S.op('act', lambda e: e.activation(out=cz[:], in_=cz[:], func=AF.Exp, scale=-1.0), reads=['cz'], writes=['cz'])
        S.op('act', lambda e: e.activation(out=cz[:], in_=cz[:], func=AF.Ln, bias=1.0), reads=['cz'], writes=['cz'])
        ts(cz[:], cz[:], -8.0, None, ALU.mult, rd=['cz'], wr=['cz'])
        xlat_ = P.sb(st, "xlat_", [128, 2, 2052]); xctx_ = P.sb(st, "xctx_", [128, 2, 260])
        xcl = P.sb(st, "xcl", [128, 2, 2048]); xcc = P.sb(st, "xcc", [128, 2, 256])
        xbl = P.sb(st, "xbl", [128, 2, 2048], BF16); xbc = P.sb(st, "xbc", [128, 2, 256], BF16)
        wf = P.sb(st, "wf", [128, 4, 2, 256]); wbf = P.sb(st, "wbf", [128, 4, 2, 256], BF16)
        aA2 = [P.sb(st, "aA%d" % i, [128, 256 + NTP]) for i in range(2)]; bA2 = [P.sb(st, "bA%d" % i, [128, 256 + NTP]) for i in range(2)]; hA2 = [P.sb(st, "hA%d" % i, [128, 256 + NTP]) for i in range(2)]
        aB2 = [P.sb(st, "aB%d" % i, [128, 2304]) for i in range(2)]; bB2 = [P.sb(st, "bB%d" % i, [128, 2304]) for i in range(2)]; hB2 = [P.sb(st, "hB%d" % i, [128, 2304]) for i in range(2)]
        tqA2 = [P.sb(st, "tqA%d" % i, [128, 256 + NTP]) for i in range(2)]; tqB2 = [P.sb(st, "tqB%d" % i, [128, 2304]) for i in range(2)]
        ylo = [P.sb(st, "ylo%d" % i, [128, NTP]) for i in range(2)]
        S.op('dve', lambda e: e.memset(xlat_[:], 0.0), writes=['xlat_'])
        S.op('dve', lambda e: e.memset(xctx_[:], 0.0), writes=['xctx_'])
        pb = 0
        for n in range(16):
            for i2 in range(2):
                r0 = (2 * n + i2) * 128
                S.dma('pool', xlat_[:, i2, 2:2 + NT], xl_m[r0:r0 + 128, 0:NT], writes=['xlat_'])
                S.dma('pool', xlat_[:, i2, 2 + NT:2 + 2048], xl_p[r0:r0 + 128, 256:NPF], writes=['xlat_'])
                S.dma('pool', xctx_[:, i2, 2:258], xl_p[r0:r0 + 128, 0:256], writes=['xctx_'])
            for z in range(2):
                for gi, wsrc in enumerate((lru_wa, lru_wx)):
                    S.dma('sp', wf[:, z * 2 + gi, :, :], wsrc[z, n].rearrange("(k p) f -> p k f", p=128), writes=['wf'])
            S.op('pool', lambda e: e.tensor_copy(out=wbf[:], in_=wf[:]), reads=['wf'], writes=['wbf'])
            for i2 in range(2):
                kt = 2 * n + i2
                for (src, dst, dstb, L, sk, dk, dbk) in ((xlat_, xcl, xbl, 2048, 'xlat_', 'xcl', 'xbl'), (xctx_, xcc, xbc, 256, 'xctx_', 'xcc', 'xbc')):
                    S.op('dve', lambda e, src=src, dst=dst, L=L, i2=i2, kt=kt: e.tensor_scalar(out=dst[:, i2, :], in0=src[:, i2, 0:L], scalar1=c5[:, kt, 0:1], scalar2=cb5[:, kt:kt + 1], op0=ALU.mult, op1=ALU.add),
                         reads=[sk, 'c5', 'cb5'], writes=[dk])
                    for j in range(1, 5):
                        S.op('dve', lambda e, src=src, dst=dst, L=L, i2=i2, kt=kt, j=j: e.scalar_tensor_tensor(out=dst[:, i2, :], in0=src[:, i2, j:j + L], scalar=c5[:, kt, j:j + 1], in1=dst[:, i2, :], op0=ALU.mult, op1=ALU.add),
                             reads=[sk, 'c5', dk], writes=[dk])
                    S.op('pool', lambda e, dst=dst, dstb=dstb, i2=i2: e.tensor_copy(out=dstb[:, i2, :], in_=dst[:, i2, :]), reads=[dk], writes=[dbk])
            for j2 in range(2):
                kt = 2 * n + j2
                zinfo = []
                aA, bA, hA, tqA, aB, bB, hB, tqB = aA2[j2], bA2[j2], hA2[j2], tqA2[j2], aB2[j2], bB2[j2], hB2[j2], tqB2[j2]
                kA_, kB_ = '%d' % j2, '%d' % j2
                for z in range(2):
                    if z == 0:
                        segs = [(xbc, 'xbc', 0, 256, 0), (xbl, 'xbl', 0, 512, 256), (xbl, 'xbl', 512, 512, 768), (xbl, 'xbl', 1024, 2, 1280)]
                        ab, bb_, tqz, abk, bbk, tqk, L = aA, bA, tqA, ('aA', j2), ('bA', j2), ('tqA', j2), 1282
                    else:
                        segs = [(xbl, 'xbl', q * 512, 512, q * 512) for q in range(4)] + [(xbc, 'xbc', 0, 256, 2048)]
                        ab, bb_, tqz, abk, bbk, tqk, L = aB, bB, tqB, ('aB', j2), ('bB', j2), ('tqB', j2), 2304
                    zinfo.append((ab, bb_, tqz, abk, bbk, tqk, L))
                    for (xb_, xbk, c0, nn, o0) in segs:
                        b_r = pb % 8; b_i = (pb + 1) % 8; pb += 2
                        for gi, bnk in ((0, b_r), (1, b_i)):
                            for i2 in range(2):
                                S.op('pe', lambda e, o_=ps[:, bnk, 0:nn], l=wbf[:, z * 2 + gi, i2, j2 * 128:(j2 + 1) * 128], r=xb_[:, i2, c0:c0 + nn], i2=i2:
                                     e.matmul(o_, lhsT=l, rhs=r, start=(i2 == 0), stop=(i2 == 1)),
                                     reads=['wbf', xbk], writes=[('ps', bnk)], inc=(i2 == 1))
                        S.op('act', lambda e, nn=nn, b_r=b_r, z=z, kt=kt, ab=ab, o0=o0: e.activation(out=ab[:, o0:o0 + nn], in_=ps[:, b_r, 0:nn], func=AF.Sigmoid, bias=bat[:, z, kt:kt + 1]),
                             reads=[('ps', b_r), 'bat'], writes=[abk])
                        S.op('act', lambda e, nn=nn, b_i=b_i, z=z, kt=kt, bb_=bb_, o0=o0: e.activation(out=bb_[:, o0:o0 + nn], in_=ps[:, b_i, 0:nn], func=AF.Sigmoid, bias=bxt[:, z, kt:kt + 1]),
                             reads=[('ps', b_i), 'bxt'], writes=[bbk])
                for z, (ab, bb_, tqz, abk, bbk, tqk, L) in enumerate(zinfo):
                    S.op('act', lambda e, ab=ab, L=L, z=z, kt=kt: e.activation(out=ab[:, 0:L], in_=ab[:, 0:L], func=AF.Exp, scale=cz[:, z, kt:kt + 1]),
                         reads=[abk, 'cz'], writes=[abk])
                    tt(tqz[:, 0:L], ab[:, 0:L], ab[:, 0:L], ALU.mult, rd=[abk], wr=[tqk])
                for z, (ab, bb_, tqz, abk, bbk, tqk, L) in enumerate(zinfo):
                    S.op('act', lambda e, tqz=tqz, L=L: e.activation(out=tqz[:, 0:L], in_=tqz[:, 0:L], func=AF.Sqrt, scale=-1.0, bias=1.0), reads=[tqk], writes=[tqk])
                    tt(bb_[:, 0:L], bb_[:, 0:L], tqz[:, 0:L], ALU.mult, rd=[bbk, tqk], wr=[bbk])
                tt(bA[:, 0:256], bA[:, 0:256], xcc[:, j2, :], ALU.mult, rd=[('bA', j2), 'xcc'], wr=[('bA', j2)])
                tt(bA[:, 256:1282], bA[:, 256:1282], xcl[:, j2, 0:1026], ALU.mult, rd=[('bA', j2), 'xcl'], wr=[('bA', j2)])
                tt(bB[:, 0:2048], bB[:, 0:2048], xcl[:, j2, :], ALU.mult, rd=[('bB', j2), 'xcl'], wr=[('bB', j2)])
                tt(bB[:, 2048:2304], bB[:, 2048:2304], xcc[:, j2, :], ALU.mult, rd=[('bB', j2), 'xcc'], wr=[('bB', j2)])
                LA = 256 + NT
                S.op('dve', lambda e, hA=hA, aA=aA, bA=bA: e.tensor_tensor_scan(out=hA[:, 0:LA], data0=aA[:, 0:LA], data1=bA[:, 0:LA], initial=0.0, op0=ALU.mult, op1=ALU.add),
                     reads=[('aA', j2), ('bA', j2)], writes=[('hA', j2)])
                S.op('dve', lambda e, hB=hB, aB=aB, bB=bB: e.tensor_tensor_scan(out=hB[:, ::-1], data0=aB[:, ::-1], data1=bB[:, ::-1], initial=0.0, op0=ALU.mult, op1=ALU.add),
                     reads=[('aB', j2), ('bB', j2)], writes=[('hB', j2)])
                yo = ylo[kt % 2]; yok = 'ylo%d' % (kt % 2)
                tt(yo[:, 0:NT], hA[:, 256:256 + NT], hB[:, 0:NT], ALU.add, rd=[('hA', j2), ('hB', j2)], wr=[yok])
                S.dma('pool', yl_s[kt * 128:(kt + 1) * 128, 0:NT], yo[:, 0:NT], reads=[yok], writes=['yl_s'])
        S.barrier()
    if stop_after <= 4:
        S.finish(); return nc

    zst = contextlib.ExitStack()
    zT = P.sb(zst, "zT", [128, KT, NTP], BF16)
    with contextlib.ExitStack() as st:
        jpi = P.sb(st, "jpi", [128, 1], I32); jp = P.sb(st, "jp", [128, 1])
        reli = P.sb(st, "reli", [128, 128], I32); rel = P.sb(st, "rel", [128, 128])
        relp = P.sb(st, "relp", [128, 128]); reln = P.sb(st, "reln", [128, 128])
        indA = P.sb(st, "indA", [128, 128]); indB = P.sb(st, "indB", [128, 128]); mtmp = P.sb(st, "mtmp", [128, 128])
        Mall = P.sb(st, "Mall", [128, H, 128])
        ev = P.sb(st, "ev", [128, 8])
        kdA = P.sb(st, "kdA", [128, H]); kdB = P.sb(st, "kdB", [128, H]); qdA = P.sb(st, "qdA", [128, H]); qdB = P.sb(st, "qdB", [128, H])
        cdA = P.sb(st, "cdA", [128, H]); cdB = P.sb(st, "cdB", [128, H]); g1A = P.sb(st, "g1A", [128, H]); g1B = P.sb(st, "g1B", [128, H])
        eA = P.sb(st, "eA", [128, 2]); eB = P.sb(st, "eB", [128, 10])
        wA = P.sb(st, "wA", [128, H, 2]); wB = P.sb(st, "wB", [128, H, 10])
        onesc = P.sb(st, "onesc", [128, 1])
        S.op('pool', lambda e: e.iota(jpi[:], pattern=[[0, 1]], base=0, channel_multiplier=1), writes=['jpi'])
        S.op('pool', lambda e: e.iota(reli[:], pattern=[[1, 128]], base=0, channel_multiplier=-1), writes=['reli'])
        S.op('dve', lambda e: e.tensor_copy(out=jp[:], in_=jpi[:]), reads=['jpi'], writes=['jp'])
        S.op('dve', lambda e: e.tensor_copy(out=rel[:], in_=reli[:]), reads=['reli'], writes=['rel'])
        S.op('dve', lambda e: e.memset(onesc[:], 1.0), writes=['onesc'])
        ts(relp[:], rel[:], 0.0, None, ALU.max, rd=['rel'], wr=['relp'])
        ts(reln[:], rel[:], -1.0, 0.0, ALU.mult, ALU.max, rd=['rel'], wr=['reln'])
        ts(indA[:], rel[:], 0.0, None, ALU.is_ge, rd=['rel'], wr=['indA'])
        ts(indB[:], rel[:], 0.0, None, ALU.is_le, rd=['rel'], wr=['indB'])
        ts(ev[:, 0:1], jp[:], -1.0, 127.0, ALU.mult, ALU.add, rd=['jp'], wr=['ev'])
        S.op('dve', lambda e: e.tensor_copy(out=ev[:, 1:2], in_=jp[:]), reads=['jp'], writes=['ev'])
        ts(ev[:, 2:3], jp[:], 1.0, None, ALU.add, rd=['jp'], wr=['ev'])
        ts(ev[:, 3:4], jp[:], -1.0, 128.0, ALU.mult, ALU.add, rd=['jp'], wr=['ev'])
        S.op('dve', lambda e: e.memset(ev[:, 4:5], 128.0), writes=['ev'])
        S.op('dve', lambda e: e.memset(ev[:, 5:6], 1.0), writes=['ev'])
        for (dst, z, col, nm) in ((kdA, 0, 0, 'kdA'), (kdB, 1, 1, 'kdB'), (qdA, 0, 2, 'qdA'), (qdB, 1, 3, 'qdB'), (cdA, 0, 4, 'cdA'), (cdB, 1, 4, 'cdB'), (g1A, 0, 5, 'g1A'), (g1B, 1, 5, 'g1B')):
            S.op('act', lambda e, dst=dst, z=z, col=col: e.activation(out=dst[:], in_=lg[:, z * H:(z + 1) * H], func=AF.Exp, scale=ev[:, col:col + 1]),
                 reads=['lg', 'ev'], writes=[nm])
        for c in range(10):
            if c < 2:
                ts(eA[:, c:c + 1], jp[:], -1.0, float(255 - c * 128), ALU.mult, ALU.add, rd=['jp'], wr=['eA'])
                ts(eB[:, c:c + 1], jp[:], float(c * 128 + 1023), None, ALU.add, rd=['jp'], wr=['eB'])
            else:
                ts(eB[:, c:c + 1], jp[:], float(c * 128 - 256), None, ALU.add, rd=['jp'], wr=['eB'])
        for h in range(H):
            S.op('act', lambda e, h=h: e.activation(out=wA[:, h, :], in_=eA[:], func=AF.Exp, scale=lg[:, h:h + 1]), reads=['eA', 'lg'], writes=['wA'])
            S.op('act', lambda e, h=h: e.activation(out=wB[:, h, :], in_=eB[:], func=AF.Exp, scale=lg[:, H + h:H + h + 1]), reads=['eB', 'lg'], writes=['wB'])
            S.op('act', lambda e, h=h: e.activation(out=Mall[:, h, :], in_=relp[:], func=AF.Exp, scale=lg[:, h:h + 1]), reads=['relp', 'lg'], writes=['Mall'])
            tt(Mall[:, h, :], Mall[:, h, :], indA[:], ALU.mult, rd=['Mall', 'indA'], wr=['Mall'])
            S.op('act', lambda e, h=h: e.activation(out=mtmp[:], in_=reln[:], func=AF.Exp, scale=lg[:, H + h:H + h + 1]), reads=['reln', 'lg'], writes=['mtmp'])
            tt(mtmp[:], mtmp[:], indB[:], ALU.mult, rd=['mtmp', 'indB'], wr=['mtmp'])
            tt(Mall[:, h, :], Mall[:, h, :], mtmp[:], ALU.add, rd=['Mall', 'mtmp'], wr=['Mall'])

        kTp = P.sb(st, "kTp", [128, NPFP], BF16); vTp = P.sb(st, "vTp", [128, 2, NPFP], BF16)
        kTm = P.sb(st, "kTm", [128, NTP], BF16); qTm = P.sb(st, "qTm", [128, NTP], BF16); vTm = P.sb(st, "vTm", [128, 2, NTP], BF16)
        sgh = P.sb(st, "sgh", [128, 2, NTP])
        kAc = P.sb(st, "kAc", [128, 128], BF16); kBc = P.sb(st, "kBc", [128, 128], BF16); vc = P.sb(st, "vc", [128, 256], BF16)
        kBall = P.sb(st, "kBall", [128, 9, 128], BF16); vtok = P.sb(st, "vtok", [128, 9, 256], BF16)
        SA = P.sb(st, "SA", [128, 256]); SB = P.sb(st, "SB", [128, 256]); SB16 = P.sb(st, "SB16", [128, 256], BF16)
        SAall = P.sb(st, "SAall", [128, 9, 256], BF16)
        PT = P.sb(st, "PT", [128, 128], BF16); osb = P.sb(st, "osb", [128, 256]); on = P.sb(st, "on", [128, 256])
        bst = P.sb(st, "bst", [128, 6]); mv = P.sb(st, "mv", [128, 4])
        psb = ps[:, :, :].bitcast(BF16)

        def tr_bf(bank, off, in_ap, n, rd):
            S.op('pe', lambda e: e.transpose(psb[0:n, bank, off:off + 128], in_ap, identb[:]), reads=rd + ['identb'], writes=[('ps', bank)])
            return psb[0:n, bank, off:off + 128]

        for h in range(H):
            S.dma('pool', kTp[:], kT_p[h * 128:(h + 1) * 128, :], writes=['kTp'])
            S.dma('pool', vTp[:], vT_p[h * 256:(h + 1) * 256, :].rearrange("(e p) t -> p e t", p=128), writes=['vTp'])
            S.dma('pool', kTm[:], kT_m[h * 128:(h + 1) * 128, :], writes=['kTm'])
            S.dma('pool', qTm[:], qT_m[h * 128:(h + 1) * 128, :], writes=['qTm'])
            S.dma('pool', vTm[:], vT_m[h * 256:(h + 1) * 256, :].rearrange("(e p) t -> p e t", p=128), writes=['vTm'])
            S.dma('pool', sgh[:], sg_m[h * 256:(h + 1) * 256, :].rearrange("(e p) t -> p e t", p=128), writes=['sgh'])
            for c in range(10):
                n = min(128, NPF - c * 128)
                pk_ = tr_bf(0, 0, kTp[:, c * 128:c * 128 + n], n, ['kTp'])
                S.op('act', lambda e, n=n, c=c, h=h, pk_=pk_: e.activation(out=kBc[0:n, :], in_=pk_, func=AF.Copy, scale=wB[0:n, h, c:c + 1]), reads=[('ps', 0), 'wB'], writes=['kBc'])
                if c < 2:
                    S.op('dve', lambda e, n=n, c=c, h=h, pk_=pk_: e.tensor_scalar(out=kAc[0:n, :], in0=pk_, scalar1=wA[0:n, h, c:c + 1], scalar2=None, op0=ALU.mult), reads=[('ps', 0), 'wA'], writes=['kAc'])
                for e2 in range(2):
                    tr_bf(1, e2 * 128, vTp[:, e2, c * 128:c * 128 + n], n, ['vTp'])
                S.op('dve', lambda e, n=n: e.tensor_copy(out=vc[0:n, :], in_=psb[0:n, 1, 0:256]), reads=[('ps', 1)], writes=['vc'])
                S.op('pe', lambda e, n=n, c=c: e.matmul(ps[:, 7, 0:256], lhsT=kBc[0:n, :], rhs=vc[0:n, :], start=(c == 0), stop=(c == 9)),
                     reads=['kBc', 'vc'], writes=[('ps', 7)])
                if c < 2:
                    S.op('pe', lambda e, n=n, c=c: e.matmul(ps[:, 6, 0:256], lhsT=kAc[0:n, :], rhs=vc[0:n, :], start=(c == 0), stop=(c == 1)),
                         reads=['kAc', 'vc'], writes=[('ps', 6)])
            S.op('act', lambda e: e.activation(out=SA[:], in_=ps[:, 6, 0:256], func=AF.Copy), reads=[('ps', 6)], writes=['SA'])
            S.op('dve', lambda e: e.tensor_copy(out=SB[:], in_=ps[:, 7, 0:256]), reads=[('ps', 7)], writes=['SB'])
            S.op('act', lambda e: e.activation(out=SB16[:], in_=SB[:], func=AF.Copy), reads=['SB'], writes=['SB16'])
            for c in range(9):
                n = 128 if c < 8 else 1
                pk_ = tr_bf(0, 0, kTm[:, c * 128:c * 128 + n], n, ['kTm'])
                sA = kdA[0:n, h:h + 1] if c < 8 else onesc[0:n, 0:1]
                S.op('act', lambda e, n=n, pk_=pk_, sA=sA: e.activation(out=kAc[0:n, :], in_=pk_, func=AF.Copy, scale=sA), reads=[('ps', 0), 'kdA', 'onesc'], writes=['kAc'])
                S.op('dve', lambda e, n=n, c=c, h=h, pk_=pk_: e.tensor_scalar(out=kBall[0:n, c, :], in0=pk_, scalar1=kdB[0:n, h:h + 1], scalar2=None, op0=ALU.mult), reads=[('ps', 0), 'kdB'], writes=['kBall'])
                for e2 in range(2):
                    tr_bf(1, e2 * 128, vTm[:, e2, c * 128:c * 128 + n], n, ['vTm'])
                S.op('dve', lambda e, n=n, c=c: e.tensor_copy(out=vtok[0:n, c, :], in_=psb[0:n, 1, 0:256]), reads=[('ps', 1)], writes=['vtok'])
                S.op('act', lambda e, c=c: e.activation(out=SAall[:, c, :], in_=SA[:], func=AF.Copy), reads=['SA'], writes=['SAall'])
                if c < 8:
                    S.op('pe', lambda e, n=n, c=c: e.matmul(ps[:, 2, 0:256], lhsT=kAc[0:n, :], rhs=vtok[0:n, c, :], start=True, stop=True),
                         reads=['kAc', 'vtok'], writes=[('ps', 2)])
                    S.op('dve', lambda e, h=h: e.scalar_tensor_tensor(out=SA[:], in0=SA[:], scalar=cdA[:, h:h + 1], in1=ps[:, 2, 0:256], op0=ALU.mult, op1=ALU.add),
                         reads=['SA', 'cdA', ('ps', 2)], writes=['SA'])
            for c in range(8, -1, -1):
                n = 128 if c < 8 else 1
                t0 = c * 128
                S.op('pe', lambda e, n=n, t0=t0: e.matmul(ps[0:n, 3, 0:n], lhsT=kTm[:, t0:t0 + n], rhs=qTm[:, t0:t0 + n], start=True, stop=True),
                     reads=['kTm', 'qTm'], writes=[('ps', 3)])
                tt(PT[0:n, 0:n], ps[0:n, 3, 0:n], Mall[0:n, h, 0:n], ALU.mult, rd=[('ps', 3), 'Mall'], wr=['PT'])
                S.op('pe', lambda e, n=n, c=c: e.matmul(ps[0:n, 4, 0:256], lhsT=PT[0:n, 0:n], rhs=vtok[0:n, c, :], start=True, stop=True),
                     reads=['PT', 'vtok'], writes=[('ps', 4)])
                S.op('pe', lambda e, n=n, c=c, t0=t0: e.matmul(ps[0:n, 5, 0:256], lhsT=qTm[:, t0:t0 + n], rhs=SAall[:, c, :], start=True, stop=True),
                     reads=['qTm', 'SAall'], writes=[('ps', 5)])
                S.op('pe', lambda e, n=n, t0=t0: e.matmul(ps[0:n, 5, 256:512], lhsT=qTm[:, t0:t0 + n], rhs=SB16[:], start=True, stop=True),
                     reads=['qTm', 'SB16'], writes=[('ps', 5)])
                S.op('act', lambda e, n=n: e.activation(out=osb[0:n, :], in_=ps[0:n, 4, 0:256], func=AF.Copy), reads=[('ps', 4)], writes=['osb'])
                sqB = qdB[0:n, h:h + 1] if c < 8 else g1B[0:n, h:h + 1]
                S.op('dve', lambda e, n=n, h=h: e.scalar_tensor_tensor(out=osb[0:n, :], in0=ps[0:n, 5, 0:256], scalar=qdA[0:n, h:h + 1], in1=osb[0:n, :], op0=ALU.mult, op1=ALU.add),
                     reads=[('ps', 5), 'qdA', 'osb'], writes=['osb'])
                S.op('dve', lambda e, n=n, sqB=sqB: e.scalar_tensor_tensor(out=osb[0:n, :], in0=ps[0:n, 5, 256:512], scalar=sqB, in1=osb[0:n, :], op0=ALU.mult, op1=ALU.add),
                     reads=[('ps', 5), 'qdB', 'g1B', 'osb'], writes=['osb'])
                S.op('dve', lambda e, n=n: e.bn_stats(out=bst[0:n, :], in_=osb[0:n, :]), reads=['osb'], writes=['bst'])
                S.op('dve', lambda e, n=n: e.bn_aggr(out=mv[0:n, 0:2], in_=bst[0:n, :]), reads=['bst'], writes=['mv'])
                ts(mv[0:n, 2:3], mv[0:n, 1:2], EPS, None, ALU.add, rd=['mv'], wr=['mv'])
                S.op('act', lambda e, n=n: e.activation(out=mv[0:n, 2:3], in_=mv[0:n, 2:3], func=AF.Sqrt), reads=['mv'], writes=['mv'])
                S.op('dve', lambda e, n=n: e.reciprocal(out=mv[0:n, 3:4], in_=mv[0:n, 2:3]), reads=['mv'], writes=['mv'])
                ts(on[0:n, :], osb[0:n, :], mv[0:n, 0:1], mv[0:n, 3:4], ALU.subtract, ALU.mult, rd=['osb', 'mv'], wr=['on'])
                for e2 in range(2):
                    S.op('pe', lambda e, n=n, e2=e2: e.transpose(ps[:, 2, e2 * 128:e2 * 128 + n], on[0:n, e2 * 128:(e2 + 1) * 128], ident[0:n, 0:n]),
                         reads=['on', 'ident'], writes=[('ps', 2)])
                    tt(zT[:, 2 * h + e2, t0:t0 + n], ps[:, 2, e2 * 128:e2 * 128 + n], sgh[:, e2, t0:t0 + n], ALU.mult, rd=[('ps', 2), 'sgh'], wr=['zT'])
                if c > 0:
                    S.op('pe', lambda e, n=n, c=c: e.matmul(ps[:, 6, 0:256], lhsT=kBall[0:n, c, :], rhs=vtok[0:n, c, :], start=True, stop=True),
                         reads=['kBall', 'vtok'], writes=[('ps', 6)])
                    scB = cdB[:, h:h + 1] if c < 8 else g1B[:, h:h + 1]
                    S.op('dve', lambda e, scB=scB: e.scalar_tensor_tensor(out=SB[:], in0=SB[:], scalar=scB, in1=ps[:, 6, 0:256], op0=ALU.mult, op1=ALU.add),
                         reads=['SB', 'cdB', 'g1B', ('ps', 6)], writes=['SB'])
                    S.op('act', lambda e: e.activation(out=SB16[:], in_=SB[:], func=AF.Copy), reads=['SB'], writes=['SB16'])
            if ada_gen is not None:
                for _ in range(4):
                    next(ada_gen, None)
        if ada_gen is not None:
            for _ in ada_gen:
                pass
        S.barrier()
    late_tables()
    if 'zT' in P.dbg:
        dz = nc.dram_tensor("dbg_zT", [128, KT, NTP], BF16, kind="ExternalOutput").ap()
        S.dma('sp', dz, zT[:], reads=['zT']); S.barrier()
    if stop_after <= 5:
        S.finish(); return nc

    def gated_epi(st, tag, gate_src, add_src, dst_dram, dst_sb, dst_key):
        gb = [P.sb(st, "%s_g%d" % (tag, i), [128, NTP]) for i in range(1)]
        ab_ = [P.sb(st, "%s_a%d" % (tag, i), [128, NTP]) for i in range(1)] if add_src is not None else None
        ob = [P.sb(st, "%s_ob%d" % (tag, i), [128, NTP]) for i in range(1)]
        state = {'i': 0}

        def epi(si, ch, pss):
            i = state['i']; state['i'] += 1
            j = 0
            cidx = si * 2 + ch
            r0 = cidx * 128
            gk, ak, ok = "%s_g%d" % (tag, j), "%s_a%d" % (tag, j), "%s_ob%d" % (tag, j)
            S.dma('pool', gb[j][:, 0:NT], gate_src[r0:r0 + 128, 0:NT], writes=[gk])
            if add_src is not None:
                S.dma('pool', ab_[j][:, 0:NT], add_src[r0:r0 + 128, 0:NT], writes=[ak])
            for (t0, n), (pap, pk) in zip(tgm, pss):
                tt(ob[j][:, t0:t0 + n], pap, gb[j][:, t0:t0 + n], ALU.mult, rd=[pk, gk], wr=[ok])
            if add_src is not None:
                tt(ob[j][:, 0:NT], ob[j][:, 0:NT], ab_[j][:, 0:NT], ALU.add, rd=[ok, ak], wr=[ok])
            if dst_dram is not None:
                S.dma('pool', dst_dram[r0:r0 + 128, 0:NT], ob[j][:, 0:NT], reads=[ok], writes=[('dram', id(dst_dram))])
            else:
                S.op('act', lambda e: e.activation(out=dst_sb[:, cidx, 0:NT], in_=ob[j][:, 0:NT], func=AF.Copy), reads=[ok], writes=[dst_key])
        return epi

    sq_slabs = [[(s_ * 256, 256)] for s_ in range(16)]
    with contextlib.ExitStack() as st:
        P.proj_fm('ro', w_ret_o, D, sq_slabs, zT, 'zT', tgm, gated_epi(st, 'ro', sr_m, None, yr_s, None, None), PB, ps)
    zst.close()
    S.barrier()
    yst = contextlib.ExitStack()
    yT = P.sb(yst, "yT", [128, KT, NTP], BF16)
    with contextlib.ExitStack() as st:
        uT = P.sb(st, "uT", [128, KT, NTP], BF16)
        with contextlib.ExitStack() as st2:
            g_ = [P.sb(st2, "u_g%d" % i, [128, NTP]) for i in range(2)]
            y_ = [P.sb(st2, "u_y%d" % i, [128, NTP]) for i in range(2)]
            for kt in range(KT):
                j = kt % 2
                S.dma('pool', g_[j][:, 0:NT], gg_m[kt * 128:(kt + 1) * 128, 0:NT], writes=['u_g%d' % j])
                S.dma('pool', y_[j][:, 0:NT], yl_s[kt * 128:(kt + 1) * 128, 0:NT], writes=['u_y%d' % j])
                tt(uT[:, kt, 0:NT], g_[j][:, 0:NT], y_[j][:, 0:NT], ALU.mult, rd=['u_g%d' % j, 'u_y%d' % j], wr=['uT'])
            S.barrier()
        P.proj_fm('lo', w_lru_o, D, sq_slabs, uT, 'uT', tgm, gated_epi(st, 'lo', sl_m, yr_s, None, yT, 'yT'), PB, ps)
    S.barrier()
    if 'yT' in P.dbg:
        dz = nc.dram_tensor("dbg_yT", [128, KT, NTP], BF16, kind="ExternalOutput").ap()
        S.dma('sp', dz, yT[:], reads=['yT']); S.barrier()
    if stop_after <= 6:
        S.finish(); return nc
    with contextlib.ExitStack() as st:
        ssacc = P.sb(st, "ssacc", [128, NTP])
        S.op('dve', lambda e: e.memset(ssacc[:], 0.0), writes=['ssacc'])
        xb2 = [P.sb(st, "wo_x%d" % i, [128, NTP]) for i in range(2)]
        ob2 = [P.sb(st, "wo_o%d" % i, [128, NTP]) for i in range(2)]
        sq2 = P.sb(st, "wo_sq", [128, NTP])
        stt = {'i': 0}

        def epi_wo(si, ch, pss):
            i = stt['i']; stt['i'] += 1
            j = i % 2
            cidx = si * 2 + ch
            r0 = cidx * 128
            xk, ok = 'wo_x%d' % j, 'wo_o%d' % j
            S.dma('pool', xb2[j][:, 0:NT], xT_m[r0:r0 + 128, 0:NT], writes=[xk])
            for (t0, n), (pap, pk) in zip(tgm, pss):
                S.op('dve', lambda e, t0=t0, n=n, pap=pap: e.scalar_tensor_tensor(out=ob2[j][:, t0:t0 + n], in0=pap, scalar=G1[:, cidx:cidx + 1], in1=xb2[j][:, t0:t0 + n], op0=ALU.mult, op1=ALU.add),
                     reads=[pk, 'G1', xk], writes=[ok])
            S.op('act', lambda e: e.activation(out=sq2[:, 0:NT], in_=ob2[j][:, 0:NT], func=AF.Square), reads=[ok], writes=['wo_sq'])
            tt(ssacc[:, 0:NT], ssacc[:, 0:NT], sq2[:, 0:NT], ALU.add, rd=['ssacc', 'wo_sq'], wr=['ssacc'])
            S.dma('pool', xlatT[r0:r0 + 128, 0:NT], ob2[j][:, 0:NT], reads=[ok], writes=['xlatT'])
        P.proj_fm('wo', w_out, D, sq_slabs, yT, 'yT', tgm, epi_wo, PB, ps)
        if stop_after <= 6.5:
            S.finish(); return nc
        for gi, (t0, n) in enumerate(tgm):
            S.op('pe', lambda e, gi=gi, t0=t0, n=n: e.matmul(ps[:, gi, 0:n], lhsT=ones[:], rhs=ssacc[:, t0:t0 + n], start=True, stop=True),
                 reads=['ones', 'ssacc'], writes=[('ps', gi)])
            S.op('act', lambda e, gi=gi, t0=t0, n=n: e.activation(out=rstd2[:, t0:t0 + n], in_=ps[:, gi, 0:n], func=AF.Sqrt, scale=1.0 / D, bias=epsc[:, 0:1]),
                 reads=[('ps', gi), 'epsc'], writes=['rstd2'])
        S.op('dve', lambda e: e.reciprocal(out=rstd2[:, 0:NT], in_=rstd2[:, 0:NT]), reads=['rstd2'], writes=['rstd2'])
        S.barrier()
        if stop_after <= 6.7:
            dz = nc.dram_tensor('dbg_rstd2', [128, NTP], F32, kind='ExternalOutput').ap()
            S.dma('sp', dz, rstd2[:], reads=['rstd2']); S.finish(); return nc
    yst.close()
    S.barrier()
    hst = contextlib.ExitStack()
    h2T = P.sb(hst, "h2T", [128, KT, NTP], BF16)
    with contextlib.ExitStack() as st:
        xb3 = [P.sb(st, "n2_x%d" % i, [128, NTP]) for i in range(2)]
        for kt in range(KT):
            j = kt % 2
            xk = 'n2_x%d' % j
            S.dma('pool', xb3[j][:, 0:NT], xlatT[kt * 128:(kt + 1) * 128, 0:NT], writes=[xk])
            tt(xb3[j][:, 0:NT], xb3[j][:, 0:NT], rstd2[:, 0:NT], ALU.mult, rd=[xk, 'rstd2'], wr=[xk])
            S.op('act', lambda e, j=j, kt=kt: e.activation(out=h2T[:, kt, 0:NT], in_=xb3[j][:, 0:NT], func=AF.Identity, scale=A2[:, kt:kt + 1], bias=B2[:, kt:kt + 1]),
                 reads=[xk, 'A2', 'B2'], writes=['h2T'])
        S.barrier()
    if 'h2T' in P.dbg:
        dz = nc.dram_tensor("dbg_h2T", [128, KT, NTP], BF16, kind="ExternalOutput").ap()
        S.dma('sp', dz, h2T[:], reads=['h2T']); S.barrier()
    if stop_after <= 7:
        S.finish(); return nc
    with contextlib.ExitStack() as st:
        fcw = P.sb(st, "fcw", [128, 2 * FFT, 3]); fcb = P.sb(st, "fcb", [128, 2 * FFT])
        S.dma('sp', fcw[:], fconvT, writes=['fcw'])
        S.dma('sp', fcb[:], fconvbT, writes=['fcb'])
        ub = [P.sb(st, "ub%d" % i, [128, NTP + 2]) for i in range(2)]
        acc = [P.sb(st, "acc%d" % i, [128, 1024]) for i in range(2)]
        sgt = [P.sb(st, "sgt%d" % i, [128, 1024]) for i in range(2)]
        ao = [P.sb(st, "ao%d" % i, [128, 1024], BF16) for i in range(2)]
        for i in range(2):
            S.op('dve', lambda e, i=i: e.memset(ub[i][:], 0.0), writes=['ub%d' % i])
        stt9 = {'i': 0}

        def epi_up(si, ch, pss):
            i = stt9['i']; stt9['i'] += 1
            j = i % 2
            isv = ch >= 2
            gch = 2 * si + (ch % 2)
            cw = gch + (FFT if isv else 0)
            ubk, acck = 'ub%d' % j, 'acc%d' % j
            for (t0, n), (pap, pk) in zip(tgm, pss):
                S.op('act', lambda e, t0=t0, n=n, pap=pap: e.activation(out=ub[j][:, 1 + t0:1 + t0 + n], in_=pap, func=AF.Copy), reads=[pk], writes=[ubk])
            S.op('dve', lambda e: e.tensor_scalar(out=acc[j][:], in0=ub[j][:, 0:1024], scalar1=fcw[:, cw, 0:1], scalar2=fcb[:, cw:cw + 1], op0=ALU.mult, op1=ALU.add),
                 reads=[ubk, 'fcw', 'fcb'], writes=[acck])
            for tap in (1, 2):
                S.op('dve', lambda e, tap=tap: e.scalar_tensor_tensor(out=acc[j][:], in0=ub[j][:, tap:tap + 1024], scalar=fcw[:, cw, tap:tap + 1], in1=acc[j][:], op0=ALU.mult, op1=ALU.add),
                     reads=[ubk, 'fcw', acck], writes=[acck])
            if not isv:
                S.op('act', lambda e: e.activation(out=sgt[ch][:], in_=acc[j][:], func=AF.Silu), reads=[acck], writes=['sgt%d' % ch])
            else:
                c2 = ch - 2
                tt(ao[c2][:], acc[j][:], sgt[c2][:], ALU.mult, rd=[acck, 'sgt%d' % c2], wr=['ao%d' % c2])
                S.dma('pool', a_scr[gch * 128:(gch + 1) * 128, :], ao[c2][:], reads=['ao%d' % c2], writes=['a_scr'])
        up_slabs = [[(s_ * 256, 256), (FF + s_ * 256, 256)] for s_ in range(43)]
        P.proj_fm('up', w_up, D, up_slabs, h2T, 'h2T', tgm, epi_up, PB, ps)
    hst.close()
    S.barrier()
    if 'a_scr' in P.dbg and stop_after <= 8:
        S.finish(); return nc
    with contextlib.ExitStack() as st:
        fnT_ = P.sb(st, "fnT_", [128, KT])
        S.dma('sp', fnT_[:], fnormT, writes=['fnT_'])
        KQ = 4
        wst = [P.sb(st, "dn_ws%d" % i, [128, KQ, 384]) for i in range(2)]
        wbb = [P.sb(st, "dn_wb%d" % i, [128, KQ, 384], BF16) for i in range(2)]
        abuf = [P.sb(st, "dn_a%d" % i, [128, KQ, 1024], BF16) for i in range(2)]
        xl4 = [P.sb(st, "dn_x%d" % i, [128, 1024]) for i in range(2)]
        xo4 = [P.sb(st, "dn_o%d" % i, [128, 1024]) for i in range(2)]
        xf4 = [P.sb(st, "dn_f%d" % i, [128, 1024]) for i in range(2)]
        sq4 = P.sb(st, "dn_sq", [128, 1024]); ssa = P.sb(st, "dn_ssa", [128, 1024])
        ot = [P.sb(st, "dn_ot%d" % i, [128, 8, 128]) for i in range(2)]
        rs3 = P.sb(st, "dn_rs3", [128, 8])
        big = [P.sb(st, "dn_big%d" % i, [128, D]) for i in range(2)]
        S.op('dve', lambda e: e.memset(ssa[:], 0.0), writes=['dn_ssa'])
        y_v = y_out.rearrange("(i p) f -> p i f", p=128)
        grp = 0
        ei = 0
        for cg in range(11):
            c0 = cg * 384
            ncol = min(384, D - c0)
            nch = ncol // 128
            nkq = (FFT + KQ - 1) // KQ
            for kq in range(nkq):
                k0 = kq * KQ
                kn = min(KQ, FFT - k0)
                j = grp % 2; grp += 1
                S.dma('sp', wst[j][:, 0:kn, 0:ncol], w_down[k0 * 128:(k0 + kn) * 128, c0:c0 + ncol].rearrange("(k p) c -> p k c", p=128), writes=['dn_ws%d' % j])
                S.dma('pool', abuf[j][:, 0:kn, :], a_scr[k0 * 128:(k0 + kn) * 128, :].rearrange("(k p) t -> p k t", p=128), reads=['a_scr'], writes=['dn_a%d' % j])
                if grp % 2:
                    S.op('act', lambda e, j=j, kn=kn, ncol=ncol: e.activation(out=wbb[j][:, 0:kn, 0:ncol], in_=wst[j][:, 0:kn, 0:ncol], func=AF.Copy), reads=['dn_ws%d' % j], writes=['dn_wb%d' % j])
                else:
                    S.op('dve', lambda e, j=j, kn=kn, ncol=ncol: e.tensor_copy(out=wbb[j][:, 0:kn, 0:ncol], in_=wst[j][:, 0:kn, 0:ncol]), reads=['dn_ws%d' % j], writes=['dn_wb%d' % j])
                for q in range(kn):
                    kt = k0 + q
                    for ch in range(nch):
                        for g in range(2):
                            b = ch * 2 + g
                            last = (q == kn - 1 and ch == nch - 1 and g == 1)
                            S.op('pe', lambda e, b=b, j=j, q=q, ch=ch, g=g, kt=kt: e.matmul(ps[:, b, :], lhsT=wbb[j][:, q, ch * 128:(ch + 1) * 128], rhs=abuf[j][:, q, g * 512:(g + 1) * 512], start=(kt == 0), stop=(kt == FFT - 1)),
                                 reads=['dn_wb%d' % j, 'dn_a%d' % j], writes=[('ps', b)], inc=(kt == FFT - 1 or last))
            for ch in range(nch):
                cidx = cg * 3 + ch
                j = ei % 2; ei += 1
                xk, ok, fk, otk = 'dn_x%d' % j, 'dn_o%d' % j, 'dn_f%d' % j, 'dn_ot%d' % j
                S.dma('pool', xl4[j][:], xlatT[cidx * 128:(cidx + 1) * 128, 0:1024], writes=[xk])
                for g in range(2):
                    b = ch * 2 + g
                    S.op('dve', lambda e, b=b, g=g, j=j, cidx=cidx: e.scalar_tensor_tensor(out=xo4[j][:, g * 512:(g + 1) * 512], in0=ps[:, b, :], scalar=G2[:, cidx:cidx + 1], in1=xl4[j][:, g * 512:(g + 1) * 512], op0=ALU.mult, op1=ALU.add),
                         reads=[('ps', b), 'G2', xk], writes=[ok])
                S.op('act', lambda e, j=j: e.activation(out=sq4[:], in_=xo4[j][:], func=AF.Square), reads=[ok], writes=['dn_sq'])
                tt(ssa[:], ssa[:], sq4[:], ALU.add, rd=['dn_ssa', 'dn_sq'], wr=['dn_ssa'])
                S.op('act', lambda e, j=j, cidx=cidx: e.activation(out=xf4[j][:], in_=xo4[j][:], func=AF.Identity, scale=fnT_[:, cidx:cidx + 1]), reads=[ok, 'fnT_'], writes=[fk])
                for tb in range(2):
                    bk = 6 + tb
                    for q in range(4):
                        ti_ = tb * 4 + q
                        S.op('pe', lambda e, bk=bk, q=q, ti_=ti_, j=j: e.transpose(ps[:, bk, q * 128:(q + 1) * 128], xf4[j][:, ti_ * 128:(ti_ + 1) * 128], ident[:]),
                             reads=[fk, 'ident'], writes=[('ps', bk)], inc=(q == 3))
                    P.evac(ot[j][:, tb * 4:(tb + 1) * 4, :], ps[:, bk, :].rearrange("p (q f) -> p q f", q=4), [('ps', bk)], [otk])
                S.dma('pool', y_v[:, :, cidx * 128:(cidx + 1) * 128], ot[j][:], reads=[otk], writes=['y_out'])
        for i in range(8):
            S.op('pe', lambda e, i=i: e.matmul(ps[:, 0, i:i + 1], lhsT=ssa[:, i * 128:(i + 1) * 128], rhs=ones[:, 0:1], start=True, stop=True),
                 reads=['dn_ssa', 'ones'], writes=[('ps', 0)])
        S.op('act', lambda e: e.activation(out=rs3[:], in_=ps[:, 0, 0:8], func=AF.Sqrt, scale=1.0 / D, bias=epsc[:, 0:1]), reads=[('ps', 0), 'epsc'], writes=['dn_rs3'])
        S.op('dve', lambda e: e.reciprocal(out=rs3[:], in_=rs3[:]), reads=['dn_rs3'], writes=['dn_rs3'])
        for i in range(8):
            j = i % 2
            bk_ = 'dn_big%d' % j
            S.dma('sp', big[j][:], y_out[i * 128:(i + 1) * 128, :], reads=['y_out'], writes=[bk_])
            S.op('act', lambda e, i=i, j=j: e.activation(out=big[j][:], in_=big[j][:], func=AF.Identity, scale=rs3[:, i:i + 1]), reads=[bk_, 'dn_rs3'], writes=[bk_])
            S.dma('sp', y_out[i * 128:(i + 1) * 128, :], big[j][:], reads=[bk_], writes=['y_out2'])
        S.barrier()
    S.finish()
    return nc


def _pT(v, nchunk):
    return np.ascontiguousarray(np.asarray(v, np.float32).reshape(nchunk, 128).T)


def prep_shared(inp):
    sh = {}
    sh["w_ada"] = np.ascontiguousarray(inp["w_ada"][0])
    sh["b_adaT"] = _pT(inp["b_ada"][0], 192)
    sh["norm1T"] = _pT(inp["norm1"][0], KT)
    sh["norm2T"] = _pT(inp["norm2"][0], KT)
    sh["fnormT"] = _pT(inp["final_norm"], KT)
    sh["w_in"] = np.ascontiguousarray(inp["w_in"][0])
    sh["convbT"] = _pT(inp["lru_conv_b"][0], KT)
    sh["w_ret_o"] = np.ascontiguousarray(inp["w_ret_o"][0])
    sh["w_lru_o"] = np.ascontiguousarray(inp["w_lru_o"][0])
    sh["w_out"] = np.ascontiguousarray(inp["w_out"][0])
    sh["w_up"] = np.ascontiguousarray(inp["w_up"][0])
    sh["fconvbT"] = _pT(inp["ffn_conv_b"][0], 2 * FFT)
    sh["w_down"] = np.ascontiguousarray(inp["w_down"][0])
    sh["ident"] = np.eye(128, dtype=np.float32)
    r = np.zeros((128, 128), np.float32)
    for d in range(64):
        r[d + 64, d] = -1.0
        r[d, d + 64] = 1.0
    sh["rmat"] = r
    return sh


def prep_core(inp, sh, b, half):
    flip = (half == 1)
    m = dict(sh)
    xb = inp["x"][b]
    cx = inp["ctx"][b]
    if flip:
        xb = xb[::-1]
        cx = cx[::-1]
    xm = np.zeros((1152, D), np.float32)
    xm[:NT] = xb[:NT]
    xp = np.zeros((NPFP, D), np.float32)
    xp[:256] = cx
    xp[256:NPF] = xb[NT:2048]
    m["x_main"] = xm
    m["x_pre"] = xp
    cT = np.stack([_pT(inp["c"][b], KT), _pT(inp["c_ctx"], KT)], axis=-1)
    m["cT"] = np.ascontiguousarray(cT)
    zs = [1, 0] if flip else [0, 1]
    m["retlog"] = np.ascontiguousarray(np.broadcast_to(inp["ret_decay_logit"][0][zs].reshape(1, 2 * H), (128, 2 * H)))
    cw = inp["lru_conv_w"][0]
    c5 = np.zeros((5, D), np.float32)
    if flip:
        c5[1:5] = cw[::-1]
    else:
        c5[0:4] = cw
    m["conv5T"] = np.ascontiguousarray(np.stack([_pT(c5[j], KT) for j in range(5)], axis=-1))
    m["lru_wa"] = np.ascontiguousarray(inp["lru_wa"][0][zs])
    m["lru_wx"] = np.ascontiguousarray(inp["lru_wx"][0][zs])
    for nm, key in (("lru_baT", "lru_ba"), ("lru_bxT", "lru_bx"), ("lru_lamT", "lru_lambda")):
        a = inp[key][0][zs]
        m[nm] = np.ascontiguousarray(np.stack([_pT(a[0], KT), _pT(a[1], KT)], axis=1))
    fw = inp["ffn_conv_w"][0]
    if flip:
        fw = fw[::-1]
    m["fconvT"] = np.ascontiguousarray(np.stack([_pT(fw[j], 2 * FFT) for j in range(3)], axis=-1))
    d = np.arange(128)
    rc = np.zeros((128, 5), np.float32)
    rc[:, 0] = -1.0 if flip else 1.0
    rc[:, 1] = 31.0 if flip else 0.0
    rc[:, 2] = 63.0 if flip else 0.0
    rc[:, 3] = d % 32
    rc[:, 4] = ((d % 64) < 32).astype(np.float32)
    m["ropec"] = rc
    return m


def kernel(**inputs):
    inp = {k: np.asarray(v) for k, v in inputs.items()}
    nc = build()
    sh = prep_shared(inp)
    maps = []
    for core in range(8):
        maps.append(prep_core(inp, sh, core // 2, core % 2))
    res = run_bass_kernel_spmd(nc, maps, core_ids=list(range(8)))
    out = np.empty((4, 2048, D), np.float32)
    for core in range(8):
        b, half = core // 2, core % 2
        y = np.asarray(res.results[core]["y_out"], np.float32)
        if half == 0:
            out[b, 0:1024] = y
        else:
            out[b, 1024:2048] = y[::-1]
    return out
```

```python
import contextlib
import numpy as np
import concourse.bass as bass
import concourse.mybir as mybir
from concourse.bass_utils import run_bass_kernel_spmd

F32 = mybir.dt.float32
BF16 = mybir.dt.bfloat16
I32 = mybir.dt.int32
AF = mybir.ActivationFunctionType
ALU = mybir.AluOpType
AX = mybir.AxisListType

D = 4096
KT = 32
NT = 1025
NTP = 1026
NPF = 1279
NPFP = 1280
H = 16
FF = 11008
FFT = 86
EPS = 1e-6
TWO_PI = 6.283185307179586


class Sched:
    ENG = ('pe', 'act', 'dve', 'pool', 'sp')

    def __init__(self, nc, stack, n_dma_sems=(('sp', 14), ('pool', 14))):
        self.nc = nc
        self.sem = {}
        for e in self.ENG:
            self.sem['c_' + e] = stack.enter_context(nc.semaphore('c_' + e))
        self.cnt = {e: 0 for e in self.ENG}
        self.pending = {e: False for e in self.ENG}
        self.seen = {e: {} for e in self.ENG}
        self.prog = {e: [] for e in self.ENG}
        self.last_write = {}
        self.readers = {}
        self.dma_sems = {}
        self.dma_uses = {}
        self.dma_rr = {}
        for q, n in n_dma_sems:
            names = []
            for i in range(n):
                nm = 'd_%s_%d' % (q, i)
                self.sem[nm] = stack.enter_context(nc.semaphore(nm))
                self.dma_uses[nm] = 0
                names.append(nm)
            self.dma_sems[q] = names
            self.dma_rr[q] = 0

    def _deps(self, eng, reads, writes):
        deps = {}

        def add(tok):
            if tok is None:
                return
            s, v = tok
            if deps.get(s, 0) < v:
                deps[s] = v
        for r in reads:
            add(self.last_write.get(r))
            if isinstance(r, tuple) and r[0] == 'ps':
                for s, v in self.readers.get(r, {}).items():
                    if s != 'c_' + eng:
                        add((s, v))
        for r in writes:
            add(self.last_write.get(r))
            for s, v in self.readers.get(r, {}).items():
                add((s, v))
        waits = []
        for s, v in deps.items():
            if eng == 'pe' and s == 'c_pe':
                continue
            if self.seen[eng].get(s, 0) < v:
                self.seen[eng][s] = v
                waits.append((s, v))
        return waits

    def _commit(self, tok, reads, writes):
        for r in writes:
            self.last_write[r] = tok
            self.readers[r] = {}
        for r in reads:
            d = self.readers.setdefault(r, {})
            if d.get(tok[0], 0) < tok[1]:
                d[tok[0]] = tok[1]

    def op(self, eng, emit, reads=(), writes=(), inc=True):
        waits = self._deps(eng, reads, writes)
        if inc:
            self.cnt[eng] += 1
            self.pending[eng] = False
            tok = ('c_' + eng, self.cnt[eng])
        else:
            self.pending[eng] = True
            tok = ('c_' + eng, self.cnt[eng] + 1)
        self.prog[eng].append((waits, emit, ('c_' + eng, 1) if inc else None))
        self._commit(tok, reads, writes)
        return tok

    def dma(self, q, out, in_, reads=(), writes=(), **kw):
        names = self.dma_sems[q]
        nm = names[self.dma_rr[q] % len(names)]
        self.dma_rr[q] += 1
        waits = self._deps(q, reads, writes)
        prev = self.dma_uses[nm] * 16
        if prev and self.seen[q].get(nm, 0) < prev:
            self.seen[q][nm] = prev
            waits.append((nm, prev))
        self.dma_uses[nm] += 1
        tok = (nm, self.dma_uses[nm] * 16)

        def emit(e, out=out, in_=in_, kw=kw):
            return e.dma_start(out=out, in_=in_, **kw)
        self.prog[q].append((waits, emit, (nm, 16)))
        self._commit(tok, reads, writes)
        return tok

    def barrier(self):
        toks = []
        for e in ('pe', 'act', 'dve', 'pool'):
            if self.pending[e]:
                raise RuntimeError('pending non-inc op on %s at barrier' % e)
            if self.cnt[e]:
                toks.append(('c_' + e, self.cnt[e]))
        for nm, u in self.dma_uses.items():
            if u:
                toks.append((nm, u * 16))
        for e in self.ENG:
            waits = []
            for s, v in toks:
                if self.seen[e].get(s, 0) < v:
                    self.seen[e][s] = v
                    waits.append((s, v))
            if waits:
                self.prog[e].append((waits, None, None))
        self.last_write = {}
        self.readers = {}

    def finish(self):
        nc = self.nc
        self.barrier()
        sem = self.sem
        prog = self.prog
        with nc.Block() as block:
            def run(e):
                def body(engobj):
                    for waits, emit, inc in prog[e]:
                        for s, v in waits:
                            engobj.wait_ge(sem[s], v)
                        if emit is None:
                            continue
                        ins = emit(engobj)
                        if inc is not None:
                            ins.then_inc(sem[inc[0]], inc[1])
                return body
            block.tensor(run('pe'))
            block.scalar(run('act'))
            block.vector(run('dve'))
            block.gpsimd(run('pool'))
            block.sync(run('sp'))


class Prog:
    def __init__(self, dbg=None, only_inputs=None):
        self.dbg = dbg or {}
        self.only_inputs = only_inputs
        self.nc = bass.Bass("TRN2", target_bir_lowering=False)
        self.outer = contextlib.ExitStack()
        self.S = Sched(self.nc, self.outer)
        self.cast_rr = 0
        self.evac_rr = 0

    def din(self, name, shape, dt=F32):
        if self.only_inputs is not None and name not in self.only_inputs:
            return None
        return self.nc.dram_tensor(name, list(shape), dt, kind="ExternalInput").ap()

    def dscr(self, name, shape, dt=F32):
        kind = "ExternalOutput" if name in self.dbg else "Internal"
        return self.nc.dram_tensor(name, list(shape), dt, kind=kind).ap()

    def sb(self, st, name, shape, dt=F32):
        self._uid = getattr(self, "_uid", 0) + 1
        return st.enter_context(self.nc.sbuf_tensor("s%d_%s" % (self._uid, name), list(shape), dt))

    def proj_fm(self, *a, **kw):
        for _ in self.proj_gen(*a, **kw):
            pass

    def proj_gen(self, tag, w, K, col_segs, act, act_key, tok_groups, epilogue, banks, ps, nstg=3, flush=False):
        S = self.S
        nc = self.nc
        kt_n = K // 128
        SW = sum(n for _, n in col_segs[0])
        KQ = 4
        with contextlib.ExitStack() as st:
            stg = [self.sb(st, '%s_stg%d' % (tag, i), [128, KQ, SW], F32) for i in range(nstg)]
            slab = [self.sb(st, '%s_slab%d' % (tag, i), [128, kt_n, SW], BF16) for i in range(2)]
            stg_i = 0
            pending_epi = None
            bank_rr = 0
            nb = len(banks)
            stt_ = {'stg_i': 0, 'bank_rr': 0}

            def load(si):
                segs = col_segs[si]
                sl = slab[si % 2]
                slk = '%s_slab%d' % (tag, si % 2)
                kq0 = 0
                while kq0 < kt_n:
                    kq = min(KQ, kt_n - kq0)
                    sg = stg[stt_['stg_i'] % nstg]
                    sgk = '%s_stg%d' % (tag, stt_['stg_i'] % nstg)
                    stt_['stg_i'] += 1
                    off = 0
                    for (c0, ncol) in segs:
                        src = w[kq0 * 128:(kq0 + kq) * 128, c0:c0 + ncol].rearrange("(k p) c -> p k c", p=128)
                        S.dma('sp', sg[:, 0:kq, off:off + ncol], src, writes=[sgk])
                        off += ncol
                    wk = [(slk, kq0 + j) for j in range(kq)]
                    if self.cast_rr % 2 == 0:
                        S.op('act', lambda e, o=sl[:, kq0:kq0 + kq, :], i=sg[:, 0:kq, :]: e.activation(out=o, in_=i, func=AF.Copy),
                             reads=[sgk], writes=wk)
                    else:
                        S.op('dve', lambda e, o=sl[:, kq0:kq0 + kq, :], i=sg[:, 0:kq, :]: e.tensor_copy(out=o, in_=i),
                             reads=[sgk], writes=wk)
                    self.cast_rr += 1
                    kq0 += kq

            if flush:
                load(0)
            for si, segs in enumerate(col_segs):
                sl = slab[si % 2]
                slk = '%s_slab%d' % (tag, si % 2)
                if flush:
                    if si + 1 < len(col_segs):
                        load(si + 1)
                else:
                    load(si)
                for ch in range(SW // 128):
                    pss = []
                    for (t0, n) in tok_groups:
                        b = banks[stt_['bank_rr'] % nb]
                        stt_['bank_rr'] += 1
                        pk = ('ps', b)
                        for kt in range(kt_n):
                            S.op('pe', lambda e, o=ps[:, b, 0:n], l=sl[:, kt, ch * 128:(ch + 1) * 128], r=act[:, kt, t0:t0 + n], a=(kt == 0), z=(kt == kt_n - 1):
                                 e.matmul(o, lhsT=l, rhs=r, start=a, stop=z),
                                 reads=[(slk, kt), act_key], writes=[pk], inc=(kt == kt_n - 1))
                        pss.append((ps[:, b, 0:n], pk))
                    if pending_epi is not None:
                        pending_epi()
                    pending_epi = (lambda si=si, ch=ch, pss=pss: epilogue(si, ch, pss))
                if flush and pending_epi is not None:
                    pending_epi()
                    pending_epi = None
                yield si
            if pending_epi is not None:
                pending_epi()
            S.barrier()

    def evac(self, out, in_, reads, writes, func=None, **kw):
        S = self.S
        if func is None and not kw:
            self.evac_rr += 1
            if self.evac_rr % 2:
                return S.op('dve', lambda e: e.tensor_copy(out=out, in_=in_), reads=reads, writes=writes)
            func = AF.Copy
        return S.op('act', lambda e: e.activation(out=out, in_=in_, func=func, **kw), reads=reads, writes=writes)


def build(dbg=None, stop_after=99, only_inputs=None):
    P = Prog(dbg, only_inputs)
    nc, S = P.nc, P.S
    o = P.outer
    x_main = P.din("x_main", [1152, D])
    x_pre = P.din("x_pre", [NPFP, D])
    cT = P.din("cT", [128, KT, 2])
    w_ada = P.din("w_ada", [D, 6 * D])
    b_adaT = P.din("b_adaT", [128, 192])
    norm1T = P.din("norm1T", [128, KT])
    norm2T = P.din("norm2T", [128, KT])
    fnormT = P.din("fnormT", [128, KT])
    w_in = P.din("w_in", [D, 28672])
    retlog = P.din("retlog", [128, 2 * H])
    conv5T = P.din("conv5T", [128, KT, 5])
    convbT = P.din("convbT", [128, KT])
    lru_wa = P.din("lru_wa", [2, 16, 256, 256])
    lru_wx = P.din("lru_wx", [2, 16, 256, 256])
    lru_baT = P.din("lru_baT", [128, 2, KT])
    lru_bxT = P.din("lru_bxT", [128, 2, KT])
    lru_lamT = P.din("lru_lamT", [128, 2, KT])
    w_ret_o = P.din("w_ret_o", [D, D])
    w_lru_o = P.din("w_lru_o", [D, D])
    w_out = P.din("w_out", [D, D])
    w_up = P.din("w_up", [D, 2 * FF])
    fconvT = P.din("fconvT", [128, 2 * FFT, 3])
    fconvbT = P.din("fconvbT", [128, 2 * FFT])
    w_down = P.din("w_down", [FF, D])
    ident_in = P.din("ident", [128, 128])
    rmat_in = P.din("rmat", [128, 128])
    ropec = P.din("ropec", [128, 5])
    y_out = nc.dram_tensor("y_out", [1024, D], F32, kind="ExternalOutput").ap()

    xT_m = P.dscr("xT_m", [D, NTP])
    kT_m = P.dscr("kT_m", [2048, NTP], BF16)
    qT_m = P.dscr("qT_m", [2048, NTP], BF16)
    vT_m = P.dscr("vT_m", [D, NTP], BF16)
    xl_m = P.dscr("xl_m", [D, NTP])
    sg_m = P.dscr("sg_m", [D, NTP])
    gg_m = P.dscr("gg_m", [D, NTP])
    sr_m = P.dscr("sr_m", [D, NTP])
    sl_m = P.dscr("sl_m", [D, NTP])
    kT_p = P.dscr("kT_p", [2048, NPFP], BF16)
    vT_p = P.dscr("vT_p", [D, NPFP], BF16)
    xl_p = P.dscr("xl_p", [D, NPFP])
    yl_s = P.dscr("yl_s", [D, NTP])
    yr_s = P.dscr("yr_s", [D, NTP])
    xlatT = P.dscr("xlatT", [D, NTP])
    a_scr = P.dscr("a_scr", [FF, 1024], BF16)
    st_scr = P.dscr("st_scr", [2, H, 128, 256])
    hl_scr = P.dscr("hl_scr", [2, D])

    T = lambda name, shape, dt=F32: P.sb(o, name, shape, dt)
    ps = o.enter_context(nc.psum_tensor("ps", [128, 8, 512], F32))

    ident = T("ident", [128, 128])
    identb = T("identb", [128, 128], BF16)
    rmat = T("rmatb", [128, 128], BF16)
    rtmp = T("rtmp", [128, 128])
    ones = T("ones", [128, 128])
    modT = T("modT", [128, 192, 2])
    A1l = T("A1l", [128, KT]); B1l = T("B1l", [128, KT])
    A1c = T("A1c", [128, KT]); B1c = T("B1c", [128, KT])
    A2 = T("A2", [128, KT]); B2 = T("B2", [128, KT])
    G1 = T("G1", [128, KT]); G2 = T("G2", [128, KT])
    n1 = T("n1", [128, KT]); n2 = T("n2", [128, KT]); bada = T("bada", [128, 192])
    lg = T("lg", [128, 2 * H])
    rc = T("rc", [128, 5])
    epsc = T('epsc', [128, 1])
    rstd2 = T('rstd2', [128, NTP])
    S.op('dve', lambda e: e.memset(epsc[:], EPS), writes=['epsc'])
    S.dma('sp', ident[:], ident_in, writes=['ident'])
    S.dma('sp', rtmp[:], rmat_in, writes=['rtmp'])
    S.dma('sp', n1[:], norm1T, writes=['n1'])
    S.dma('sp', n2[:], norm2T, writes=['n2'])
    S.dma('sp', bada[:], b_adaT, writes=['bada'])
    S.dma('sp', lg[:], retlog, writes=['lg'])
    S.dma('sp', rc[:], ropec, writes=['rc'])
    S.op('dve', lambda e: e.tensor_copy(out=identb[:], in_=ident[:]), reads=['ident'], writes=['identb'])
    S.op('dve', lambda e: e.tensor_copy(out=rmat[:], in_=rtmp[:]), reads=['rtmp'], writes=['rmat'])
    S.op('dve', lambda e: e.memset(ones[:], 1.0), writes=['ones'])
    S.op('act', lambda e: e.activation(out=lg[:], in_=lg[:], func=AF.Exp, scale=-1.0), reads=['lg'], writes=['lg'])
    S.op('act', lambda e: e.activation(out=lg[:], in_=lg[:], func=AF.Ln, bias=1.0), reads=['lg'], writes=['lg'])
    S.op('dve', lambda e: e.tensor_scalar(out=lg[:], in0=lg[:], scalar1=-1.0, scalar2=None, op0=ALU.mult), reads=['lg'], writes=['lg'])

    if 'inj_mod' in P.dbg:
        S.dma('sp', modT[:], P.din('modT_in', [128, 192, 2]), writes=['modT'])
    cf = T("cf", [128, KT, 2]); cb = T("cb", [128, KT, 2], BF16)
    ada_gen = None
    if 'inj_mod' not in P.dbg:
        S.dma('sp', cf[:], cT, writes=['cf'])
        S.op('act', lambda e: e.activation(out=cb[:], in_=cf[:], func=AF.Silu), reads=['cf'], writes=['cb'])

        def mk_epi_ada(slabs):
            def epi_ada(si, ch, pss):
                j = slabs[si][0][0] // 128 + ch
                (p0, pk), = pss
                S.op('dve', lambda e: e.tensor_scalar(out=modT[:, j, :], in0=p0, scalar1=bada[:, j:j + 1], scalar2=None, op0=ALU.add),
                     reads=[pk, 'bada'], writes=['modT'])
            return epi_ada
        early = [[(s_ * 512, 512)] for s_ in range(16)]
        late = [[(8192 + s_ * 256, 256)] for s_ in range(64)]
        P.proj_fm('ada', w_ada, D, early, cb, 'cb', [(0, 2)], mk_epi_ada(early), [0, 1, 2, 3], ps, nstg=8)
        ada_gen = P.proj_gen('ada2', w_ada, D, late, cb, 'cb', [(0, 2)], mk_epi_ada(late), [3, 4], ps, nstg=3, flush=True)
    def ts(out, in0, s1, s2, op0, op1=None, rd=(), wr=()):
        if op1 is None:
            S.op('dve', lambda e: e.tensor_scalar(out=out, in0=in0, scalar1=s1, scalar2=None, op0=op0), reads=rd, writes=wr)
        else:
            S.op('dve', lambda e: e.tensor_scalar(out=out, in0=in0, scalar1=s1, scalar2=s2, op0=op0, op1=op1), reads=rd, writes=wr)

    def tt(out, a, b, op, rd=(), wr=()):
        S.op('dve', lambda e: e.tensor_tensor(out=out, in0=a, in1=b, op=op), reads=rd, writes=wr)
    ts(A1l[:], modT[:, 32:64, 0], 1.0, None, ALU.add, rd=['modT'], wr=['A1l'])
    tt(A1l[:], A1l[:], n1[:], ALU.mult, rd=['A1l', 'n1'], wr=['A1l'])
    ts(A1c[:], modT[:, 32:64, 1], 1.0, None, ALU.add, rd=['modT'], wr=['A1c'])
    tt(A1c[:], A1c[:], n1[:], ALU.mult, rd=['A1c', 'n1'], wr=['A1c'])
    for dst, lo, col, nm in ((B1l, 0, 0, 'B1l'), (B1c, 0, 1, 'B1c')):
        S.op('dve', lambda e, dst=dst, lo=lo, col=col: e.tensor_copy(out=dst[:], in_=modT[:, lo:lo + 32, col]), reads=['modT'], writes=[nm])

    def late_tables():
        ts(A2[:], modT[:, 128:160, 0], 1.0, None, ALU.add, rd=['modT'], wr=['A2'])
        tt(A2[:], A2[:], n2[:], ALU.mult, rd=['A2', 'n2'], wr=['A2'])
        for dst, lo, col, nm in ((B2, 96, 0, 'B2'), (G1, 64, 0, 'G1'), (G2, 160, 0, 'G2')):
            S.op('dve', lambda e, dst=dst, lo=lo, col=col: e.tensor_copy(out=dst[:], in_=modT[:, lo:lo + 32, col]), reads=['modT'], writes=[nm])
        S.barrier()
    S.barrier()
    if 'modT' in P.dbg:
        dbg_mod = nc.dram_tensor("dbg_modT", [128, 192, 2], F32, kind="ExternalOutput").ap()
        S.dma('sp', dbg_mod, modT[:], reads=['modT'])
    if stop_after <= 0:
        S.finish(); return nc

    ro = contextlib.ExitStack()
    CQ = P.sb(ro, "CQ", [128, NTP]); SQ = P.sb(ro, "SQ", [128, NTP])
    CKm = P.sb(ro, "CKm", [128, NTP]); SKm = P.sb(ro, "SKm", [128, NTP])
    CKp = P.sb(ro, "CKp", [128, NPFP]); SKp = P.sb(ro, "SKp", [128, NPFP])
    with contextlib.ExitStack() as st:
        NL = 2048
        ri = P.sb(st, "ri", [128, NL], I32); ci = P.sb(st, "ci", [128, NL], I32)
        rf = P.sb(st, "rf", [128, NL]); cfl = P.sb(st, "cfl", [128, NL])
        ang = P.sb(st, "ang", [128, NL]); u = P.sb(st, "u", [128, NL]); ui = P.sb(st, "ui", [128, NL], I32)
        uf = P.sb(st, "uf", [128, NL]); m1 = P.sb(st, "m1", [128, NL])
        cosf = P.sb(st, "cosf", [128, NL]); sinf = P.sb(st, "sinf", [128, NL])
        invf = P.sb(st, "invf", [128, 1])
        S.op('pool', lambda e: e.iota(ri[:], pattern=[[1, 32], [0, 64]], base=0, channel_multiplier=0), writes=['ri'])
        S.op('pool', lambda e: e.iota(ci[:], pattern=[[0, 32], [1, 64]], base=0, channel_multiplier=0), writes=['ci'])
        S.op('dve', lambda e: e.tensor_copy(out=rf[:], in_=ri[:]), reads=['ri'], writes=['rf'])
        S.op('dve', lambda e: e.tensor_copy(out=cfl[:], in_=ci[:]), reads=['ci'], writes=['cfl'])
        ts(rf[:], rf[:], rc[:, 0:1], rc[:, 1:2], ALU.mult, ALU.add, rd=['rf', 'rc'], wr=['rf'])
        ts(cfl[:], cfl[:], rc[:, 0:1], rc[:, 2:3], ALU.mult, ALU.add, rd=['cfl', 'rc'], wr=['cfl'])
        tt(rf[:], rf[:], cfl[:], ALU.subtract, rd=['rf', 'cfl'], wr=['rf'])
        S.op('dve', lambda e: e.scalar_tensor_tensor(out=ang[:], in0=rf[:], scalar=rc[:, 4:5], in1=cfl[:], op0=ALU.mult, op1=ALU.add),
             reads=['rf', 'cfl', 'rc'], writes=['ang'])
        S.op('act', lambda e: e.activation(out=invf[:], in_=rc[:, 3:4], func=AF.Exp, scale=-float(np.log(10000.0) / 32.0)), reads=['rc'], writes=['invf'])
        ts(ang[:], ang[:], invf[:, 0:1], 1.0 / TWO_PI, ALU.mult, ALU.mult, rd=['ang', 'invf'], wr=['ang'])
        for (dst, shift, nm) in ((sinf, 0.0, 'sinf'), (cosf, 0.25, 'cosf')):
            ts(u[:], ang[:], shift, None, ALU.add, rd=['ang'], wr=['u'])
            S.op('dve', lambda e: e.tensor_copy(out=ui[:], in_=u[:]), reads=['u'], writes=['ui'])
            S.op('dve', lambda e: e.tensor_copy(out=uf[:], in_=ui[:]), reads=['ui'], writes=['uf'])
            tt(u[:], u[:], uf[:], ALU.subtract, rd=['u', 'uf'], wr=['u'])
            ts(m1[:], u[:], 0.5, None, ALU.is_gt, rd=['u'], wr=['m1'])
            tt(u[:], u[:], m1[:], ALU.subtract, rd=['u', 'm1'], wr=['u'])
            ts(m1[:], u[:], -0.5, None, ALU.is_lt, rd=['u'], wr=['m1'])
            tt(u[:], u[:], m1[:], ALU.add, rd=['u', 'm1'], wr=['u'])
            S.op('act', lambda e, dst=dst: e.activation(out=dst[:], in_=u[:], func=AF.Sin, scale=TWO_PI), reads=['u'], writes=[nm])
        sc = float(128 ** -0.5)
        S.op('dve', lambda e: e.tensor_copy(out=CQ[:, 0:NT], in_=cosf[:, 0:NT]), reads=['cosf'], writes=['CQ'])
        S.op('dve', lambda e: e.tensor_copy(out=SQ[:, 0:NT], in_=sinf[:, 0:NT]), reads=['sinf'], writes=['SQ'])
        ts(CKm[:, 0:NT], cosf[:, 0:NT], sc, None, ALU.mult, rd=['cosf'], wr=['CKm'])
        ts(SKm[:, 0:NT], sinf[:, 0:NT], sc, None, ALU.mult, rd=['sinf'], wr=['SKm'])
        S.op('dve', lambda e: e.memset(CKp[:], sc), writes=['CKp'])
        S.op('dve', lambda e: e.memset(SKp[:], 0.0), writes=['SKp'])
        ts(CKp[:, 256:NPF], cosf[:, 1025:2048], sc, None, ALU.mult, rd=['cosf', 'CKp'], wr=['CKp'])
        ts(SKp[:, 256:NPF], sinf[:, 1025:2048], sc, None, ALU.mult, rd=['sinf', 'SKp'], wr=['SKp'])
        S.barrier()
        if 'rope' in P.dbg:
            d1 = nc.dram_tensor("dbg_cos", [128, NL], F32, kind="ExternalOutput").ap()
            d2 = nc.dram_tensor("dbg_sin", [128, NL], F32, kind="ExternalOutput").ap()
            S.dma('sp', d1, cosf[:], reads=['cosf']); S.dma('sp', d2, sinf[:], reads=['sinf'])
            S.barrier()
    if stop_after <= 1:
        S.finish(); return nc


    def norm_tm(st, x_dram, ntile, hT, hkey, AB_of_tile, xT_spill):
        xt = [P.sb(st, "xt%d" % i, [128, D], F32) for i in range(2)]
        xs = [P.sb(st, "xs%d" % i, [128, D], F32) for i in range(2)]
        ss = P.sb(st, "ss", [128, 4]); xtb = None
        if xT_spill is not None:
            xtb = [P.sb(st, "xtb%d" % i, [128, KT, 128], F32) for i in range(2)]
        bank = 0
        for i in range(ntile):
            x_, xs_ = xt[i % 2], xs[i % 2]
            xk, xsk = 'xt%d' % (i % 2), 'xs%d' % (i % 2)
            S.dma('sp', x_[:], x_dram[i * 128:(i + 1) * 128, :], writes=[xk])
            S.op('act', lambda e, x_=x_, xs_=xs_: e.activation(out=xs_[:], in_=x_[:], func=AF.Square, accum_out=ss[:, 0:1]), reads=[xk], writes=[xsk, 'ss'])
            ts(ss[:, 1:2], ss[:, 0:1], 1.0 / D, EPS, ALU.mult, ALU.add, rd=['ss'], wr=['ss'])
            S.op('act', lambda e: e.activation(out=ss[:, 2:3], in_=ss[:, 1:2], func=AF.Sqrt), reads=['ss'], writes=['ss'])
            S.op('dve', lambda e: e.reciprocal(out=ss[:, 3:4], in_=ss[:, 2:3]), reads=['ss'], writes=['ss'])
            S.op('act', lambda e, x_=x_, xs_=xs_: e.activation(out=xs_[:], in_=x_[:], func=AF.Identity, scale=ss[:, 3:4]), reads=[xk, 'ss'], writes=[xsk])
            A, B, abk = AB_of_tile(i)
            for k4 in range(KT // 4):
                b = bank % 8; bank += 1
                for q in range(4):
                    kt = k4 * 4 + q
                    S.op('pe', lambda e, o_=ps[:, b, q * 128:(q + 1) * 128], i_=xs_[:, kt * 128:(kt + 1) * 128]: e.transpose(o_, i_, ident[:]),
                         reads=[xsk, 'ident'], writes=[('ps', b)], inc=(q == 3))
                nc_ = min(128, hT.shape[2] - i * 128)
                for q in range(4):
                    kt = k4 * 4 + q
                    S.op('act', lambda e, o_=hT[:, kt, i * 128:i * 128 + nc_], i_=ps[:, b, q * 128:q * 128 + nc_], A=A, B=B, kt=kt:
                         e.activation(out=o_, in_=i_, func=AF.Identity, scale=A[:, kt:kt + 1], bias=B[:, kt:kt + 1]),
                         reads=[('ps', b)] + abk, writes=[hkey])
            if xT_spill is not None:
                xb_ = xtb[i % 2]; xbk = 'xtb%d' % (i % 2)
                for k4 in range(KT // 4):
                    b = bank % 8; bank += 1
                    for q in range(4):
                        kt = k4 * 4 + q
                        S.op('pe', lambda e, o_=ps[:, b, q * 128:(q + 1) * 128], i_=x_[:, kt * 128:(kt + 1) * 128]: e.transpose(o_, i_, ident[:]),
                             reads=[xk, 'ident'], writes=[('ps', b)], inc=(q == 3))
                    S.op('dve', lambda e, o_=xb_[:, k4 * 4:(k4 + 1) * 4, :], i_=ps[:, b, :].rearrange("p (k t) -> p k t", k=4): e.tensor_copy(out=o_, in_=i_),
                         reads=[('ps', b)], writes=[xbk])
                n = min(128, NTP - i * 128)
                S.dma('sp', xT_spill[:, i * 128:i * 128 + n].rearrange("(k p) t -> p k t", p=128), xb_[:, :, 0:n], reads=[xbk], writes=['xT_spill'])
        S.barrier()

    def spill_epi(dst, width, row0_of, func, out_dt, tok_groups, nbuf_tag):
        bufs = [P.sb(cur_st[0], "%s_o%d" % (nbuf_tag, i), [128, width], out_dt) for i in range(2)]
        state = {'i': 0}

        def epi(si, ch, pss, dst=dst, r0=None, func=func):
            i = state['i']; state['i'] += 1
            bt = bufs[i % 2]; bk = "%s_o%d" % (nbuf_tag, i % 2)
            for (t0, n), (pap, pk) in zip(tok_groups, pss):
                P.evac(bt[:, t0:t0 + n], pap, [pk], [bk], func=func)
            if r0 is None:
                r0 = row0_of(si, ch)
            ntok = tok_groups[-1][0] + tok_groups[-1][1]
            S.dma('pool', dst[r0:r0 + 128, 0:ntok], bt[:, 0:ntok], reads=[bk], writes=[('dram', id(dst))])
        return epi

    cur_st = [None]

    def rope_epi(dst, Ct, St, ckeys, width, tok_groups, tag):
        kb = [P.sb(cur_st[0], "%s_kb%d" % (tag, i), [128, width], BF16) for i in range(1)]
        t1 = [P.sb(cur_st[0], "%s_t1%d" % (tag, i), [128, width], F32) for i in range(1)]
        t2 = [P.sb(cur_st[0], "%s_t2%d" % (tag, i), [128, width], F32) for i in range(1)]
        ob = [P.sb(cur_st[0], "%s_ob%d" % (tag, i), [128, width], BF16) for i in range(1)]
        state = {'i': 0}

        def epi(si, ch, pss, head, dst=dst, Ct=Ct, St=St, ckeys=ckeys):
            i = state['i']; state['i'] += 1
            j = 0
            kbk, t1k, t2k, obk = "%s_kb%d" % (tag, j), "%s_t1%d" % (tag, j), "%s_t2%d" % (tag, j), "%s_ob%d" % (tag, j)
            for gi, ((t0, n), (pap, pk)) in enumerate(zip(tok_groups, pss)):
                S.op('act', lambda e, o_=kb[j][:, t0:t0 + n], i_=pap: e.activation(out=o_, in_=i_, func=AF.Copy), reads=[pk], writes=[kbk])
                tt(t1[j][:, t0:t0 + n], pap, Ct[:, t0:t0 + n], ALU.mult, rd=[pk] + ckeys, wr=[t1k])
                rb = 6 + (gi % 2)
                import os
                if os.environ.get('ROPEDBG') == 'noR':
                    tt(t2[j][:, t0:t0 + n], pap, St[:, t0:t0 + n], ALU.mult, rd=[pk] + ckeys, wr=[t2k])
                else:
                    S.op('pe', lambda e, o_=ps[:, rb, 0:n], r_=kb[j][:, t0:t0 + n]: e.matmul(o_, lhsT=rmat[:], rhs=r_, start=True, stop=True),
                         reads=[kbk, 'rmat'], writes=[('ps', rb)])
                    tt(t2[j][:, t0:t0 + n], ps[:, rb, 0:n], St[:, t0:t0 + n], ALU.mult, rd=[('ps', rb)] + ckeys, wr=[t2k])
                tt(ob[j][:, t0:t0 + n], t1[j][:, t0:t0 + n], t2[j][:, t0:t0 + n], ALU.add, rd=[t1k, t2k], wr=[obk])
            ntok = tok_groups[-1][0] + tok_groups[-1][1]
            S.dma('pool', dst[head * 128:(head + 1) * 128, 0:ntok], ob[j][:, 0:ntok], reads=[obk], writes=[('dram', id(dst))])
        return epi

    PB = [0, 1, 2, 3, 4, 5]
    with contextlib.ExitStack() as st:
        cur_st[0] = st
        hTp = P.sb(st, "hTp", [128, KT, NPFP], BF16)
        with contextlib.ExitStack() as st2:
            norm_tm(st2, x_pre, 10, hTp, 'hTp', lambda i: (A1c, B1c, ['A1c', 'B1c']) if i < 2 else (A1l, B1l, ['A1l', 'B1l']), None)
        if 'hTp' in P.dbg:
            dz = nc.dram_tensor('dbg_hTp', [128, KT, NPFP], BF16, kind='ExternalOutput').ap()
            S.dma('sp', dz, hTp[:], reads=['hTp']); S.barrier()
            if stop_after <= 1.5:
                S.finish(); return nc
        tgp = [(0, 428), (428, 426), (854, 425)]
        pin_slabs = [[(s_ * 256, 256)] for s_ in range(40)]
        if 'few' in P.dbg:
            import os; pin_slabs = [pin_slabs[int(i)] for i in os.environ.get('FEW', '0,1,2,8,9,24,25').split(',')]
        e_k = rope_epi(kT_p, CKp, SKp, ['CKp', 'SKp'], NPFP, tgp, 'pk')
        e_v = spill_epi(vT_p, NPFP, lambda si, ch: pin_slabs[si][0][0] + ch * 128 - 2048, None, BF16, tgp, 'pv')
        e_x = spill_epi(xl_p, NPFP, lambda si, ch: pin_slabs[si][0][0] + ch * 128 - 6144, None, F32, tgp, 'px')

        def epi_p(si, ch, pss):
            c0 = pin_slabs[si][0][0] + ch * 128
            if c0 < 2048:
                e_k(si, ch, pss, c0 // 128)
            elif c0 < 6144:
                e_v(si, ch, pss)
            else:
                e_x(si, ch, pss)
        P.proj_fm('pin', w_in, D, pin_slabs, hTp, 'hTp', tgp, epi_p, PB, ps)
    if stop_after <= 2:
        S.finish(); return nc

    tgm = [(0, 342), (342, 342), (684, 341)]
    with contextlib.ExitStack() as st:
        cur_st[0] = st
        hTm = P.sb(st, "hTm", [128, KT, NTP], BF16)
        with contextlib.ExitStack() as st2:
            norm_tm(st2, x_main, 9, hTm, 'hTm', lambda i: (A1l, B1l, ['A1l', 'B1l']), xT_m)
        r_e = rope_epi(None, None, None, None, NTP, tgm, 'mk')
        e_b = spill_epi(None, NTP, None, None, BF16, tgm, 'mvb')
        e_f = spill_epi(None, NTP, None, None, F32, tgm, 'mvf')

        def epi_m(si, ch, pss):
            c0 = si * 256 + ch * 128
            if c0 < 2048:
                r_e(si, ch, pss, c0 // 128, dst=kT_m, Ct=CKm, St=SKm, ckeys=['CKm', 'SKm'])
            elif c0 < 6144:
                e_b(si, ch, pss, dst=vT_m, r0=c0 - 2048, func=None)
            elif c0 < 10240:
                e_f(si, ch, pss, dst=xl_m, r0=c0 - 6144, func=None)
            elif c0 < 12288:
                r_e(si, ch, pss, (c0 - 10240) // 128, dst=qT_m, Ct=CQ, St=SQ, ckeys=['CQ', 'SQ'])
            elif c0 < 16384:
                e_f(si, ch, pss, dst=sg_m, r0=c0 - 12288, func=AF.Silu)
            elif c0 < 20480:
                e_f(si, ch, pss, dst=gg_m, r0=c0 - 16384, func=AF.Gelu)
            elif c0 < 24576:
                e_f(si, ch, pss, dst=sr_m, r0=c0 - 20480, func=AF.Sigmoid)
            else:
                e_f(si, ch, pss, dst=sl_m, r0=c0 - 24576, func=AF.Sigmoid)
        P.proj_fm('min', w_in, D, [[(s * 256, 256)] for s in range(112)], hTm, 'hTm', tgm, epi_m, PB, ps)
    ro.close()
    S.barrier()
    if stop_after <= 3:
        S.finish(); return nc

    with contextlib.ExitStack() as st:
        cz = P.sb(st, "cz", [128, 2, KT]); bat = P.sb(st, "bat", [128, 2, KT]); bxt = P.sb(st, "bxt", [128, 2, KT])
        c5 = P.sb(st, "c5", [128, KT, 5]); cb5 = P.sb(st, "cb5", [128, KT])
        S.dma('sp', cz[:], lru_lamT, writes=['cz'])
        S.dma('sp', bat[:], lru_baT, writes=['bat'])
        S.dma('sp', bxt[:], lru_bxT, writes=['bxt'])
        S.dma('sp', c5[:], conv5T, writes=['c5'])
        S.dma('sp', cb5[:], convbT, writes=['cb5'])
        S.op('act', lambda e: e.activation(out=cz[:], in_=cz[:], func=AF.Exp, scale=-1.0), reads=['cz'], writes=['cz'])
        S.op('act', lambda e: e.activation(out=cz[:], in_=cz[:], func=AF.Ln, bias=1.0), reads=['cz'], writes=['cz'])
        ts(cz[:], cz[:], -8.0, None, ALU.mult, rd=['cz'], wr=['cz'])
        xlat_ = P.sb(st, "xlat_", [128, 2, 2052]); xctx_ = P.sb(st, "xctx_", [128, 2, 260])
        xcl = P.sb(st, "xcl", [128, 2, 2048]); xcc = P.sb(st, "xcc", [128, 2, 256])
        xbl = P.sb(st, "xbl", [128, 2, 2048], BF16); xbc = P.sb(st, "xbc", [128, 2, 256], BF16)
        wf = P.sb(st, "wf", [128, 4, 2, 256]); wbf = P.sb(st, "wbf", [128, 4, 2, 256], BF16)
        aA2 = [P.sb(st, "aA%d" % i, [128, 256 + NTP]) for i in range(2)]; bA2 = [P.sb(st, "bA%d" % i, [128, 256 + NTP]) for i in range(2)]; hA2 = [P.sb(st, "hA%d" % i, [128, 256 + NTP]) for i in range(2)]
        aB2 = [P.sb(st, "aB%d" % i, [128, 2304]) for i in range(2)]; bB2 = [P.sb(st, "bB%d" % i, [128, 2304]) for i in range(2)]; hB2 = [P.sb(st, "hB%d" % i, [128, 2304]) for i in range(2)]
        tqA2 = [P.sb(st, "tqA%d" % i, [128, 256 + NTP]) for i in range(2)]; tqB2 = [P.sb(st, "tqB%d" % i, [128, 2304]) for i in range(2)]
        ylo = [P.sb(st, "ylo%d" % i, [128, NTP]) for i in range(2)]
        S.op('dve', lambda e: e.memset(xlat_[:], 0.0), writes=['xlat_'])
        S.op('dve', lambda e: e.memset(xctx_[:], 0.0), writes=['xctx_'])
        pb = 0
        for n in range(16):
            for i2 in range(2):
                r0 = (2 * n + i2) * 128
                S.dma('pool', xlat_[:, i2, 2:2 + NT], xl_m[r0:r0 + 128, 0:NT], writes=['xlat_'])
                S.dma('pool', xlat_[:, i2, 2 + NT:2 + 2048], xl_p[r0:r0 + 128, 256:NPF], writes=['xlat_'])
                S.dma('pool', xctx_[:, i2, 2:258], xl_p[r0:r0 + 128, 0:256], writes=['xctx_'])
            for z in range(2):
                for gi, wsrc in enumerate((lru_wa, lru_wx)):
                    S.dma('sp', wf[:, z * 2 + gi, :, :], wsrc[z, n].rearrange("(k p) f -> p k f", p=128), writes=['wf'])
            S.op('pool', lambda e: e.tensor_copy(out=wbf[:], in_=wf[:]), reads=['wf'], writes=['wbf'])
            for i2 in range(2):
                kt = 2 * n + i2
                for (src, dst, dstb, L, sk, dk, dbk) in ((xlat_, xcl, xbl, 2048, 'xlat_', 'xcl', 'xbl'), (xctx_, xcc, xbc, 256, 'xctx_', 'xcc', 'xbc')):
                    S.op('dve', lambda e, src=src, dst=dst, L=L, i2=i2, kt=kt: e.tensor_scalar(out=dst[:, i2, :], in0=src[:, i2, 0:L], scalar1=c5[:, kt, 0:1], scalar2=cb5[:, kt:kt + 1], op0=ALU.mult, op1=ALU.add),
                         reads=[sk, 'c5', 'cb5'], writes=[dk])
                    for j in range(1, 5):
                        S.op('dve', lambda e, src=src, dst=dst, L=L, i2=i2, kt=kt, j=j: e.scalar_tensor_tensor(out=dst[:, i2, :], in0=src[:, i2, j:j + L], scalar=c5[:, kt, j:j + 1], in1=dst[:, i2, :], op0=ALU.mult, op1=ALU.add),
                             reads=[sk, 'c5', dk], writes=[dk])
                    S.op('pool', lambda e, dst=dst, dstb=dstb, i2=i2: e.tensor_copy(out=dstb[:, i2, :], in_=dst[:, i2, :]), reads=[dk], writes=[dbk])
            for j2 in range(2):
                kt = 2 * n + j2
                zinfo = []
                aA, bA, hA, tqA, aB, bB, hB, tqB = aA2[j2], bA2[j2], hA2[j2], tqA2[j2], aB2[j2], bB2[j2], hB2[j2], tqB2[j2]
                kA_, kB_ = '%d' % j2, '%d' % j2
                for z in range(2):
                    if z == 0:
                        segs = [(xbc, 'xbc', 0, 256, 0), (xbl, 'xbl', 0, 512, 256), (xbl, 'xbl', 512, 512, 768), (xbl, 'xbl', 1024, 2, 1280)]
                        ab, bb_, tqz, abk, bbk, tqk, L = aA, bA, tqA, ('aA', j2), ('bA', j2), ('tqA', j2), 1282
                    else:
                        segs = [(xbl, 'xbl', q * 512, 512, q * 512) for q in range(4)] + [(xbc, 'xbc', 0, 256, 2048)]
                        ab, bb_, tqz, abk, bbk, tqk, L = aB, bB, tqB, ('aB', j2), ('bB', j2), ('tqB', j2), 2304
                    zinfo.append((ab, bb_, tqz, abk, bbk, tqk, L))
                    for (xb_, xbk, c0, nn, o0) in segs:
                        b_r = pb % 8; b_i = (pb + 1) % 8; pb += 2
                        for gi, bnk in ((0, b_r), (1, b_i)):
                            for i2 in range(2):
                                S.op('pe', lambda e, o_=ps[:, bnk, 0:nn], l=wbf[:, z * 2 + gi, i2, j2 * 128:(j2 + 1) * 128], r=xb_[:, i2, c0:c0 + nn], i2=i2:
                                     e.matmul(o_, lhsT=l, rhs=r, start=(i2 == 0), stop=(i2 == 1)),
                                     reads=['wbf', xbk], writes=[('ps', bnk)], inc=(i2 == 1))
                        S.op('act', lambda e, nn=nn, b_r=b_r, z=z, kt=kt, ab=ab, o0=o0: e.activation(out=ab[:, o0:o0 + nn], in_=ps[:, b_r, 0:nn], func=AF.Sigmoid, bias=bat[:, z, kt:kt + 1]),
                             reads=[('ps', b_r), 'bat'], writes=[abk])
                        S.op('act', lambda e, nn=nn, b_i=b_i, z=z, kt=kt, bb_=bb_, o0=o0: e.activation(out=bb_[:, o0:o0 + nn], in_=ps[:, b_i, 0:nn], func=AF.Sigmoid, bias=bxt[:, z, kt:kt + 1]),
                             reads=[('ps', b_i), 'bxt'], writes=[bbk])
                for z, (ab, bb_, tqz, abk, bbk, tqk, L) in enumerate(zinfo):
                    S.op('act', lambda e, ab=ab, L=L, z=z, kt=kt: e.activation(out=ab[:, 0:L], in_=ab[:, 0:L], func=AF.Exp, scale=cz[:, z, kt:kt + 1]),
                         reads=[abk, 'cz'], writes=[abk])
                    tt(tqz[:, 0:L], ab[:, 0:L], ab[:, 0:L], ALU.mult, rd=[abk], wr=[tqk])
                for z, (ab, bb_, tqz, abk, bbk, tqk, L) in enumerate(zinfo):
                    S.op('act', lambda e, tqz=tqz, L=L: e.activation(out=tqz[:, 0:L], in_=tqz[:, 0:L], func=AF.Sqrt, scale=-1.0, bias=1.0), reads=[tqk], writes=[tqk])
                    tt(bb_[:, 0:L], bb_[:, 0:L], tqz[:, 0:L], ALU.mult, rd=[bbk, tqk], wr=[bbk])
                tt(bA[:, 0:256], bA[:, 0:256], xcc[:, j2, :], ALU.mult, rd=[('bA', j2), 'xcc'], wr=[('bA', j2)])
                tt(bA[:, 256:1282], bA[:, 256:1282], xcl[:, j2, 0:1026], ALU.mult, rd=[('bA', j2), 'xcl'], wr=[('bA', j2)])
                tt(bB[:, 0:2048], bB[:, 0:2048], xcl[:, j2, :], ALU.mult, rd=[('bB', j2), 'xcl'], wr=[('bB', j2)])
                tt(bB[:, 2048:2304], bB[:, 2048:2304], xcc[:, j2, :], ALU.mult, rd=[('bB', j2), 'xcc'], wr=[('bB', j2)])
                LA = 256 + NT
                S.op('dve', lambda e, hA=hA, aA=aA, bA=bA: e.tensor_tensor_scan(out=hA[:, 0:LA], data0=aA[:, 0:LA], data1=bA[:, 0:LA], initial=0.0, op0=ALU.mult, op1=ALU.add),
                     reads=[('aA', j2), ('bA', j2)], writes=[('hA', j2)])
                S.op('dve', lambda e, hB=hB, aB=aB, bB=bB: e.tensor_tensor_scan(out=hB[:, ::-1], data0=aB[:, ::-1], data1=bB[:, ::-1], initial=0.0, op0=ALU.mult, op1=ALU.add),
                     reads=[('aB', j2), ('bB', j2)], writes=[('hB', j2)])
                yo = ylo[kt % 2]; yok = 'ylo%d' % (kt % 2)
                tt(yo[:, 0:NT], hA[:, 256:256 + NT], hB[:, 0:NT], ALU.add, rd=[('hA', j2), ('hB', j2)], wr=[yok])
                S.dma('pool', yl_s[kt * 128:(kt + 1) * 128, 0:NT], yo[:, 0:NT], reads=[yok], writes=['yl_s'])
        S.barrier()
    if stop_after <= 4:
        S.finish(); return nc

    zst = contextlib.ExitStack()
    zT = P.sb(zst, "zT", [128, KT, NTP], BF16)
    with contextlib.ExitStack() as st:
        jpi = P.sb(st, "jpi", [128, 1], I32); jp = P.sb(st, "jp", [128, 1])
        reli = P.sb(st, "reli", [128, 128], I32); rel = P.sb(st, "rel", [128, 128])
        relp = P.sb(st, "relp", [128, 128]); reln = P.sb(st, "reln", [128, 128])
        indA = P.sb(st, "indA", [128, 128]); indB = P.sb(st, "indB", [128, 128]); mtmp = P.sb(st, "mtmp", [128, 128])
        Mall = P.sb(st, "Mall", [128, H, 128])
        ev = P.sb(st, "ev", [128, 8])
        kdA = P.sb(st, "kdA", [128, H]); kdB = P.sb(st, "kdB", [128, H]); qdA = P.sb(st, "qdA", [128, H]); qdB = P.sb(st, "qdB", [128, H])
        cdA = P.sb(st, "cdA", [128, H]); cdB = P.sb(st, "cdB", [128, H]); g1A = P.sb(st, "g1A", [128, H]); g1B = P.sb(st, "g1B", [128, H])
        eA = P.sb(st, "eA", [128, 2]); eB = P.sb(st, "eB", [128, 10])
        wA = P.sb(st, "wA", [128, H, 2]); wB = P.sb(st, "wB", [128, H, 10])
        onesc = P.sb(st, "onesc", [128, 1])
        S.op('pool', lambda e: e.iota(jpi[:], pattern=[[0, 1]], base=0, channel_multiplier=1), writes=['jpi'])
        S.op('pool', lambda e: e.iota(reli[:], pattern=[[1, 128]], base=0, channel_multiplier=-1), writes=['reli'])
        S.op('dve', lambda e: e.tensor_copy(out=jp[:], in_=jpi[:]), reads=['jpi'], writes=['jp'])
        S.op('dve', lambda e: e.tensor_copy(out=rel[:], in_=reli[:]), reads=['reli'], writes=['rel'])
        S.op('dve', lambda e: e.memset(onesc[:], 1.0), writes=['onesc'])
        ts(relp[:], rel[:], 0.0, None, ALU.max, rd=['rel'], wr=['relp'])
        ts(reln[:], rel[:], -1.0, 0.0, ALU.mult, ALU.max, rd=['rel'], wr=['reln'])
        ts(indA[:], rel[:], 0.0, None, ALU.is_ge, rd=['rel'], wr=['indA'])
        ts(indB[:], rel[:], 0.0, None, ALU.is_le, rd=['rel'], wr=['indB'])
        ts(ev[:, 0:1], jp[:], -1.0, 127.0, ALU.mult, ALU.add, rd=['jp'], wr=['ev'])
        S.op('dve', lambda e: e.tensor_copy(out=ev[:, 1:2], in_=jp[:]), reads=['jp'], writes=['ev'])
        ts(ev[:, 2:3], jp[:], 1.0, None, ALU.add, rd=['jp'], wr=['ev'])
        ts(ev[:, 3:4], jp[:], -1.0, 128.0, ALU.mult, ALU.add, rd=['jp'], wr=['ev'])
        S.op('dve', lambda e: e.memset(ev[:, 4:5], 128.0), writes=['ev'])
        S.op('dve', lambda e: e.memset(ev[:, 5:6], 1.0), writes=['ev'])
        for (dst, z, col, nm) in ((kdA, 0, 0, 'kdA'), (kdB, 1, 1, 'kdB'), (qdA, 0, 2, 'qdA'), (qdB, 1, 3, 'qdB'), (cdA, 0, 4, 'cdA'), (cdB, 1, 4, 'cdB'), (g1A, 0, 5, 'g1A'), (g1B, 1, 5, 'g1B')):
            S.op('act', lambda e, dst=dst, z=z, col=col: e.activation(out=dst[:], in_=lg[:, z * H:(z + 1) * H], func=AF.Exp, scale=ev[:, col:col + 1]),
                 reads=['lg', 'ev'], writes=[nm])
        for c in range(10):
            if c < 2:
                ts(eA[:, c:c + 1], jp[:], -1.0, float(255 - c * 128), ALU.mult, ALU.add, rd=['jp'], wr=['eA'])
                ts(eB[:, c:c + 1], jp[:], float(c * 128 + 1023), None, ALU.add, rd=['jp'], wr=['eB'])
            else:
                ts(eB[:, c:c + 1], jp[:], float(c * 128 - 256), None, ALU.add, rd=['jp'], wr=['eB'])
        for h in range(H):
            S.op('act', lambda e, h=h: e.activation(out=wA[:, h, :], in_=eA[:], func=AF.Exp, scale=lg[:, h:h + 1]), reads=['eA', 'lg'], writes=['wA'])
            S.op('act', lambda e, h=h: e.activation(out=wB[:, h, :], in_=eB[:], func=AF.Exp, scale=lg[:, H + h:H + h + 1]), reads=['eB', 'lg'], writes=['wB'])
            S.op('act', lambda e, h=h: e.activation(out=Mall[:, h, :], in_=relp[:], func=AF.Exp, scale=lg[:, h:h + 1]), reads=['relp', 'lg'], writes=['Mall'])
            tt(Mall[:, h, :], Mall[:, h, :], indA[:], ALU.mult, rd=['Mall', 'indA'], wr=['Mall'])
            S.op('act', lambda e, h=h: e.activation(out=mtmp[:], in_=reln[:], func=AF.Exp, scale=lg[:, H + h:H + h + 1]), reads=['reln', 'lg'], writes=['mtmp'])
            tt(mtmp[:], mtmp[:], indB[:], ALU.mult, rd=['mtmp', 'indB'], wr=['mtmp'])
            tt(Mall[:, h, :], Mall[:, h, :], mtmp[:], ALU.add, rd=['Mall', 'mtmp'], wr=['Mall'])

        kTp = P.sb(st, "kTp", [128, NPFP], BF16); vTp = P.sb(st, "vTp", [128, 2, NPFP], BF16)
        kTm = P.sb(st, "kTm", [128, NTP], BF16); qTm = P.sb(st, "qTm", [128, NTP], BF16); vTm = P.sb(st, "vTm", [128, 2, NTP], BF16)
        sgh = P.sb(st, "sgh", [128, 2, NTP])
        kAc = P.sb(st, "kAc", [128, 128], BF16); kBc = P.sb(st, "kBc", [128, 128], BF16); vc = P.sb(st, "vc", [128, 256], BF16)
        kBall = P.sb(st, "kBall", [128, 9, 128], BF16); vtok = P.sb(st, "vtok", [128, 9, 256], BF16)
        SA = P.sb(st, "SA", [128, 256]); SB = P.sb(st, "SB", [128, 256]); SB16 = P.sb(st, "SB16", [128, 256], BF16)
        SAall = P.sb(st, "SAall", [128, 9, 256], BF16)
        PT = P.sb(st, "PT", [128, 128], BF16); osb = P.sb(st, "osb", [128, 256]); on = P.sb(st, "on", [128, 256])
        bst = P.sb(st, "bst", [128, 6]); mv = P.sb(st, "mv", [128, 4])
        psb = ps[:, :, :].bitcast(BF16)

        def tr_bf(bank, off, in_ap, n, rd):
            S.op('pe', lambda e: e.transpose(psb[0:n, bank, off:off + 128], in_ap, identb[:]), reads=rd + ['identb'], writes=[('ps', bank)])
            return psb[0:n, bank, off:off + 128]

        for h in range(H):
            S.dma('pool', kTp[:], kT_p[h * 128:(h + 1) * 128, :], writes=['kTp'])
            S.dma('pool', vTp[:], vT_p[h * 256:(h + 1) * 256, :].rearrange("(e p) t -> p e t", p=128), writes=['vTp'])
            S.dma('pool', kTm[:], kT_m[h * 128:(h + 1) * 128, :], writes=['kTm'])
            S.dma('pool', qTm[:], qT_m[h * 128:(h + 1) * 128, :], writes=['qTm'])
            S.dma('pool', vTm[:], vT_m[h * 256:(h + 1) * 256, :].rearrange("(e p) t -> p e t", p=128), writes=['vTm'])
            S.dma('pool', sgh[:], sg_m[h * 256:(h + 1) * 256, :].rearrange("(e p) t -> p e t", p=128), writes=['sgh'])
            for c in range(10):
                n = min(128, NPF - c * 128)
                pk_ = tr_bf(0, 0, kTp[:, c * 128:c * 128 + n], n, ['kTp'])
                S.op('act', lambda e, n=n, c=c, h=h, pk_=pk_: e.activation(out=kBc[0:n, :], in_=pk_, func=AF.Copy, scale=wB[0:n, h, c:c + 1]), reads=[('ps', 0), 'wB'], writes=['kBc'])
                if c < 2:
                    S.op('dve', lambda e, n=n, c=c, h=h, pk_=pk_: e.tensor_scalar(out=kAc[0:n, :], in0=pk_, scalar1=wA[0:n, h, c:c + 1], scalar2=None, op0=ALU.mult), reads=[('ps', 0), 'wA'], writes=['kAc'])
                for e2 in range(2):
                    tr_bf(1, e2 * 128, vTp[:, e2, c * 128:c * 128 + n], n, ['vTp'])
                S.op('dve', lambda e, n=n: e.tensor_copy(out=vc[0:n, :], in_=psb[0:n, 1, 0:256]), reads=[('ps', 1)], writes=['vc'])
                S.op('pe', lambda e, n=n, c=c: e.matmul(ps[:, 7, 0:256], lhsT=kBc[0:n, :], rhs=vc[0:n, :], start=(c == 0), stop=(c == 9)),
                     reads=['kBc', 'vc'], writes=[('ps', 7)])
                if c < 2:
                    S.op('pe', lambda e, n=n, c=c: e.matmul(ps[:, 6, 0:256], lhsT=kAc[0:n, :], rhs=vc[0:n, :], start=(c == 0), stop=(c == 1)),
                         reads=['kAc', 'vc'], writes=[('ps', 6)])
            S.op('act', lambda e: e.activation(out=SA[:], in_=ps[:, 6, 0:256], func=AF.Copy), reads=[('ps', 6)], writes=['SA'])
            S.op('dve', lambda e: e.tensor_copy(out=SB[:], in_=ps[:, 7, 0:256]), reads=[('ps', 7)], writes=['SB'])
            S.op('act', lambda e: e.activation(out=SB16[:], in_=SB[:], func=AF.Copy), reads=['SB'], writes=['SB16'])
            for c in range(9):
                n = 128 if c < 8 else 1
                pk_ = tr_bf(0, 0, kTm[:, c * 128:c * 128 + n], n, ['kTm'])
                sA = kdA[0:n, h:h + 1] if c < 8 else onesc[0:n, 0:1]
                S.op('act', lambda e, n=n, pk_=pk_, sA=sA: e.activation(out=kAc[0:n, :], in_=pk_, func=AF.Copy, scale=sA), reads=[('ps', 0), 'kdA', 'onesc'], writes=['kAc'])
                S.op('dve', lambda e, n=n, c=c, h=h, pk_=pk_: e.tensor_scalar(out=kBall[0:n, c, :], in0=pk_, scalar1=kdB[0:n, h:h + 1], scalar2=None, op0=ALU.mult), reads=[('ps', 0), 'kdB'], writes=['kBall'])
                for e2 in range(2):
                    tr_bf(1, e2 * 128, vTm[:, e2, c * 128:c * 128 + n], n, ['vTm'])
                S.op('dve', lambda e, n=n, c=c: e.tensor_copy(out=vtok[0:n, c, :], in_=psb[0:n, 1, 0:256]), reads=[('ps', 1)], writes=['vtok'])
                S.op('act', lambda e, c=c: e.activation(out=SAall[:, c, :], in_=SA[:], func=AF.Copy), reads=['SA'], writes=['SAall'])
                if c < 8:
                    S.op('pe', lambda e, n=n, c=c: e.matmul(ps[:, 2, 0:256], lhsT=kAc[0:n, :], rhs=vtok[0:n, c, :], start=True, stop=True),
                         reads=['kAc', 'vtok'], writes=[('ps', 2)])
                    S.op('dve', lambda e, h=h: e.scalar_tensor_tensor(out=SA[:], in0=SA[:], scalar=cdA[:, h:h + 1], in1=ps[:, 2, 0:256], op0=ALU.mult, op1=ALU.add),
                         reads=['SA', 'cdA', ('ps', 2)], writes=['SA'])
            for c in range(8, -1, -1):
                n = 128 if c < 8 else 1
                t0 = c * 128
                S.op('pe', lambda e, n=n, t0=t0: e.matmul(ps[0:n, 3, 0:n], lhsT=kTm[:, t0:t0 + n], rhs=qTm[:, t0:t0 + n], start=True, stop=True),
                     reads=['kTm', 'qTm'], writes=[('ps', 3)])
                tt(PT[0:n, 0:n], ps[0:n, 3, 0:n], Mall[0:n, h, 0:n], ALU.mult, rd=[('ps', 3), 'Mall'], wr=['PT'])
                S.op('pe', lambda e, n=n, c=c: e.matmul(ps[0:n, 4, 0:256], lhsT=PT[0:n, 0:n], rhs=vtok[0:n, c, :], start=True, stop=True),
                     reads=['PT', 'vtok'], writes=[('ps', 4)])
                S.op('pe', lambda e, n=n, c=c, t0=t0: e.matmul(ps[0:n, 5, 0:256], lhsT=qTm[:, t0:t0 + n], rhs=SAall[:, c, :], start=True, stop=True),
                     reads=['qTm', 'SAall'], writes=[('ps', 5)])
                S.op('pe', lambda e, n=n, t0=t0: e.matmul(ps[0:n, 5, 256:512], lhsT=qTm[:, t0:t0 + n], rhs=SB16[:], start=True, stop=True),
                     reads=['qTm', 'SB16'], writes=[('ps', 5)])
                S.op('act', lambda e, n=n: e.activation(out=osb[0:n, :], in_=ps[0:n, 4, 0:256], func=AF.Copy), reads=[('ps', 4)], writes=['osb'])
                sqB = qdB[0:n, h:h + 1] if c < 8 else g1B[0:n, h:h + 1]
                S.op('dve', lambda e, n=n, h=h: e.scalar_tensor_tensor(out=osb[0:n, :], in0=ps[0:n, 5, 0:256], scalar=qdA[0:n, h:h + 1], in1=osb[0:n, :], op0=ALU.mult, op1=ALU.add),
                     reads=[('ps', 5), 'qdA', 'osb'], writes=['osb'])
                S.op('dve', lambda e, n=n, sqB=sqB: e.scalar_tensor_tensor(out=osb[0:n, :], in0=ps[0:n, 5, 256:512], scalar=sqB, in1=osb[0:n, :], op0=ALU.mult, op1=ALU.add),
                     reads=[('ps', 5), 'qdB', 'g1B', 'osb'], writes=['osb'])
                S.op('dve', lambda e, n=n: e.bn_stats(out=bst[0:n, :], in_=osb[0:n, :]), reads=['osb'], writes=['bst'])
                S.op('dve', lambda e, n=n: e.bn_aggr(out=mv[0:n, 0:2], in_=bst[0:n, :]), reads=['bst'], writes=['mv'])
                ts(mv[0:n, 2:3], mv[0:n, 1:2], EPS, None, ALU.add, rd=['mv'], wr=['mv'])
                S.op('act', lambda e, n=n: e.activation(out=mv[0:n, 2:3], in_=mv[0:n, 2:3], func=AF.Sqrt), reads=['mv'], writes=['mv'])
                S.op('dve', lambda e, n=n: e.reciprocal(out=mv[0:n, 3:4], in_=mv[0:n, 2:3]), reads=['mv'], writes=['mv'])
                ts(on[0:n, :], osb[0:n, :], mv[0:n, 0:1], mv[0:n, 3:4], ALU.subtract, ALU.mult, rd=['osb', 'mv'], wr=['on'])
                for e2 in range(2):
                    S.op('pe', lambda e, n=n, e2=e2: e.transpose(ps[:, 2, e2 * 128:e2 * 128 + n], on[0:n, e2 * 128:(e2 + 1) * 128], ident[0:n, 0:n]),
                         reads=['on', 'ident'], writes=[('ps', 2)])
                    tt(zT[:, 2 * h + e2, t0:t0 + n], ps[:, 2, e2 * 128:e2 * 128 + n], sgh[:, e2, t0:t0 + n], ALU.mult, rd=[('ps', 2), 'sgh'], wr=['zT'])
                if c > 0:
                    S.op('pe', lambda e, n=n, c=c: e.matmul(ps[:, 6, 0:256], lhsT=kBall[0:n, c, :], rhs=vtok[0:n, c, :], start=True, stop=True),
                         reads=['kBall', 'vtok'], writes=[('ps', 6)])
                    scB = cdB[:, h:h + 1] if c < 8 else g1B[:, h:h + 1]
                    S.op('dve', lambda e, scB=scB: e.scalar_tensor_tensor(out=SB[:], in0=SB[:], scalar=scB, in1=ps[:, 6, 0:256], op0=ALU.mult, op1=ALU.add),
                         reads=['SB', 'cdB', 'g1B', ('ps', 6)], writes=['SB'])
                    S.op('act', lambda e: e.activation(out=SB16[:], in_=SB[:], func=AF.Copy), reads=['SB'], writes=['SB16'])
            if ada_gen is not None:
                for _ in range(4):
                    next(ada_gen, None)
        if ada_gen is not None:
            for _ in ada_gen:
                pass
        S.barrier()
    late_tables()
    if 'zT' in P.dbg:
        dz = nc.dram_tensor("dbg_zT", [128, KT, NTP], BF16, kind="ExternalOutput").ap()
        S.dma('sp', dz, zT[:], reads=['zT']); S.barrier()
    if stop_after <= 5:
        S.finish(); return nc

    def gated_epi(st, tag, gate_src, add_src, dst_dram, dst_sb, dst_key):
        gb = [P.sb(st, "%s_g%d" % (tag, i), [128, NTP]) for i in range(1)]
        ab_ = [P.sb(st, "%s_a%d" % (tag, i), [128, NTP]) for i in range(1)] if add_src is not None else None
        ob = [P.sb(st, "%s_ob%d" % (tag, i), [128, NTP]) for i in range(1)]
        state = {'i': 0}

        def epi(si, ch, pss):
            i = state['i']; state['i'] += 1
            j = 0
            cidx = si * 2 + ch
            r0 = cidx * 128
            gk, ak, ok = "%s_g%d" % (tag, j), "%s_a%d" % (tag, j), "%s_ob%d" % (tag, j)
            S.dma('pool', gb[j][:, 0:NT], gate_src[r0:r0 + 128, 0:NT], writes=[gk])
            if add_src is not None:
                S.dma('pool', ab_[j][:, 0:NT], add_src[r0:r0 + 128, 0:NT], writes=[ak])
            for (t0, n), (pap, pk) in zip(tgm, pss):
                tt(ob[j][:, t0:t0 + n], pap, gb[j][:, t0:t0 + n], ALU.mult, rd=[pk, gk], wr=[ok])
            if add_src is not None:
                tt(ob[j][:, 0:NT], ob[j][:, 0:NT], ab_[j][:, 0:NT], ALU.add, rd=[ok, ak], wr=[ok])
            if dst_dram is not None:
                S.dma('pool', dst_dram[r0:r0 + 128, 0:NT], ob[j][:, 0:NT], reads=[ok], writes=[('dram', id(dst_dram))])
            else:
                S.op('act', lambda e: e.activation(out=dst_sb[:, cidx, 0:NT], in_=ob[j][:, 0:NT], func=AF.Copy), reads=[ok], writes=[dst_key])
        return epi

    sq_slabs = [[(s_ * 256, 256)] for s_ in range(16)]
    with contextlib.ExitStack() as st:
        P.proj_fm('ro', w_ret_o, D, sq_slabs, zT, 'zT', tgm, gated_epi(st, 'ro', sr_m, None, yr_s, None, None), PB, ps)
    zst.close()
    S.barrier()
    yst = contextlib.ExitStack()
    yT = P.sb(yst, "yT", [128, KT, NTP], BF16)
    with contextlib.ExitStack() as st:
        uT = P.sb(st, "uT", [128, KT, NTP], BF16)
        with contextlib.ExitStack() as st2:
            g_ = [P.sb(st2, "u_g%d" % i, [128, NTP]) for i in range(2)]
            y_ = [P.sb(st2, "u_y%d" % i, [128, NTP]) for i in range(2)]
            for kt in range(KT):
                j = kt % 2
                S.dma('pool', g_[j][:, 0:NT], gg_m[kt * 128:(kt + 1) * 128, 0:NT], writes=['u_g%d' % j])
                S.dma('pool', y_[j][:, 0:NT], yl_s[kt * 128:(kt + 1) * 128, 0:NT], writes=['u_y%d' % j])
                tt(uT[:, kt, 0:NT], g_[j][:, 0:NT], y_[j][:, 0:NT], ALU.mult, rd=['u_g%d' % j, 'u_y%d' % j], wr=['uT'])
            S.barrier()
        P.proj_fm('lo', w_lru_o, D, sq_slabs, uT, 'uT', tgm, gated_epi(st, 'lo', sl_m, yr_s, None, yT, 'yT'), PB, ps)
    S.barrier()
    if 'yT' in P.dbg:
        dz = nc.dram_tensor("dbg_yT", [128, KT, NTP], BF16, kind="ExternalOutput").ap()
        S.dma('sp', dz, yT[:], reads=['yT']); S.barrier()
    if stop_after <= 6:
        S.finish(); return nc
    with contextlib.ExitStack() as st:
        ssacc = P.sb(st, "ssacc", [128, NTP])
        S.op('dve', lambda e: e.memset(ssacc[:], 0.0), writes=['ssacc'])
        xb2 = [P.sb(st, "wo_x%d" % i, [128, NTP]) for i in range(2)]
        ob2 = [P.sb(st, "wo_o%d" % i, [128, NTP]) for i in range(2)]
        sq2 = P.sb(st, "wo_sq", [128, NTP])
        stt = {'i': 0}

        def epi_wo(si, ch, pss):
            i = stt['i']; stt['i'] += 1
            j = i % 2
            cidx = si * 2 + ch
            r0 = cidx * 128
            xk, ok = 'wo_x%d' % j, 'wo_o%d' % j
            S.dma('pool', xb2[j][:, 0:NT], xT_m[r0:r0 + 128, 0:NT], writes=[xk])
            for (t0, n), (pap, pk) in zip(tgm, pss):
                S.op('dve', lambda e, t0=t0, n=n, pap=pap: e.scalar_tensor_tensor(out=ob2[j][:, t0:t0 + n], in0=pap, scalar=G1[:, cidx:cidx + 1], in1=xb2[j][:, t0:t0 + n], op0=ALU.mult, op1=ALU.add),
                     reads=[pk, 'G1', xk], writes=[ok])
            S.op('act', lambda e: e.activation(out=sq2[:, 0:NT], in_=ob2[j][:, 0:NT], func=AF.Square), reads=[ok], writes=['wo_sq'])
            tt(ssacc[:, 0:NT], ssacc[:, 0:NT], sq2[:, 0:NT], ALU.add, rd=['ssacc', 'wo_sq'], wr=['ssacc'])
            S.dma('pool', xlatT[r0:r0 + 128, 0:NT], ob2[j][:, 0:NT], reads=[ok], writes=['xlatT'])
        P.proj_fm('wo', w_out, D, sq_slabs, yT, 'yT', tgm, epi_wo, PB, ps)
        if stop_after <= 6.5:
            S.finish(); return nc
        for gi, (t0, n) in enumerate(tgm):
            S.op('pe', lambda e, gi=gi, t0=t0, n=n: e.matmul(ps[:, gi, 0:n], lhsT=ones[:], rhs=ssacc[:, t0:t0 + n], start=True, stop=True),
                 reads=['ones', 'ssacc'], writes=[('ps', gi)])
            S.op('act', lambda e, gi=gi, t0=t0, n=n: e.activation(out=rstd2[:, t0:t0 + n], in_=ps[:, gi, 0:n], func=AF.Sqrt, scale=1.0 / D, bias=epsc[:, 0:1]),
                 reads=[('ps', gi), 'epsc'], writes=['rstd2'])
        S.op('dve', lambda e: e.reciprocal(out=rstd2[:, 0:NT], in_=rstd2[:, 0:NT]), reads=['rstd2'], writes=['rstd2'])
        S.barrier()
        if stop_after <= 6.7:
            dz = nc.dram_tensor('dbg_rstd2', [128, NTP], F32, kind='ExternalOutput').ap()
            S.dma('sp', dz, rstd2[:], reads=['rstd2']); S.finish(); return nc
    yst.close()
    S.barrier()
    hst = contextlib.ExitStack()
    h2T = P.sb(hst, "h2T", [128, KT, NTP], BF16)
    with contextlib.ExitStack() as st:
        xb3 = [P.sb(st, "n2_x%d" % i, [128, NTP]) for i in range(2)]
        for kt in range(KT):
            j = kt % 2
            xk = 'n2_x%d' % j
            S.dma('pool', xb3[j][:, 0:NT], xlatT[kt * 128:(kt + 1) * 128, 0:NT], writes=[xk])
            tt(xb3[j][:, 0:NT], xb3[j][:, 0:NT], rstd2[:, 0:NT], ALU.mult, rd=[xk, 'rstd2'], wr=[xk])
            S.op('act', lambda e, j=j, kt=kt: e.activation(out=h2T[:, kt, 0:NT], in_=xb3[j][:, 0:NT], func=AF.Identity, scale=A2[:, kt:kt + 1], bias=B2[:, kt:kt + 1]),
                 reads=[xk, 'A2', 'B2'], writes=['h2T'])
        S.barrier()
    if 'h2T' in P.dbg:
        dz = nc.dram_tensor("dbg_h2T", [128, KT, NTP], BF16, kind="ExternalOutput").ap()
        S.dma('sp', dz, h2T[:], reads=['h2T']); S.barrier()
    if stop_after <= 7:
        S.finish(); return nc
    with contextlib.ExitStack() as st:
        fcw = P.sb(st, "fcw", [128, 2 * FFT, 3]); fcb = P.sb(st, "fcb", [128, 2 * FFT])
        S.dma('sp', fcw[:], fconvT, writes=['fcw'])
        S.dma('sp', fcb[:], fconvbT, writes=['fcb'])
        ub = [P.sb(st, "ub%d" % i, [128, NTP + 2]) for i in range(2)]
        acc = [P.sb(st, "acc%d" % i, [128, 1024]) for i in range(2)]
        sgt = [P.sb(st, "sgt%d" % i, [128, 1024]) for i in range(2)]
        ao = [P.sb(st, "ao%d" % i, [128, 1024], BF16) for i in range(2)]
        for i in range(2):
            S.op('dve', lambda e, i=i: e.memset(ub[i][:], 0.0), writes=['ub%d' % i])
        stt9 = {'i': 0}

        def epi_up(si, ch, pss):
            i = stt9['i']; stt9['i'] += 1
            j = i % 2
            isv = ch >= 2
            gch = 2 * si + (ch % 2)
            cw = gch + (FFT if isv else 0)
            ubk, acck = 'ub%d' % j, 'acc%d' % j
            for (t0, n), (pap, pk) in zip(tgm, pss):
                S.op('act', lambda e, t0=t0, n=n, pap=pap: e.activation(out=ub[j][:, 1 + t0:1 + t0 + n], in_=pap, func=AF.Copy), reads=[pk], writes=[ubk])
            S.op('dve', lambda e: e.tensor_scalar(out=acc[j][:], in0=ub[j][:, 0:1024], scalar1=fcw[:, cw, 0:1], scalar2=fcb[:, cw:cw + 1], op0=ALU.mult, op1=ALU.add),
                 reads=[ubk, 'fcw', 'fcb'], writes=[acck])
            for tap in (1, 2):
                S.op('dve', lambda e, tap=tap: e.scalar_tensor_tensor(out=acc[j][:], in0=ub[j][:, tap:tap + 1024], scalar=fcw[:, cw, tap:tap + 1], in1=acc[j][:], op0=ALU.mult, op1=ALU.add),
                     reads=[ubk, 'fcw', acck], writes=[acck])
            if not isv:
                S.op('act', lambda e: e.activation(out=sgt[ch][:], in_=acc[j][:], func=AF.Silu), reads=[acck], writes=['sgt%d' % ch])
            else:
                c2 = ch - 2
                tt(ao[c2][:], acc[j][:], sgt[c2][:], ALU.mult, rd=[acck, 'sgt%d' % c2], wr=['ao%d' % c2])
                S.dma('pool', a_scr[gch * 128:(gch + 1) * 128, :], ao[c2][:], reads=['ao%d' % c2], writes=['a_scr'])
        up_slabs = [[(s_ * 256, 256), (FF + s_ * 256, 256)] for s_ in range(43)]
        P.proj_fm('up', w_up, D, up_slabs, h2T, 'h2T', tgm, epi_up, PB, ps)
    hst.close()
    S.barrier()
    if 'a_scr' in P.dbg and stop_after <= 8:
        S.finish(); return nc
    with contextlib.ExitStack() as st:
        fnT_ = P.sb(st, "fnT_", [128, KT])
        S.dma('sp', fnT_[:], fnormT, writes=['fnT_'])
        KQ = 4
        wst = [P.sb(st, "dn_ws%d" % i, [128, KQ, 384]) for i in range(2)]
        wbb = [P.sb(st, "dn_wb%d" % i, [128, KQ, 384], BF16) for i in range(2)]
        abuf = [P.sb(st, "dn_a%d" % i, [128, KQ, 1024], BF16) for i in range(2)]
        xl4 = [P.sb(st, "dn_x%d" % i, [128, 1024]) for i in range(2)]
        xo4 = [P.sb(st, "dn_o%d" % i, [128, 1024]) for i in range(2)]
        xf4 = [P.sb(st, "dn_f%d" % i, [128, 1024]) for i in range(2)]
        sq4 = P.sb(st, "dn_sq", [128, 1024]); ssa = P.sb(st, "dn_ssa", [128, 1024])
        ot = [P.sb(st, "dn_ot%d" % i, [128, 8, 128]) for i in range(2)]
        rs3 = P.sb(st, "dn_rs3", [128, 8])
        big = [P.sb(st, "dn_big%d" % i, [128, D]) for i in range(2)]
        S.op('dve', lambda e: e.memset(ssa[:], 0.0), writes=['dn_ssa'])
        y_v = y_out.rearrange("(i p) f -> p i f", p=128)
        grp = 0
        ei = 0
        for cg in range(11):
            c0 = cg * 384
            ncol = min(384, D - c0)
            nch = ncol // 128
            nkq = (FFT + KQ - 1) // KQ
            for kq in range(nkq):
                k0 = kq * KQ
                kn = min(KQ, FFT - k0)
                j = grp % 2; grp += 1
                S.dma('sp', wst[j][:, 0:kn, 0:ncol], w_down[k0 * 128:(k0 + kn) * 128, c0:c0 + ncol].rearrange("(k p) c -> p k c", p=128), writes=['dn_ws%d' % j])
                S.dma('pool', abuf[j][:, 0:kn, :], a_scr[k0 * 128:(k0 + kn) * 128, :].rearrange("(k p) t -> p k t", p=128), reads=['a_scr'], writes=['dn_a%d' % j])
                if grp % 2:
                    S.op('act', lambda e, j=j, kn=kn, ncol=ncol: e.activation(out=wbb[j][:, 0:kn, 0:ncol], in_=wst[j][:, 0:kn, 0:ncol], func=AF.Copy), reads=['dn_ws%d' % j], writes=['dn_wb%d' % j])
                else:
                    S.op('dve', lambda e, j=j, kn=kn, ncol=ncol: e.tensor_copy(out=wbb[j][:, 0:kn, 0:ncol], in_=wst[j][:, 0:kn, 0:ncol]), reads=['dn_ws%d' % j], writes=['dn_wb%d' % j])
                for q in range(kn):
                    kt = k0 + q
                    for ch in range(nch):
                        for g in range(2):
                            b = ch * 2 + g
                            last = (q == kn - 1 and ch == nch - 1 and g == 1)
                            S.op('pe', lambda e, b=b, j=j, q=q, ch=ch, g=g, kt=kt: e.matmul(ps[:, b, :], lhsT=wbb[j][:, q, ch * 128:(ch + 1) * 128], rhs=abuf[j][:, q, g * 512:(g + 1) * 512], start=(kt == 0), stop=(kt == FFT - 1)),
                                 reads=['dn_wb%d' % j, 'dn_a%d' % j], writes=[('ps', b)], inc=(kt == FFT - 1 or last))
            for ch in range(nch):
                cidx = cg * 3 + ch
                j = ei % 2; ei += 1
                xk, ok, fk, otk = 'dn_x%d' % j, 'dn_o%d' % j, 'dn_f%d' % j, 'dn_ot%d' % j
                S.dma('pool', xl4[j][:], xlatT[cidx * 128:(cidx + 1) * 128, 0:1024], writes=[xk])
                for g in range(2):
                    b = ch * 2 + g
                    S.op('dve', lambda e, b=b, g=g, j=j, cidx=cidx: e.scalar_tensor_tensor(out=xo4[j][:, g * 512:(g + 1) * 512], in0=ps[:, b, :], scalar=G2[:, cidx:cidx + 1], in1=xl4[j][:, g * 512:(g + 1) * 512], op0=ALU.mult, op1=ALU.add),
                         reads=[('ps', b), 'G2', xk], writes=[ok])
                S.op('act', lambda e, j=j: e.activation(out=sq4[:], in_=xo4[j][:], func=AF.Square), reads=[ok], writes=['dn_sq'])
                tt(ssa[:], ssa[:], sq4[:], ALU.add, rd=['dn_ssa', 'dn_sq'], wr=['dn_ssa'])
                S.op('act', lambda e, j=j, cidx=cidx: e.activation(out=xf4[j][:], in_=xo4[j][:], func=AF.Identity, scale=fnT_[:, cidx:cidx + 1]), reads=[ok, 'fnT_'], writes=[fk])
                for tb in range(2):
                    bk = 6 + tb
                    for q in range(4):
                        ti_ = tb * 4 + q
                        S.op('pe', lambda e, bk=bk, q=q, ti_=ti_, j=j: e.transpose(ps[:, bk, q * 128:(q + 1) * 128], xf4[j][:, ti_ * 128:(ti_ + 1) * 128], ident[:]),
                             reads=[fk, 'ident'], writes=[('ps', bk)], inc=(q == 3))
                    P.evac(ot[j][:, tb * 4:(tb + 1) * 4, :], ps[:, bk, :].rearrange("p (q f) -> p q f", q=4), [('ps', bk)], [otk])
                S.dma('pool', y_v[:, :, cidx * 128:(cidx + 1) * 128], ot[j][:], reads=[otk], writes=['y_out'])
        for i in range(8):
            S.op('pe', lambda e, i=i: e.matmul(ps[:, 0, i:i + 1], lhsT=ssa[:, i * 128:(i + 1) * 128], rhs=ones[:, 0:1], start=True, stop=True),
                 reads=['dn_ssa', 'ones'], writes=[('ps', 0)])
        S.op('act', lambda e: e.activation(out=rs3[:], in_=ps[:, 0, 0:8], func=AF.Sqrt, scale=1.0 / D, bias=epsc[:, 0:1]), reads=[('ps', 0), 'epsc'], writes=['dn_rs3'])
        S.op('dve', lambda e: e.reciprocal(out=rs3[:], in_=rs3[:]), reads=['dn_rs3'], writes=['dn_rs3'])
        for i in range(8):
            j = i % 2
            bk_ = 'dn_big%d' % j
            S.dma('sp', big[j][:], y_out[i * 128:(i + 1) * 128, :], reads=['y_out'], writes=[bk_])
            S.op('act', lambda e, i=i, j=j: e.activation(out=big[j][:], in_=big[j][:], func=AF.Identity, scale=rs3[:, i:i + 1]), reads=[bk_, 'dn_rs3'], writes=[bk_])
            S.dma('sp', y_out[i * 128:(i + 1) * 128, :], big[j][:], reads=[bk_], writes=['y_out2'])
        S.barrier()
    S.finish()
    return nc


def _pT(v, nchunk):
    return np.ascontiguousarray(np.asarray(v, np.float32).reshape(nchunk, 128).T)


def prep_shared(inp):
    sh = {}
    sh["w_ada"] = np.ascontiguousarray(inp["w_ada"][0])
    sh["b_adaT"] = _pT(inp["b_ada"][0], 192)
    sh["norm1T"] = _pT(inp["norm1"][0], KT)
    sh["norm2T"] = _pT(inp["norm2"][0], KT)
    sh["fnormT"] = _pT(inp["final_norm"], KT)
    sh["w_in"] = np.ascontiguousarray(inp["w_in"][0])
    sh["convbT"] = _pT(inp["lru_conv_b"][0], KT)
    sh["w_ret_o"] = np.ascontiguousarray(inp["w_ret_o"][0])
    sh["w_lru_o"] = np.ascontiguousarray(inp["w_lru_o"][0])
    sh["w_out"] = np.ascontiguousarray(inp["w_out"][0])
    sh["w_up"] = np.ascontiguousarray(inp["w_up"][0])
    sh["fconvbT"] = _pT(inp["ffn_conv_b"][0], 2 * FFT)
    sh["w_down"] = np.ascontiguousarray(inp["w_down"][0])
    sh["ident"] = np.eye(128, dtype=np.float32)
    r = np.zeros((128, 128), np.float32)
    for d in range(64):
        r[d + 64, d] = -1.0
        r[d, d + 64] = 1.0
    sh["rmat"] = r
    return sh


def prep_core(inp, sh, b, half):
    flip = (half == 1)
    m = dict(sh)
    xb = inp["x"][b]
    cx = inp["ctx"][b]
    if flip:
        xb = xb[::-1]
        cx = cx[::-1]
    xm = np.zeros((1152, D), np.float32)
    xm[:NT] = xb[:NT]
    xp = np.zeros((NPFP, D), np.float32)
    xp[:256] = cx
    xp[256:NPF] = xb[NT:2048]
    m["x_main"] = xm
    m["x_pre"] = xp
    cT = np.stack([_pT(inp["c"][b], KT), _pT(inp["c_ctx"], KT)], axis=-1)
    m["cT"] = np.ascontiguousarray(cT)
    zs = [1, 0] if flip else [0, 1]
    m["retlog"] = np.ascontiguousarray(np.broadcast_to(inp["ret_decay_logit"][0][zs].reshape(1, 2 * H), (128, 2 * H)))
    cw = inp["lru_conv_w"][0]
    c5 = np.zeros((5, D), np.float32)
    if flip:
        c5[1:5] = cw[::-1]
    else:
        c5[0:4] = cw
    m["conv5T"] = np.ascontiguousarray(np.stack([_pT(c5[j], KT) for j in range(5)], axis=-1))
    m["lru_wa"] = np.ascontiguousarray(inp["lru_wa"][0][zs])
    m["lru_wx"] = np.ascontiguousarray(inp["lru_wx"][0][zs])
    for nm, key in (("lru_baT", "lru_ba"), ("lru_bxT", "lru_bx"), ("lru_lamT", "lru_lambda")):
        a = inp[key][0][zs]
        m[nm] = np.ascontiguousarray(np.stack([_pT(a[0], KT), _pT(a[1], KT)], axis=1))
    fw = inp["ffn_conv_w"][0]
    if flip:
        fw = fw[::-1]
    m["fconvT"] = np.ascontiguousarray(np.stack([_pT(fw[j], 2 * FFT) for j in range(3)], axis=-1))
    d = np.arange(128)
    rc = np.zeros((128, 5), np.float32)
    rc[:, 0] = -1.0 if flip else 1.0
    rc[:, 1] = 31.0 if flip else 0.0
    rc[:, 2] = 63.0 if flip else 0.0
    rc[:, 3] = d % 32
    rc[:, 4] = ((d % 64) < 32).astype(np.float32)
    m["ropec"] = rc
    return m


def kernel(**inputs):
    inp = {k: np.asarray(v) for k, v in inputs.items()}
    nc = build()
    sh = prep_shared(inp)
    maps = []
    for core in range(8):
        maps.append(prep_core(inp, sh, core // 2, core % 2))
    res = run_bass_kernel_spmd(nc, maps, core_ids=list(range(8)))
    out = np.empty((4, 2048, D), np.float32)
    for core in range(8):
        b, half = core // 2, core % 2
        y = np.asarray(res.results[core]["y_out"], np.float32)
        if half == 0:
            out[b, 0:1024] = y
        else:
            out[b, 1024:2048] = y[::-1]
    return out
```
